# Optimizing a Trainium2 kernel written in Bass

```python
import jax, jax.numpy as jnp
from jax import lax
import numpy as np

D_MODEL = 1024
BATCH = 16
SEQ = 2048
DEPTH = 2
DEC_BATCH = 4
DEC_SEQ = 8192
PAST_LEN = 128

N_GROUPS = 4
D_POOL = D_MODEL
D_CONV = D_MODEL
D_FNET = D_MODEL
G_POOL = D_POOL // N_GROUPS
G_FNET = D_FNET // N_GROUPS
POOL_WINDOWS = (2, 4, 8, 16)
CONV_WIDTH = 31
CONV_PAD = CONV_WIDTH // 2
D_FF = 4 * D_MODEL
D_PLE = 256
N_BRANCH = 3
D_IN = D_POOL + 2 * D_CONV + D_FNET + N_BRANCH * D_MODEL
SPLITS = (D_POOL, D_POOL + 2 * D_CONV, D_POOL + 2 * D_CONV + D_FNET)
EPS = 1e-6

kernel_name = "hybrid_pool_conv_fourier_encoder"


def rmsnorm(x, g):
    xf = x.astype(jnp.float32)
    y = xf * lax.rsqrt(jnp.mean(xf * xf, axis=-1, keepdims=True) + EPS)
    return (y * g.astype(jnp.float32)).astype(x.dtype)


def layernorm(x, g, b):
    xf = x.astype(jnp.float32)
    mu = jnp.mean(xf, axis=-1, keepdims=True)
    xc = xf - mu
    var = jnp.mean(xc * xc, axis=-1, keepdims=True)
    y = xc * lax.rsqrt(var + EPS) * g.astype(jnp.float32) + b.astype(jnp.float32)
    return y.astype(x.dtype)


def pool_mixer(u, w, scale):
    B, S, _ = u.shape
    uf = u.astype(jnp.float32).reshape(B, S, N_GROUPS, G_POOL)
    c = jnp.concatenate([jnp.zeros((B, 1, N_GROUPS, G_POOL), jnp.float32),
                         jnp.cumsum(uf, axis=1)], axis=1)
    t = np.arange(S)
    means = []
    for gi, win in enumerate(POOL_WINDOWS):
        lo = np.maximum(t - win // 2, 0)
        hi = np.minimum(t + win // 2 - 1, S - 1)
        cnt = (hi - lo + 1).astype(np.float32)
        cg = c[:, :, gi, :]
        means.append((cg[:, hi + 1] - cg[:, lo]) / jnp.asarray(cnt)[None, :, None])
    pooled = (jnp.stack(means, axis=2) - uf).astype(u.dtype)
    y = jnp.einsum('bsgc,gcd->bsgd', pooled, w).reshape(B, S, D_MODEL)
    return y * scale


def conv_mixer(u2, conv_w, conv_b, ln_g, ln_b, w_out):
    a, g = jnp.split(u2, 2, axis=-1)
    v = a * jax.nn.sigmoid(g)
    v = lax.conv_general_dilated(v, conv_w[:, None, :], window_strides=(1,),
                                 padding=[(CONV_PAD, CONV_PAD)],
                                 dimension_numbers=('NWC', 'WIO', 'NWC'),
                                 feature_group_count=D_CONV) + conv_b
    v = jax.nn.silu(layernorm(v, ln_g, ln_b))
    return v @ w_out


def fnet_mixer(u, w):
    B, S, _ = u.shape
    uf = u.astype(jnp.float32).reshape(B, S, N_GROUPS, G_FNET)
    f = jnp.fft.fft2(uf, axes=(1, 3), norm='ortho').real.astype(u.dtype)
    return jnp.einsum('bsgc,gcd->bsgd', f, w).reshape(B, S, D_MODEL)


def trunk(x, p, norm_mix_g, w_in, b_gate, pool_w, pool_scale, conv_w, conv_b,
          conv_ln_g, conv_ln_b, conv_out_w, fnet_w, w_o, norm_ff_g, w_ff1, w_ff2,
          norm_ple_g, w_ple_gate, w_ple_proj, final_norm_g):
    h = x
    B, S, _ = x.shape
    for i in range(DEPTH):
        xn = rmsnorm(h, norm_mix_g[i])
        z = xn @ w_in[i]
        u_pool, u_conv, u_fnet, gate_pre = jnp.split(z, SPLITS, axis=-1)
        gates = jax.nn.sigmoid(gate_pre + b_gate[i]).reshape(B, S, N_BRANCH, D_MODEL)
        y_a = pool_mixer(u_pool, pool_w[i], pool_scale[i])
        y_b = conv_mixer(u_conv, conv_w[i], conv_b[i], conv_ln_g[i], conv_ln_b[i], conv_out_w[i])
        y_c = fnet_mixer(u_fnet, fnet_w[i])
        merged = gates[:, :, 0] * y_a + gates[:, :, 1] * y_b + gates[:, :, 2] * y_c
        h = h + merged @ w_o[i]
        hf = rmsnorm(h, norm_ff_g[i]) @ w_ff1[i]
        h = h + jnp.square(jax.nn.relu(hf)) @ w_ff2[i]
        pg = jax.nn.sigmoid(rmsnorm(h, norm_ple_g[i]) @ w_ple_gate[i])
        h = h + (p[i] @ w_ple_proj[i]) * pg
    return rmsnorm(h, final_norm_g)


def setup_inputs(seed: int = 0) -> dict:
    key = jax.random.key(seed)
    ks = jax.random.split(key, 24)

    def nrm(k, shape, scale):
        return jax.random.normal(k, shape, jnp.float32) * scale

    def gain(k, shape):
        return 1.0 + 0.02 * jax.random.normal(k, shape, jnp.float32)

    L, D = DEPTH, D_MODEL
    return {
        "x_prompt": nrm(ks[0], (BATCH, SEQ, D), 1.0),
        "x_sample": nrm(ks[1], (DEC_BATCH, DEC_SEQ, D), 1.0),
        "p_prompt": nrm(ks[2], (DEPTH, BATCH, SEQ, D_PLE), 1.0),
        "p_sample": nrm(ks[3], (DEPTH, DEC_BATCH, DEC_SEQ, D_PLE), 1.0),
        "norm_mix_g": gain(ks[4], (L, D)),
        "w_in": nrm(ks[5], (L, D, D_IN), D ** -0.5),
        "b_gate": nrm(ks[6], (L, N_BRANCH * D), 0.02),
        "pool_w": nrm(ks[7], (L, N_GROUPS, G_POOL, G_POOL), G_POOL ** -0.5),
        "pool_scale": gain(ks[8], (L, D)),
        "conv_w": nrm(ks[9], (L, CONV_WIDTH, D_CONV), CONV_WIDTH ** -0.5),
        "conv_b": nrm(ks[10], (L, D_CONV), 0.02),
        "conv_ln_g": gain(ks[11], (L, D_CONV)),
        "conv_ln_b": nrm(ks[12], (L, D_CONV), 0.02),
        "conv_out_w": nrm(ks[13], (L, D_CONV, D), D_CONV ** -0.5),
        "fnet_w": nrm(ks[14], (L, N_GROUPS, G_FNET, G_FNET), G_FNET ** -0.5),
        "w_o": nrm(ks[15], (L, D, D), D ** -0.5),
        "norm_ff_g": gain(ks[16], (L, D)),
        "w_ff1": nrm(ks[17], (L, D, D_FF), D ** -0.5),
        "w_ff2": nrm(ks[18], (L, D_FF, D), D_FF ** -0.5),
        "norm_ple_g": gain(ks[19], (L, D)),
        "w_ple_gate": nrm(ks[20], (L, D, D), D ** -0.5),
        "w_ple_proj": nrm(ks[21], (L, D_PLE, D), D_PLE ** -0.5),
        "final_norm_g": gain(ks[22], (D,)),
    }


def reference(x_prompt, x_sample, p_prompt, p_sample, norm_mix_g, w_in, b_gate, pool_w,
              pool_scale, conv_w, conv_b, conv_ln_g, conv_ln_b, conv_out_w, fnet_w, w_o,
              norm_ff_g, w_ff1, w_ff2, norm_ple_g, w_ple_gate, w_ple_proj, final_norm_g):
    y_prompt = trunk(x_prompt, p_prompt, norm_mix_g, w_in, b_gate, pool_w, pool_scale, conv_w,
                     conv_b, conv_ln_g, conv_ln_b, conv_out_w, fnet_w, w_o, norm_ff_g, w_ff1,
                     w_ff2, norm_ple_g, w_ple_gate, w_ple_proj, final_norm_g)
    y_sample = trunk(x_sample, p_sample, norm_mix_g, w_in, b_gate, pool_w, pool_scale, conv_w,
                     conv_b, conv_ln_g, conv_ln_b, conv_out_w, fnet_w, w_o, norm_ff_g, w_ff1,
                     w_ff2, norm_ple_g, w_ple_gate, w_ple_proj, final_norm_g)
    return (y_prompt, y_sample)
```

```python
import contextlib
import numpy as np
import ml_dtypes
import concourse.bass as bass
import concourse.mybir as mybir
from concourse.bass_utils import run_bass_kernel_spmd

F32 = mybir.dt.float32
BF16 = mybir.dt.bfloat16
AF = mybir.ActivationFunctionType
ALU = mybir.AluOpType

D = 1024
NT = 8192
T = 512
NTILE = NT // T
L = 2
EPS = 1e-6
NPANEL = 44
PAN = 4096
WINS = (2, 4, 8, 16)

DCH = (0, 1)

P_POOL = 0
P_GA = 2
P_POOLW = 4
P_CG0, P_CA0, P_CG1, P_CA1 = 5, 6, 7, 8
P_CD = 9
P_GB = 17
P_CO = 19
P_GC = 21
P_WO = 23
P_FF1 = 25
P_FF2 = 33
P_PG = 41
P_PP = 43

ORDER = [0, 1, 2, 3, 4, 5, 6, 7, 8] + [9 + c for c in range(8) if c not in DCH] + [21, 22, 17, 18, 19, 20, 23, 24] + list(range(25, 41)) + [43, 41, 42]

V_G1, V_BG, V_PS, V_CB, V_LG, V_LB, V_G2, V_G3 = 0, 8, 32, 40, 48, 56, 64, 72
V_GF = 160
V_CW = 168
NVEC = 168 + 2 * 248
C_MSEG, C_EPS, C_TAB = 0, 1, 8
NCST = 136


class Buf:
    __slots__ = ("name", "w", "r")

    def __init__(self, name):
        self.name = name
        self.w = None
        self.r = {}


class EngState:
    def __init__(self, name, e, sem):
        self.name = name
        self.e = e
        self.sem = sem
        self.count = 0
        self.known = {}


class FW:
    def __init__(self, nc, n_dma=40):
        self.nc = nc
        self.eng = {}
        for name, e in (("pe", nc.tensor), ("act", nc.scalar), ("dve", nc.vector),
                        ("pool", nc.gpsimd), ("sp", nc.sync)):
            self.eng[name] = EngState(name, e, nc.alloc_semaphore("s_" + name))
        self.dsem = [nc.alloc_semaphore("s_dma%d" % i) for i in range(n_dma)]
        self.dgen = [0] * n_dma
        self.dnext = 0
        self.snap = {}
        self.nwait = 0

    def _wait(self, es, key, val, skip_self=False):
        if es.known.get(key, 0) >= val:
            return
        if key == es.name:
            if es.name == "pe" or skip_self or es.count - val >= 3:
                es.known[key] = val
                return
        if isinstance(key, str):
            es.e.wait_ge(self.eng[key].sem, val)
        else:
            es.e.wait_ge(self.dsem[key[1]], 16 * val)
        self.nwait += 1
        es.known[key] = val
        sn = self.snap.get((key, val))
        if sn:
            kn = es.known
            for k, v in sn.items():
                if kn.get(k, 0) < v:
                    kn[k] = v

    def _deps(self, es, reads, writes, extra=None, skip_self=False):
        evs = {}
        for b in reads:
            if b.w is not None:
                k, v = b.w
                if evs.get(k, 0) < v:
                    evs[k] = v
        for b in writes:
            if b.w is not None:
                k, v = b.w
                if evs.get(k, 0) < v:
                    evs[k] = v
            for k, v in b.r.items():
                if evs.get(k, 0) < v:
                    evs[k] = v
        if extra is not None:
            k, v = extra
            if evs.get(k, 0) < v:
                evs[k] = v
        for k, v in evs.items():
            self._wait(es, k, v, skip_self)

    def _record(self, ev, es, reads, writes):
        kn = es.known
        self.snap[ev] = {k: kn[k] for k in self.eng if k in kn}
        k, v = ev
        for b in reads:
            if b.r.get(k, 0) < v:
                b.r[k] = v
        for b in writes:
            b.w = ev
            b.r = {}

    def op(self, eng, fn, reads=(), writes=(), skip_self=False):
        es = self.eng[eng]
        self._deps(es, reads, writes, None, skip_self)
        ins = fn(es.e)
        es.count += 1
        ins.then_inc(es.sem, 1)
        ev = (es.name, es.count)
        self._record(ev, es, reads, writes)
        return ev

    def group(self, eng, fns, reads=(), writes=()):
        es = self.eng[eng]
        self._deps(es, reads, writes)
        ins = None
        for fn in fns:
            ins = fn(es.e)
        es.count += 1
        ins.then_inc(es.sem, 1)
        ev = (es.name, es.count)
        self._record(ev, es, reads, writes)
        return ev

    def dma(self, q, out, in_, reads=(), writes=()):
        es = self.eng[q]
        slot = self.dnext
        self.dnext = (slot + 1) % len(self.dsem)
        gen = self.dgen[slot]
        extra = (("d", slot), gen) if gen > 0 else None
        self._deps(es, reads, writes, extra)
        ins = es.e.dma_start(out=out, in_=in_)
        ins.then_inc(self.dsem[slot], 16)
        self.dgen[slot] = gen + 1
        ev = (("d", slot), gen + 1)
        self._record(ev, es, reads, writes)
        return ev

    def barrier(self):
        for es in self.eng.values():
            for o in self.eng.values():
                if o is not es and o.count > 0:
                    self._wait(es, o.name, o.count)
            for slot, gen in enumerate(self.dgen):
                if gen > 0:
                    self._wait(es, ("d", slot), gen)


def _dft_consts(kind):
    al = np.arange(64)[:, None].astype(np.float64)
    ka = np.arange(64)[None, :].astype(np.float64)
    M1 = np.zeros((128, 128, 128), np.float64)
    for bt in range(128):
        if kind == "sample":
            ang = -2 * np.pi * (ka * al / 64 + ka * bt / 8192)
        else:
            ang = -2 * np.pi * (ka * al / 64 + ka * (bt % 32) / 2048)
        Er, Ei = np.cos(ang), np.sin(ang)
        M1[0:64, bt, 0:64] = Er
        M1[0:64, bt, 64:128] = Ei
        M1[64:128, bt, 0:64] = -Ei
        M1[64:128, bt, 64:128] = Er
    M2 = np.zeros((128, 2, 128), np.float64)
    if kind == "sample":
        b = np.arange(128)[:, None].astype(np.float64)
        j = np.arange(128)[None, :].astype(np.float64)
        M2[:, 0, :] = np.cos(2 * np.pi * j * b / 128) / np.sqrt(8192)
        M2[:, 1, :] = np.sin(2 * np.pi * j * b / 128) / np.sqrt(8192)
    else:
        bb = np.arange(32)[:, None].astype(np.float64)
        jj = np.arange(32)[None, :].astype(np.float64)
        for q in range(4):
            M2[32 * q:32 * q + 32, 0, 32 * q:32 * q + 32] = np.cos(2 * np.pi * jj * bb / 32) / np.sqrt(2048)
            M2[32 * q:32 * q + 32, 1, 32 * q:32 * q + 32] = np.sin(2 * np.pi * jj * bb / 32) / np.sqrt(2048)
    if kind == "sample":
        p = np.arange(128)
        old = ((p % 32) // 16) * 64 + 16 * (p // 32) + p % 16
        M1 = M1[old]
    M1 = np.ascontiguousarray(M1.reshape(128, 16, 8, 128).transpose(1, 0, 2, 3))
    z = np.zeros_like(M1)
    bf = ml_dtypes.bfloat16
    if kind == "sample":
        return M1.astype(bf), z.astype(bf), M2.astype(bf)
    return z.astype(bf), M1.astype(bf), M2.astype(bf)


def _cst_table(kind):
    c = np.zeros((128, NCST), np.float32)
    c[:, C_MSEG] = 1.0 if kind == "sample" else 0.0
    c[:, C_EPS] = EPS
    for k in range(2):
        for side in range(2):
            for g, w in enumerate(WINS):
                for j in range(8):
                    if k == 1 and kind == "sample":
                        cnt = w
                    elif side == 0:
                        cnt = min(w, j + w // 2)
                    else:
                        r = 7 - j
                        cnt = min(w, r + w // 2 + 1)
                    c[:, C_TAB + ((k * 2 + side) * 4 + g) * 8 + j] = 1.0 / cnt
    return c


def _fcc_table():
    cp = np.arange(256)[:, None].astype(np.float64)
    c = np.arange(256)[None, :].astype(np.float64)
    ang = 2 * np.pi * cp * c / 256
    fc = np.stack([np.cos(ang) / 16, -np.sin(ang) / 16], 1)
    return np.ascontiguousarray(fc.reshape(2, 128, 2, 256).transpose(1, 0, 2, 3)).astype(np.float32)


def _kpanel(W, col0, ncols=512, kc0=0, nkc=8):
    blk = W[kc0 * 128:(kc0 + nkc) * 128, col0:col0 + ncols].reshape(nkc, 128, ncols).transpose(1, 0, 2)
    out = np.zeros((128, PAN), np.float32)
    out[:, :nkc * ncols] = blk.reshape(128, nkc * ncols)
    return out


def _prep_weights(inp):
    wsrc = np.zeros((L * NPANEL, 128, PAN), np.float32)
    for l in range(L):
        w_in = inp["w_in"][l]
        P = wsrc[l * NPANEL:(l + 1) * NPANEL]
        for i in range(2):
            P[P_POOL + i] = _kpanel(w_in, 512 * i)
            P[P_GA + i] = _kpanel(w_in, 4096 + 512 * i)
            P[P_GB + i] = _kpanel(w_in, 5120 + 512 * i)
            P[P_GC + i] = _kpanel(w_in, 6144 + 512 * i)
            P[P_CO + i] = _kpanel(inp["conv_out_w"][l], 512 * i)
            P[P_WO + i] = _kpanel(inp["w_o"][l], 512 * i)
            P[P_PG + i] = _kpanel(inp["w_ple_gate"][l], 512 * i)
        P[P_CA0] = _kpanel(w_in, 1024)
        P[P_CA1] = _kpanel(w_in, 1536)
        P[P_CG0] = _kpanel(w_in, 2048)
        P[P_CG1] = _kpanel(w_in, 2560)
        pw = inp["pool_w"][l]
        blk = pw.reshape(4, 2, 128, 256).transpose(2, 0, 1, 3)
        P[P_POOLW][:, :2048] = blk.reshape(128, 2048)
        cw = inp["conv_w"][l]
        for ch in range(8):
            dg = np.zeros((128, 31, 128), np.float32)
            idx = np.arange(128)
            dg[idx, :, idx] = cw[:, ch * 128:(ch + 1) * 128].T
            P[P_CD + ch][:, :31 * 128] = dg.reshape(128, 31 * 128)
        for i in range(8):
            P[P_FF1 + i] = _kpanel(inp["w_ff1"][l], 512 * i)
        for og in range(2):
            for kg in range(4):
                P[P_FF2 + og * 4 + kg] = _kpanel(inp["w_ff2"][l], 512 * og, 512, kg * 8, 8)
        P[P_PP] = _kpanel(inp["w_ple_proj"][l], 0, 1024, 0, 2)
    wfT = np.zeros((L, 4, 128, 2, 1024), np.float32)
    fw = np.zeros((L, 4, 128, 2, 256), np.float32)
    for l in range(L):
        for g in range(4):
            wf = inp["w_in"][l][:, 3072 + 256 * g:3072 + 256 * (g + 1)]
            wfT[l, g] = wf.T.reshape(2, 128, 1024).transpose(1, 0, 2)
            fw[l, g] = inp["fnet_w"][l][g].reshape(2, 128, 256).transpose(1, 0, 2)
    vecs = np.zeros((128, NVEC), np.float32)

    def col(v):
        return np.asarray(v, np.float32).reshape(-1, 128).T

    for l in range(L):
        b = l * 80
        vecs[:, b + V_G1:b + V_G1 + 8] = col(inp["norm_mix_g"][l])
        vecs[:, b + V_BG:b + V_BG + 24] = col(inp["b_gate"][l])
        vecs[:, b + V_PS:b + V_PS + 8] = col(inp["pool_scale"][l])
        vecs[:, b + V_CB:b + V_CB + 8] = col(inp["conv_b"][l])
        vecs[:, b + V_LG:b + V_LG + 8] = col(inp["conv_ln_g"][l])
        vecs[:, b + V_LB:b + V_LB + 8] = col(inp["conv_ln_b"][l])
        vecs[:, b + V_G2:b + V_G2 + 8] = col(inp["norm_ff_g"][l])
        vecs[:, b + V_G3:b + V_G3 + 8] = col(inp["norm_ple_g"][l])
    vecs[:, V_GF:V_GF + 8] = col(inp["final_norm_g"])
    for l in range(L):
        cw = inp["conv_w"][l]
        for ch in range(8):
            vecs[:, V_CW + l * 248 + ch * 31:V_CW + l * 248 + (ch + 1) * 31] = cw[:, ch * 128:(ch + 1) * 128].T
    return wsrc, wfT, fw, vecs


def build_program(n_layers=L, debug=False):
    nc = bass.Bass("TRN2", target_bir_lowering=False)
    fw_ = FW(nc)
    op, group, dma = fw_.op, fw_.group, fw_.dma

    def din(name, shape, dt=F32):
        return nc.dram_tensor(name, list(shape), dt, kind="ExternalInput").ap()

    xT = din("xT", [D, NT])
    pT = din("pT", [L * 256, NT])
    wsrc = din("wsrc", [L * NPANEL, 128, PAN])
    wfT_d = din("wfT", [L, 4, 128, 2, 1024])
    fwd_d = din("fw", [L, 4, 128, 2, 256])
    vecs_d = din("vecs", [128, NVEC])
    cst_d = din("cst", [128, NCST])
    fcc_d = din("fcc", [128, 2, 2, 256])
    m1s_d = din("m1s", [16, 128, 8, 128], BF16)
    m1p_d = din("m1p", [16, 128, 8, 128], BF16)
    m2_d = din("m2", [128, 2, 128], BF16)
    ident_d = din("ident", [128, 128], BF16)
    yT = nc.dram_tensor("yT", [D, NT], F32, kind="ExternalOutput").ap()

    dk = "ExternalOutput" if debug else "Internal"
    wbf = nc.dram_tensor("wbf", [L * NPANEL, 128, PAN], BF16, kind=dk).ap()
    hscr = nc.dram_tensor("hscr", [D, NT], F32, kind=dk).ap()
    zs = nc.dram_tensor("zs", [4, 2, 2048, 1024], BF16, kind=dk).ap()
    ys = nc.dram_tensor("ys", [128, 128, 1024], BF16, kind=dk).ap()
    ycs = nc.dram_tensor("ycs", [NT, 1024], BF16, kind=dk).ap()
    wfs = nc.dram_tensor("wfs", [128, 8, 2048], BF16).ap()
    b_wfs = [Buf("wfs%d" % g) for g in range(8)]

    b_wbf = [Buf("wbf%d" % i) for i in range(L * NPANEL)]
    b_hscr = [Buf("hscr%d" % i) for i in range(NTILE)]
    b_zs = [Buf("zs%d_%d" % (r, i)) for r in range(2) for i in range(NTILE)]
    b_ys = [Buf("ys%d" % i) for i in range(16)]
    b_ycs = Buf("ycs")
    b_yT = Buf("yT")
    b_in = Buf("inputs")

    def sb(name, shape, dt):
        return nc.alloc_sbuf_tensor("sb_" + name, list(shape), dt)

    vecs = sb("vecs", [128, NVEC], F32).ap()
    cst = sb("cst", [128, NCST], F32).ap()
    ident = sb("ident", [128, 128], BF16).ap()
    ones_b = sb("ones_b", [128, 128], BF16).ap()
    ones_f = sb("ones_f", [128, 128], F32).ap()
    dummy = sb("dummy", [128, 2], F32).ap()
    b_dummy = Buf("dummy")

    def preload_ln_table():
        op("act", lambda e: e.activation(out=dummy[:, 0:1], in_=cst[:, C_EPS:C_EPS + 1], func=AF.Ln),
           [b_const], [b_dummy])
    b_const = Buf("const")
    psb = [nc.alloc_psum_tensor("ps%d" % i, [128, 512], F32).ap() for i in range(8)]
    b_ps = [Buf("ps%d" % i) for i in range(8)]
    ps_state = {"n": 0}

    def next_bank():
        i = ps_state["n"] % 8
        ps_state["n"] += 1
        return psb[i], b_ps[i]

    dma("sp", vecs, vecs_d, [b_in], [b_const])
    dma("sp", cst, cst_d, [b_in], [b_const])
    dma("sp", ident, ident_d, [b_in], [b_const])
    op("dve", lambda e: e.memset(ones_b, 1.0), [], [b_const])
    op("dve", lambda e: e.memset(ones_f, 1.0), [], [b_const])
    eps_ap = cst[:, C_EPS:C_EPS + 1]
    mseg_ap = cst[:, C_MSEG:C_MSEG + 1]

    evac_state = {"n": 0}

    def evac_copy(out, in_, reads, writes):
        evac_state["n"] += 1
        if evac_state["n"] % 2 == 0:
            op("act", lambda e: e.activation(out=out, in_=in_, func=AF.Copy), reads, writes)
        else:
            op("dve", lambda e: e.tensor_copy(out=out, in_=in_), reads, writes)

    dumped = {}

    def dbg_dump(name, ap, bufs):
        if not debug or name in dumped:
            return
        dumped[name] = 1
        shape = list(ap.shape)
        dt_ = ap.dtype
        o = nc.dram_tensor("dbg_" + name, shape, dt_, kind="ExternalOutput").ap()
        dma("pool", o, ap, list(bufs), [Buf("dbg")])

    def precast_panels(indices, stk, engines=("act", "dve", "act")):
        nbuf = len(engines)
        count = len(indices)
        first = indices[0]
        sf = [stk.enter_context(nc.sbuf_tensor("pc_f%d_%d" % (first, i), [128, PAN], F32)).ap() for i in range(nbuf)]
        sbf = [stk.enter_context(nc.sbuf_tensor("pc_b%d_%d" % (first, i), [128, PAN], BF16)).ap() for i in range(nbuf)]
        bf_ = [Buf("pcf%d" % i) for i in range(nbuf)]
        bb_ = [Buf("pcb%d" % i) for i in range(nbuf)]

        def load(j):
            s_ = j % nbuf
            dma("pool", sf[s_], wsrc[indices[j]], [b_in], [bf_[s_]])

        def cast(j):
            i = indices[j]
            s_ = j % nbuf
            eng = engines[s_]
            if eng == "act":
                op("act", lambda e: e.activation(out=sbf[s_], in_=sf[s_], func=AF.Copy), [bf_[s_]], [bb_[s_]])
            else:
                op(eng, lambda e: e.tensor_copy(out=sbf[s_], in_=sf[s_]), [bf_[s_]], [bb_[s_]])
            dma("pool", wbf[i], sbf[s_], [bb_[s_]], [b_wbf[i]])
        return [lambda j=j: load(j) for j in range(count)], [lambda j=j: cast(j) for j in range(count)]

    def rmsnorm(h, hb, sq, sqb, rs, rsb, tmp, tmpb, xn, xnb, gcol, n):
        def pick(lst, c):
            return lst[c] if len(lst) == 8 else lst[0]
        for c in range(8):
            op("act", lambda e, c=c: e.activation(out=sq[:, c, :], in_=h[:, c, :], func=AF.Square),
               [pick(hb, c)], [pick(sqb, c)])
        ps, pb = next_bank()
        for c in range(8):
            group("pe", [lambda e, c=c: e.matmul(ps[:, 0:n], lhsT=ones_b, rhs=sq[:, c, :], start=(c == 0),
                                                 stop=(c == 7))], [pick(sqb, c), b_const], [pb])
        op("act", lambda e: e.activation(out=tmp, in_=ps[:, 0:n], func=AF.Ln, bias=eps_ap, scale=1.0 / D),
           [pb, b_const], [tmpb])
        op("act", lambda e: e.activation(out=rs, in_=tmp, func=AF.Exp, scale=-0.5), [tmpb], [rsb])
        for c in range(8):
            op("dve", lambda e, c=c: e.scalar_tensor_tensor(out=xn[:, c, :], in0=h[:, c, :],
                                                            scalar=vecs[:, gcol + c:gcol + c + 1], in1=rs,
                                                            op0=ALU.mult, op1=ALU.mult),
               [pick(hb, c), rsb, b_const], [pick(xnb, c)])

    def layer(l):
        hsrc = xT if l == 0 else hscr
        vb = l * 80

        def hsrc_bufs(i):
            return [b_in] if l == 0 else [b_hscr[i]]

        stk_box = [contextlib.ExitStack()]

        def tsb(name, shape, dt):
            return stk_box[0].enter_context(nc.sbuf_tensor("L%d_%s" % (l, name), list(shape), dt)).ap()

        def end_phase():
            fw_.barrier()
            stk_box[0].close()
            stk_box[0] = contextlib.ExitStack()

        wfp = tsb("wfp", [128, 8, 2048], BF16)
        b_wfp = Buf("wfp")

        def wf_temps(alloc, tag):
            d = dict(wft=[alloc(tag + "wft%d" % k, [128, 2, 1024], F32) for k in range(2)],
                     fwt=[alloc(tag + "fwt%d" % k, [128, 2, 256], F32) for k in range(2)],
                     gsb=alloc(tag + "gsb", [128, 2, 512], F32), fcc=alloc(tag + "fcc", [128, 2, 2, 256], F32),
                     b_wft=[Buf("wft0"), Buf("wft1")], b_fwt=[Buf("fwt0"), Buf("fwt1")],
                     b_gsb=Buf("gsb"), b_fcc=Buf("fcc"))
            dma("sp", d["fcc"], fcc_d, [b_in], [d["b_fcc"]])
            return d

        def wf_load(ll, g, d):
            dma("sp", d["wft"][g % 2], wfT_d[ll, g], [b_in], [d["b_wft"][g % 2]])
            dma("sp", d["fwt"][g % 2], fwd_d[ll, g], [b_in], [d["b_fwt"][g % 2]])

        def wf_group(ll, g, d, out_fn):
            wft, fwt, gsb, fcc = d["wft"][g % 2], d["fwt"][g % 2], d["gsb"], d["fcc"]
            b_wft_, b_fwt_ = d["b_wft"][g % 2], d["b_fwt"][g % 2]
            for cc in range(2):
                ps, pb = next_bank()
                for ri in range(2):
                    group("pe", [lambda e, kc=kc, ri=ri, cc=cc: e.matmul(
                        ps[:, ri * 256:(ri + 1) * 256], lhsT=fcc[:, kc, ri, cc * 128:(cc + 1) * 128],
                        rhs=fwt[:, kc, :], start=(kc == 0), stop=(kc == 1)) for kc in range(2)],
                        [d["b_fcc"], b_fwt_], [pb])
                evac_copy(gsb[:, cc, :], ps, [pb], [d["b_gsb"]])
            for dk_ in range(8):
                ps, pb = next_bank()
                group("pe", [lambda e, kc=kc, dk_=dk_: e.matmul(
                    ps, lhsT=wft[:, kc, dk_ * 128:(dk_ + 1) * 128], rhs=gsb[:, kc, :],
                    start=(kc == 0), stop=(kc == 1)) for kc in range(2)], [b_wft_, d["b_gsb"]], [pb])
                out_fn(g, dk_, ps, pb)

        pcl, pcc = [], []
        if l == 0:
            pc_idx = [p_ for p_ in range(NPANEL) if p_ not in [P_CD + c for c in DCH]]
            pcl, pcc = precast_panels(pc_idx, stk_box[0])
        hA = [tsb("hA%d" % i, [128, 8, T], F32) for i in range(2)]
        b_hA = [Buf("hA%d" % i) for i in range(2)]
        sqA1 = tsb("sqA", [128, 8, T], BF16)
        sqA = [sqA1, sqA1]
        b_sqA1 = [Buf("sqA_%d" % c) for c in range(8)]
        b_sqA = [b_sqA1, b_sqA1]
        xnA = [tsb("xnA%d" % i, [128, 8, T], BF16) for i in range(2)]
        b_xnA = [[Buf("xnA%d_%d" % (i, c)) for c in range(8)] for i in range(2)]
        rsA = tsb("rsA", [128, T], F32)
        tmA = tsb("tmA", [128, T], F32)
        b_rsA, b_tmA = Buf("rsA"), Buf("tmA")

        def loadA(i):
            dma("sp", hA[i % 2], hsrc[:, i * T:(i + 1) * T].rearrange("(c p) t -> p c t", p=128),
                hsrc_bufs(i), [b_hA[i % 2]])

        def normA(i):
            s_ = i % 2
            rmsnorm(hA[s_], [b_hA[s_]], sqA[s_], b_sqA[s_], rsA, b_rsA, tmA, b_tmA, xnA[s_], b_xnA[s_], vb + V_G1, T)

        if l == 0 or n_layers == 1:
            sub = contextlib.ExitStack()

            def ssb(name, shape, dt):
                return sub.enter_context(nc.sbuf_tensor("L%d_%s" % (l, name), list(shape), dt)).ap()
            d0 = wf_temps(ssb, "a")

            def out_sb(g, dk_, ps, pb):
                o = wfp[:, dk_, :].rearrange("p (r g m) -> p r g m", r=2, g=4)[:, :, g, :]
                evac_copy(o, ps.rearrange("p (r m) -> p r m", r=2), [pb], [b_wfp])
            outstanding = []
            wf_load(l, 0, d0)
            for g in range(4):
                if g + 1 < 4:
                    wf_load(l, g + 1, d0)
                for _ in range(3):
                    if pcl:
                        pcl.pop(0)()
                        outstanding.append(pcc.pop(0))
                wf_group(l, g, d0, out_sb)
                if g == 0:
                    loadA(0)
                    loadA(1)
                for f_ in outstanding:
                    f_()
                outstanding = []
            normA(0)
            sub.close()
        else:
            loadA(0)
            dma("sp", wfp, wfs, b_wfs, [b_wfp])
            loadA(1)
            normA(0)

        ztA = [tsb("ztA%d" % i, [128, 4, 2048], BF16) for i in range(2)]
        b_ztA = [Buf("ztA%d" % i) for i in range(2)]
        pc_per_tile = -(-len(pcl) // NTILE)

        for i in range(NTILE):
            s = i % 2
            if i + 2 < NTILE:
                loadA(i + 2)
            npc_t = min(pc_per_tile, len(pcl))
            for _ in range(npc_t):
                pcl.pop(0)()
            for tb in range(4):
                if tb == 2 and i + 1 < NTILE:
                    normA(i + 1)
                for pn in range(4):
                    ps, pb = next_bank()
                    group("pe", [lambda e, kc=kc, tb=tb, pn=pn: e.matmul(
                        ps, lhsT=xnA[s][:, kc, tb * 128:(tb + 1) * 128], rhs=wfp[:, kc, pn * 512:(pn + 1) * 512],
                        start=(kc == 0), stop=(kc == 7)) for kc in range(8)], b_xnA[s] + [b_wfp], [pb])
                    evac_copy(ztA[s][:, tb, pn * 512:(pn + 1) * 512], ps, [pb], [b_ztA[s]])
            for ri in range(2):
                dma("sp", zs[i // 4, ri, (i % 4) * T:(i % 4 + 1) * T, :].rearrange("(tb p) c -> p tb c", p=128),
                    ztA[s][:, :, ri * 1024:(ri + 1) * 1024], [b_ztA[s]], [b_zs[ri * NTILE + i]])
            for _ in range(npc_t):
                pcc.pop(0)()
        assert not pcl and not pcc
        end_phase()

        in1s = [tsb("in1s%d" % i, [128, 8, 1024], BF16) for i in range(2)]
        in1p = [tsb("in1p%d" % i, [128, 8, 1024], BF16) for i in range(2)]
        m1s = [tsb("m1s%d" % i, [128, 8, 128], BF16) for i in range(2)]
        m1p = [tsb("m1p%d" % i, [128, 8, 128], BF16) for i in range(2)]
        o1 = [tsb("o1_%d" % i, [128, 8, 1024], BF16) for i in range(2)]
        b_in1 = [[Buf("in1_%d_%d" % (i, j)) for j in range(6)] for i in range(2)]
        do_wf1 = (l == 0 and n_layers > 1)
        if do_wf1:
            d1 = wf_temps(tsb, "b")
            wstg = tsb("wstg", [128, 8, 512], BF16)
            b_wstg = Buf("wstg")

            def out_dram(g, dk_, ps, pb):
                evac_copy(wstg[:, dk_, :], ps, [pb], [b_wstg])
                if dk_ == 7:
                    for r_ in range(2):
                        c0 = r_ * 1024 + g * 256
                        dma("sp", wfs[:, :, c0:c0 + 256], wstg[:, :, r_ * 256:(r_ + 1) * 256],
                            [b_wstg], [b_wfs[2 * g + r_]])
        b_o1 = [Buf("o1_%d" % i) for i in range(2)]
        for gi in range(16):
            s = gi % 2
            b0 = gi * 8
            q, bp0 = b0 // 32, b0 % 32
            src = zs.rearrange("q r t c -> (q r t) c").rearrange("(p b) c -> p b c", b=128)[:, b0:b0 + 8, :]
            dma("sp", in1s[s], src, b_zs, [b_in1[s][0], b_in1[s][1]])
            src = zs[q].rearrange("r t c -> (r t) c").rearrange("(p b) c -> p b c", b=32)[:, bp0:bp0 + 8, :]
            dma("sp", in1p[s], src, b_zs, [b_in1[s][2], b_in1[s][3]])
            dma("sp", m1s[s], m1s_d[gi], [b_in], [b_in1[s][4]])
            dma("sp", m1p[s], m1p_d[gi], [b_in], [b_in1[s][5]])
            for bb in range(8):
                for hf in range(2):
                    ps, pb = next_bank()
                    group("pe", [
                        lambda e, bb=bb, hf=hf, ps=ps: e.matmul(ps, lhsT=m1s[s][:, bb, :],
                                                                rhs=in1s[s][:, bb, hf * 512:(hf + 1) * 512],
                                                                start=True, stop=False),
                        lambda e, bb=bb, hf=hf, ps=ps: e.matmul(ps, lhsT=m1p[s][:, bb, :],
                                                                rhs=in1p[s][:, bb, hf * 512:(hf + 1) * 512],
                                                                start=False, stop=True)],
                        b_in1[s], [pb])
                    evac_copy(o1[s][:, bb, hf * 512:(hf + 1) * 512], ps, [pb], [b_o1[s]])
            dma("pool", ys[b0:b0 + 8].rearrange("b p c -> p b c"), o1[s], [b_o1[s]], [b_ys[gi]])
            if do_wf1 and gi % 4 == 1:
                wf_load(1, gi // 4, d1)
                wf_group(1, gi // 4, d1, out_dram)
        end_phase()

        y2g = [tsb("y2g%d" % i, [128, 2, 8, 1024], BF16) for i in range(2)]
        b_y2g = [Buf("y2g%d" % i) for i in range(2)]
        m2 = tsb("m2", [128, 2, 128], BF16)
        b_m2 = Buf("m2")
        o2 = [tsb("o2_%d" % i, [128, 8, 1024], BF16) for i in range(2)]
        b_o2 = [Buf("o2_%d" % i) for i in range(2)]
        dma("sp", m2, m2_d, [b_in], [b_m2])
        ys4 = ys.rearrange("b (r k) c -> b r k c", r=2)
        for kg in range(8):
            s = kg % 2
            dma("sp", y2g[s], ys4[:, :, kg * 8:(kg + 1) * 8, :], b_ys, [b_y2g[s]])
            for kk in range(8):
                for hf in range(2):
                    ps, pb = next_bank()
                    group("pe", [lambda e, r=r, kk=kk, hf=hf, ps=ps: e.matmul(
                        ps, lhsT=m2[:, r, :], rhs=y2g[s][:, r, kk, hf * 512:(hf + 1) * 512],
                        start=(r == 0), stop=(r == 1)) for r in range(2)], [b_m2, b_y2g[s]], [pb])
                    evac_copy(o2[s][:, kk, hf * 512:(hf + 1) * 512], ps, [pb], [b_o2[s]])
            dst = ycs.rearrange("(j k) c -> j k c", k=64)[:, kg * 8:(kg + 1) * 8, :]
            dma("pool", dst, o2[s], [b_o2[s]], [b_ycs])
        end_phase()

        NR = 4
        wr = [tsb("wr%d" % i, [128, PAN], BF16) for i in range(NR)]
        b_wr = [Buf("wr%d" % i) for i in range(NR)]
        hB = [tsb("h%d" % i, [128, 8, T], F32) for i in range(2)]
        b_hB = [[Buf("h%d_%d" % (i, c)) for c in range(8)] for i in range(2)]
        hhB = [tsb("hh%d" % i, [128, 8, 2, 16], F32) for i in range(2)]
        b_hhL = [Buf("hhL%d" % i) for i in range(2)]
        b_hhR = [Buf("hhR%d" % i) for i in range(2)]
        xn = tsb("xn", [128, 8, T], BF16)
        b_xn = [Buf("xn%d" % c) for c in range(8)]
        xn1 = tsb("xn1", [128, 8, T], BF16)
        b_xn1 = [Buf("xn1_%d" % c) for c in range(8)]
        xh = tsb("xh", [128, 8, 2, 16], BF16)
        b_xh = Buf("xh")
        sq = tsb("sq", [128, 8, T], BF16)
        b_sq = [Buf("sq%d" % c) for c in range(8)]
        sqh = tsb("sqh", [128, 8, 2, 16], BF16)
        b_sqh = Buf("sqh")
        rs = tsb("rs", [128, T], F32)
        tm = tsb("tm", [128, T], F32)
        mu = tsb("mu", [128, T], F32)
        msq = tsb("msq", [128, T], F32)
        rsh = tsb("rsh", [128, 32], F32)
        tmh = tsb("tmh", [128, 32], F32)
        b_rs, b_tm, b_mu, b_msq, b_rsh, b_tmh = (Buf(n) for n in ("rs", "tm", "mu", "msq", "rsh", "tmh"))
        upy = tsb("upy", [128, 4352], BF16)
        up = upy.bitcast(F32).rearrange("p (c t) -> p c t", c=4)
        yct = upy[:, 0:4096].rearrange("p (b c) -> p b c", b=4)
        b_up = [Buf("up%d" % c) for c in range(4)]
        stt = tsb("st", [128, 3, 544], F32)
        st = [stt[:, k, :] for k in range(3)]
        b_st = [Buf("st%d" % k) for k in range(3)]
        pt = stt[:, 0:2, 0:512]
        pl = tsb("pl", [128, 8, T], BF16)
        b_pl = [Buf("pl%d" % c) for c in range(8)]
        gt = tsb("gt", [128, 8, T], BF16)
        b_gt = [Buf("gt%d" % c) for c in range(8)]
        sg = tsb("sg", [128, 4, 544], BF16)
        b_sg = [Buf("sg%d" % c) for c in range(4)]
        vx = tsb("vx", [128, 8, 544], BF16)
        b_vx = [Buf("vx%d" % c) for c in range(8)]
        big = tsb("big", [128, 32, T], BF16)
        hid = big
        bigf = big.rearrange("p c t -> p (c t)").bitcast(F32)
        cv = bigf[:, 0:4096].rearrange("p (c t) -> p c t", c=8)
        mg = bigf[:, 4096:8192].rearrange("p (c t) -> p c t", c=8)
        b_cv = [Buf("cv%d" % c) for c in range(8)]
        b_mg = [Buf("mg%d" % c) for c in range(8)]
        tf = [tsb("tf%d" % i, [128, T], F32) for i in range(3)]
        b_tf = [Buf("tf%d" % i) for i in range(3)]
        rl = [tsb("rl%d" % i, [128, T], BF16) for i in range(2)]
        b_rl = [Buf("rl%d" % i) for i in range(2)]
        pbf = tsb("pbf", [128, 2, T], BF16)
        b_pbf = Buf("pbf")
        mb = sq
        b_mb = b_sq
        tstate = {"n": 0, "r": 0}

        def next_tf():
            k = tstate["n"] % 3
            tstate["n"] += 1
            return tf[k], b_tf[k]

        def next_rl():
            k = tstate["r"] % 2
            tstate["r"] += 1
            return rl[k], b_rl[k]

        wstate = {"issued": 0, "used": 0}
        do_pc = (l == 0 and n_layers > 1)
        seq = []
        npc = 0
        for i in range(NTILE):
            for pi in ORDER:
                seq.append(("w", pi))
                if do_pc and P_FF2 <= pi < P_FF2 + 8 and npc < 2 * NPANEL:
                    seq.append(("c", npc))
                    npc += 1
        assert (not do_pc) or npc == 2 * NPANEL
        cst_b = [tsb("cstg%d" % k, [128, PAN // 2], BF16) for k in range(2)] if do_pc else []
        b_cstg = [Buf("cstg%d" % k) for k in range(2)]

        def ring_issue(u):
            while wstate["issued"] < min(len(seq), u + NR):
                j = wstate["issued"]
                kind, a = seq[j]
                if kind == "w":
                    gi = l * NPANEL + a
                    dma("sp", wr[j % NR], wbf[gi], [b_wbf[gi]], [b_wr[j % NR]])
                else:
                    gi = NPANEL + a // 2
                    hf = a % 2
                    dma("sp", wr[j % NR].bitcast(F32), wsrc[gi][:, hf * 2048:(hf + 1) * 2048], [b_in], [b_wr[j % NR]])
                wstate["issued"] += 1

        def panel(pi_expected):
            u = wstate["used"]
            assert seq[u] == ("w", pi_expected), (seq[u], pi_expected)
            ring_issue(u)
            wstate["used"] += 1
            return wr[u % NR], b_wr[u % NR]

        def precast_step():
            u = wstate["used"]
            if u >= len(seq) or seq[u][0] != "c":
                return
            a = seq[u][1]
            ring_issue(u)
            wstate["used"] += 1
            gi = NPANEL + a // 2
            hf = a % 2
            k = a % 2
            src = wr[u % NR].bitcast(F32)
            op("pool", lambda e: e.tensor_copy(out=cst_b[k], in_=src), [b_wr[u % NR]], [b_cstg[k]])
            dma("pool", wbf[gi][:, hf * 2048:(hf + 1) * 2048], cst_b[k], [b_cstg[k]], [b_wbf[gi]])

        def halo_view(a2d):
            return a2d.rearrange("p (a b) -> p a b", b=16)[:, 0:34:33, :]

        def load_h(i):
            s_ = i % 2
            s0 = i * T
            h, hh = hB[s_], hhB[s_]
            dma("pool", h, hsrc[:, s0:s0 + T].rearrange("(c p) t -> p c t", p=128), hsrc_bufs(i), b_hB[s_])
            if i > 0:
                dma("pool", hh[:, :, 0, :], hsrc[:, s0 - 16:s0].rearrange("(c p) t -> p c t", p=128),
                    hsrc_bufs(i - 1), [b_hhL[s_]])
            else:
                op("dve", lambda e: e.memset(hh[:, :, 0, :], 0.0), [], [b_hhL[s_]])
            if i < NTILE - 1:
                dma("pool", hh[:, :, 1, :], hsrc[:, s0 + T:s0 + T + 16].rearrange("(c p) t -> p c t", p=128),
                    hsrc_bufs(i + 1), [b_hhR[s_]])
            else:
                op("dve", lambda e: e.memset(hh[:, :, 1, :], 0.0), [], [b_hhR[s_]])
            if i % 4 == 0 and i > 0:
                op("pool", lambda e: e.tensor_scalar(out=hh[:, :, 0, :], in0=hh[:, :, 0, :], scalar1=mseg_ap,
                                                     scalar2=1.0, op0=ALU.mult, op1=ALU.mult),
                   [b_hhL[s_], b_const], [b_hhL[s_]])
            if i % 4 == 3 and i < NTILE - 1:
                op("pool", lambda e: e.tensor_scalar(out=hh[:, :, 1, :], in0=hh[:, :, 1, :], scalar1=mseg_ap,
                                                     scalar2=1.0, op0=ALU.mult, op1=ALU.mult),
                   [b_hhR[s_], b_const], [b_hhR[s_]])

        def norm1(i):
            s_ = i % 2
            h, hh = hB[s_], hhB[s_]
            rmsnorm(h, b_hB[s_], sq, b_sq, rs, b_rs, tm, b_tm, xn1, b_xn1, vb + V_G1, T)
            op("act", lambda e: e.activation(out=sqh, in_=hh, func=AF.Square), [b_hhL[s_], b_hhR[s_]], [b_sqh])
            ps, pb = next_bank()
            group("pe", [lambda e, c=c, ps=ps: e.matmul(ps[:, 0:32].rearrange("p (a b) -> p a b", b=16),
                                                         lhsT=ones_b, rhs=sqh[:, c, :, :],
                                                         start=(c == 0), stop=(c == 7)) for c in range(8)],
                  [b_sqh, b_const], [pb])
            op("act", lambda e, ps=ps: e.activation(out=tmh, in_=ps[:, 0:32], func=AF.Ln, bias=eps_ap,
                                                    scale=1.0 / D), [pb, b_const], [b_tmh])
            op("act", lambda e: e.activation(out=rsh, in_=tmh, func=AF.Exp, scale=-0.5), [b_tmh], [b_rsh])
            for c in range(8):
                op("dve", lambda e, c=c: e.scalar_tensor_tensor(
                    out=xh[:, c, :, :], in0=hh[:, c, :, :], scalar=vecs[:, vb + V_G1 + c:vb + V_G1 + c + 1],
                    in1=rsh.rearrange("p (a b) -> p a b", b=16), op0=ALU.mult, op1=ALU.mult),
                    [b_hhL[s_], b_hhR[s_], b_rsh, b_const], [b_xh])

        def proj(pi, rhs_fn, rhs_bufs, nkc, consume, halo_fn=None, ncols=512, fine=False):
            w, wb = panel(pi)
            w3 = w[:, 0:nkc * ncols].rearrange("p (k n) -> p k n", k=nkc)
            for ch in range(ncols // 128):
                ps, pb = next_bank()
                if fine and ch == 0:
                    for kc in range(nkc):
                        group("pe", [lambda e, kc=kc, ch=ch, ps=ps: e.matmul(
                            ps, lhsT=w3[:, kc, ch * 128:(ch + 1) * 128], rhs=rhs_fn(kc),
                            start=(kc == 0), stop=(kc == nkc - 1))], [wb, rhs_bufs[kc]], [pb])
                else:
                    group("pe", [lambda e, kc=kc, ch=ch, ps=ps: e.matmul(
                        ps, lhsT=w3[:, kc, ch * 128:(ch + 1) * 128], rhs=rhs_fn(kc),
                        start=(kc == 0), stop=(kc == nkc - 1)) for kc in range(nkc)], [wb] + rhs_bufs, [pb])
                psh = pbh = None
                if halo_fn is not None:
                    psh, pbh = next_bank()
                    pshv = psh[:, 0:32].rearrange("p (a b) -> p a b", b=16)
                    group("pe", [lambda e, kc=kc, ch=ch, pshv=pshv: e.matmul(
                        pshv, lhsT=w3[:, kc, ch * 128:(ch + 1) * 128], rhs=halo_fn(kc),
                        start=(kc == 0), stop=(kc == nkc - 1)) for kc in range(nkc)], [wb, b_xh], [pbh])
                consume(ch, ps, pb, psh, pbh)

        xn1_main = lambda kc: xn1[:, kc, :]
        xn_main = lambda kc: xn[:, kc, :]
        xn_halo = lambda kc: xh[:, kc, :, :]

        def gate_proj(pbase, branch):
            for pp in range(2):
                def cons_g(ch, ps, pb, psh, pbh, pp=pp):
                    gch = pp * 4 + ch
                    col = vb + V_BG + 8 * branch + gch
                    op("act", lambda e: e.activation(out=gt[:, gch, :], in_=ps, func=AF.Sigmoid,
                                                     bias=vecs[:, col:col + 1], scale=1.0),
                       [pb, b_const], [b_gt[gch]])
                proj(pbase + pp, xn1_main, b_xn1, 8, cons_g)

        def mixer(i):
            s_ = i % 2
            s0 = i * T
            h, b_h = hB[s_], b_hB[s_]
            left_edge = (i % 4 == 0)
            right_edge = (i % 4 == 3)
            kindL = 0 if i == 0 else 1
            kindR = 0 if i == NTILE - 1 else 1

            for pp in range(2):
                def cons_pool(ch, ps, pb, psh, pbh, pp=pp):
                    op("act", lambda e: e.activation(out=up[:, ch, 16:528], in_=ps, func=AF.Copy), [pb], [b_up[ch]])
                    op("act", lambda e: e.activation(out=halo_view(up[:, ch, :]),
                                                     in_=psh[:, 0:32].rearrange("p (a b) -> p a b", b=16),
                                                     func=AF.Copy), [pbh], [b_up[ch]])
                    g = pp * 2 + ch // 2
                    w = WINS[g]
                    gch = pp * 4 + ch
                    u = up[:, ch, :]
                    cur, cb = u, b_up[ch]
                    lvl = 0
                    ranges = {2: (9, 535), 4: (10, 534), 8: (12, 532), 16: (16, 528)}
                    offs = {2: (-1, 0), 4: (-1, 1), 8: (-2, 2), 16: (-4, 4)}
                    ww = 2
                    while ww <= w:
                        a, b = ranges[ww]
                        o0, o1_ = offs[ww]
                        dst, db = st[lvl % 3], b_st[lvl % 3]
                        op("dve", lambda e, dst=dst, cur=cur, a=a, b=b, o0=o0, o1_=o1_: e.tensor_tensor(
                            out=dst[:, a:b], in0=cur[:, a + o0:b + o0], in1=cur[:, a + o1_:b + o1_], op=ALU.add),
                            [cb], [db])
                        cur, cb = dst, db
                        lvl += 1
                        ww *= 2
                    op("dve", lambda e, cur=cur: e.scalar_tensor_tensor(
                        out=pl[:, gch, :], in0=cur[:, 16:528], scalar=1.0 / w, in1=u[:, 16:528],
                        op0=ALU.mult, op1=ALU.subtract), [cb, b_up[ch]], [b_pl[gch]])
                    for side, flag, kind in ((0, left_edge, kindL), (1, right_edge, kindR)):
                        if not flag:
                            continue
                        c0 = C_TAB + ((kind * 2 + side) * 4 + g) * 8
                        e0 = 16 if side == 0 else 520
                        m0 = 0 if side == 0 else 504
                        tfa, tfb = next_tf()
                        op("dve", lambda e, cur=cur, c0=c0, e0=e0, tfa=tfa: e.tensor_tensor(
                            out=tfa[:, 0:8], in0=cur[:, e0:e0 + 8], in1=cst[:, c0:c0 + 8], op=ALU.mult),
                            [cb, b_const], [tfb])
                        op("dve", lambda e, e0=e0, m0=m0, tfa=tfa: e.tensor_tensor(
                            out=pl[:, gch, m0:m0 + 8], in0=tfa[:, 0:8], in1=u[:, e0:e0 + 8], op=ALU.subtract),
                            [tfb, b_up[ch]], [b_pl[gch]])
                proj(P_POOL + pp, xn1_main, b_xn1, 8, cons_pool, halo_fn=xn_halo)
            dbg_dump('xn1', xn1, b_xn1)
            dbg_dump('xh', xh, [b_xh])
            dbg_dump('pl0', pl, b_pl)
            gate_proj(P_GA, 0)
            w, wb = panel(P_POOLW)
            w4 = w[:, 0:2048].rearrange("p (g k n) -> p g k n", g=4, k=2)
            for dch in range(8):
                g, half = dch // 2, dch % 2
                ps, pb = next_bank()
                group("pe", [lambda e, kc=kc, g=g, half=half, ps=ps: e.matmul(
                    ps, lhsT=w4[:, g, kc, half * 128:(half + 1) * 128], rhs=pl[:, 2 * g + kc, :],
                    start=(kc == 0), stop=(kc == 1)) for kc in range(2)],
                    [wb, b_pl[2 * g], b_pl[2 * g + 1]], [pb])
                op("dve", lambda e, dch=dch, ps=ps: e.scalar_tensor_tensor(
                    out=mg[:, dch, :], in0=ps, scalar=vecs[:, vb + V_PS + dch:vb + V_PS + dch + 1],
                    in1=gt[:, dch, :], op0=ALU.mult, op1=ALU.mult), [pb, b_const, b_gt[dch]], [b_mg[dch]])

            dma("pool", yct, ycs[s0:s0 + T, :].rearrange("(b p) c -> p b c", p=128), [b_ycs], b_up)

            bg = []
            for pp in range(2):
                def cons_cg(ch, ps, pb, psh, pbh):
                    op("act", lambda e: e.activation(out=sg[:, ch, 16:528], in_=ps, func=AF.Sigmoid),
                       [pb], [b_sg[ch]])
                    op("act", lambda e: e.activation(out=halo_view(sg[:, ch, :]),
                                                     in_=psh[:, 0:32].rearrange("p (a b) -> p a b", b=16),
                                                     func=AF.Sigmoid), [pbh], [b_sg[ch]])
                proj(P_CG0 if pp == 0 else P_CG1, xn1_main, b_xn1, 8, cons_cg, halo_fn=xn_halo)

                def cons_ca(ch, ps, pb, psh, pbh, pp=pp):
                    gch = pp * 4 + ch
                    op("dve", lambda e: e.tensor_tensor(out=vx[:, gch, 16:528], in0=ps, in1=sg[:, ch, 16:528],
                                                        op=ALU.mult), [pb, b_sg[ch]], [b_vx[gch]])
                    op("dve", lambda e: e.tensor_tensor(out=halo_view(vx[:, gch, :]),
                                                        in0=psh[:, 0:32].rearrange("p (a b) -> p a b", b=16),
                                                        in1=halo_view(sg[:, ch, :]), op=ALU.mult),
                       [pbh, b_sg[ch]], [b_vx[gch]])
                    if pp == 1:
                        for _ in range(4):
                            if bg:
                                bg.pop(0)()
                proj(P_CA0 if pp == 0 else P_CA1, xn1_main, b_xn1, 8, cons_ca, halo_fn=xn_halo)
                if pp == 0:
                    for k in range(31):
                        for ch in DCH:
                            wcol = vecs[:, V_CW + l * 248 + ch * 31 + k:V_CW + l * 248 + ch * 31 + k + 1]
                            if k == 0:
                                bcol = vecs[:, vb + V_CB + ch:vb + V_CB + ch + 1]
                                bg.append(lambda ch=ch, wcol=wcol, bcol=bcol: op(
                                    "dve", lambda e: e.tensor_scalar(out=cv[:, ch, :], in0=vx[:, ch, 1:513],
                                                                     scalar1=wcol, scalar2=bcol,
                                                                     op0=ALU.mult, op1=ALU.add),
                                    [b_vx[ch], b_const], [b_cv[ch]], skip_self=True))
                            else:
                                bg.append(lambda ch=ch, wcol=wcol, k=k: op(
                                    "dve", lambda e: e.scalar_tensor_tensor(
                                        out=cv[:, ch, :], in0=vx[:, ch, 1 + k:513 + k], scalar=wcol,
                                        in1=cv[:, ch, :], op0=ALU.mult, op1=ALU.add),
                                    [b_vx[ch], b_cv[ch], b_const], [b_cv[ch]], skip_self=True))
            while bg:
                bg.pop(0)()
            pe_chunks = [c for c in range(8) if c not in DCH]
            for ch in pe_chunks:
                w, wb = panel(P_CD + ch)
                w3 = w[:, 0:31 * 128].rearrange("p (k n) -> p k n", k=31)
                ps, pb = next_bank()
                group("pe", [lambda e, k=k, ch=ch, ps=ps, w3=w3: e.matmul(
                    ps, lhsT=w3[:, k, :], rhs=vx[:, ch, 1 + k:513 + k], start=(k == 0), stop=(k == 30))
                    for k in range(31)], [wb, b_vx[ch]], [pb])
                bcol = vecs[:, vb + V_CB + ch:vb + V_CB + ch + 1]
                op("act", lambda e, ch=ch, ps=ps, bcol=bcol: e.activation(out=cv[:, ch, :], in_=ps, func=AF.Identity,
                                                                          bias=bcol, scale=1.0),
                   [pb, b_const], [b_cv[ch]])
                op("act", lambda e, ch=ch, ps=ps, bcol=bcol: e.activation(out=sq[:, ch, :], in_=ps, func=AF.Square,
                                                                          bias=bcol, scale=1.0),
                   [pb, b_const], [b_sq[ch]])
                op("act", lambda e, ch=ch, ps=ps, bcol=bcol: e.activation(out=vx[:, ch, 16:528], in_=ps,
                                                                          func=AF.Identity, bias=bcol, scale=1.0),
                   [pb, b_const], [b_vx[ch]])
                if ch == pe_chunks[1]:
                    for dc in DCH:
                        op("act", lambda e, dc=dc: e.activation(out=sq[:, dc, :], in_=cv[:, dc, :], func=AF.Square),
                           [b_cv[dc]], [b_sq[dc]])
                        op("act", lambda e, dc=dc: e.activation(out=vx[:, dc, 16:528], in_=cv[:, dc, :],
                                                                func=AF.Identity), [b_cv[dc]], [b_vx[dc]])
            dbg_dump('vx', vx, b_vx)
            preload_ln_table()
            ps1, pb1 = next_bank()
            ps2, pb2 = next_bank()
            corder = list(DCH) + pe_chunks
            for n_, c in enumerate(corder):
                group("pe", [lambda e, c=c, n_=n_: e.matmul(ps2, lhsT=ones_b, rhs=sq[:, c, :], start=(n_ == 0),
                                                            stop=(n_ == 7))], [b_sq[c], b_const], [pb2])
                group("pe", [lambda e, c=c, n_=n_: e.matmul(ps1, lhsT=ones_b, rhs=vx[:, c, 16:528], start=(n_ == 0),
                                                            stop=(n_ == 7))], [b_vx[c], b_const], [pb1])
            op("act", lambda e: e.activation(out=mu, in_=ps1, func=AF.Identity, scale=1.0 / D), [pb1], [b_mu])
            op("dve", lambda e: e.tensor_tensor(out=msq, in0=mu, in1=mu, op=ALU.mult), [b_mu], [b_msq])
            op("dve", lambda e: e.scalar_tensor_tensor(out=tm, in0=ps2, scalar=1.0 / D, in1=msq,
                                                       op0=ALU.mult, op1=ALU.subtract), [pb2, b_msq], [b_tm])
            op("act", lambda e: e.activation(out=tm, in_=tm, func=AF.Ln, bias=eps_ap, scale=1.0),
               [b_tm, b_const], [b_tm])
            op("act", lambda e: e.activation(out=rs, in_=tm, func=AF.Exp, scale=-0.5), [b_tm], [b_rs])
            for ch in range(4, 8):
                op("pool", lambda e, ch=ch: e.tensor_tensor(out=cv[:, ch, :], in0=cv[:, ch, :], in1=mu,
                                                            op=ALU.subtract), [b_cv[ch], b_mu], [b_cv[ch]])
            for ch in range(8):
                if ch < 4:
                    op("dve", lambda e, ch=ch: e.tensor_tensor(out=cv[:, ch, :], in0=cv[:, ch, :], in1=mu,
                                                               op=ALU.subtract), [b_cv[ch], b_mu], [b_cv[ch]])
                op("dve", lambda e, ch=ch: e.tensor_tensor(out=cv[:, ch, :], in0=cv[:, ch, :], in1=rs, op=ALU.mult),
                   [b_cv[ch], b_rs], [b_cv[ch]], skip_self=True)
            gate_proj(P_GC, 2)
            for ch in range(8):
                op("act", lambda e, ch=ch: e.activation(
                    out=pl[:, ch, :], in_=cv[:, ch, :], func=AF.Silu,
                    bias=vecs[:, vb + V_LB + ch:vb + V_LB + ch + 1],
                    scale=vecs[:, vb + V_LG + ch:vb + V_LG + ch + 1]), [b_cv[ch], b_const], [b_pl[ch]])
            for dch in range(8):
                ps, pb = next_bank()
                psv = ps.bitcast(BF16)
                group("pe", [lambda e, b=b, dch=dch, psv=psv: e.transpose(
                    psv[:, b * 128:(b + 1) * 128], yct[:, b, dch * 128:(dch + 1) * 128], ident)
                    for b in range(4)], b_up + [b_const], [pb])
                tfa, tfb = next_tf()
                op("dve", lambda e, psv=psv, dch=dch, tfa=tfa: e.tensor_tensor(
                    out=tfa, in0=psv[:, 0:512], in1=gt[:, dch, :], op=ALU.mult), [pb, b_gt[dch]], [tfb])
                op("pool", lambda e, dch=dch, tfa=tfa: e.tensor_tensor(
                    out=mg[:, dch, :], in0=mg[:, dch, :], in1=tfa, op=ALU.add), [tfb, b_mg[dch]], [b_mg[dch]])
            gate_proj(P_GB, 1)
            for pp in range(2):
                def cons_co(ch, ps, pb, psh, pbh, pp=pp):
                    gch = pp * 4 + ch
                    tfa, tfb = next_tf()
                    op("dve", lambda e: e.tensor_tensor(out=tfa, in0=ps, in1=gt[:, gch, :], op=ALU.mult),
                       [pb, b_gt[gch]], [tfb])
                    op("dve", lambda e: e.tensor_tensor(out=mb[:, gch, :], in0=mg[:, gch, :], in1=tfa, op=ALU.add),
                       [tfb, b_mg[gch]], [b_mb[gch]])
                proj(P_CO + pp, lambda kc: pl[:, kc, :], list(b_pl), 8, cons_co)

            dbg_dump('ln', pl, b_pl)
            dbg_dump('mb', mb, b_mb)
            dbg_dump('rs', rs, [b_rs])
            dbg_dump('mu', mu, [b_mu])
            preload_ln_table()
            for pp in range(2):
                def cons_wo(ch, ps, pb, psh, pbh, pp=pp):
                    gch = pp * 4 + ch
                    op("dve", lambda e: e.tensor_tensor(out=h[:, gch, :], in0=ps, in1=h[:, gch, :], op=ALU.add),
                       [pb, b_h[gch]], [b_h[gch]])
                proj(P_WO + pp, lambda kc: mb[:, kc, :], list(b_mb), 8, cons_wo)

        def ffn(i):
            s_ = i % 2
            h, b_h = hB[s_], b_hB[s_]
            dbg_dump('h1', h, b_h)
            rmsnorm(h, b_h, sq, b_sq, rs, b_rs, tm, b_tm, xn, b_xn, vb + V_G2, T)
            for pp in range(8):
                def cons_ff1(ch, ps, pb, psh, pbh, pp=pp):
                    j = pp * 4 + ch
                    rla, rlb = next_rl()
                    op("act", lambda e: e.activation(out=rla, in_=ps, func=AF.Relu), [pb], [rlb])
                    op("dve", lambda e: e.tensor_tensor(out=hid[:, j, :], in0=ps, in1=rla, op=ALU.mult),
                       [pb, rlb], [b_cv[j // 2] if j < 16 else b_mg[(j - 16) // 2]])
                proj(P_FF1 + pp, xn_main, b_xn, 8, cons_ff1, fine=(pp == 0))
            if i + 1 < NTILE:
                norm1(i + 1)
            for og in range(2):
                banks = [next_bank() for _ in range(4)]
                for kg in range(4):
                    w, wb = panel(P_FF2 + og * 4 + kg)
                    w3 = w.rearrange("p (k n) -> p k n", k=8)
                    for ch in range(4):
                        ps, pb = banks[ch]
                        group("pe", [lambda e, kc=kc, ch=ch, ps=ps, w3=w3, kg=kg: e.matmul(
                            ps, lhsT=w3[:, kc, ch * 128:(ch + 1) * 128], rhs=hid[:, kg * 8 + kc, :],
                            start=(kg == 0 and kc == 0), stop=(kg == 3 and kc == 7)) for kc in range(8)],
                            [wb] + (b_cv[kg * 4:(kg + 1) * 4] if kg < 2 else b_mg[(kg - 2) * 4:(kg - 1) * 4]), [pb])
                        if kg == 3:
                            gch = og * 4 + ch
                            op("dve", lambda e, ps=ps, gch=gch: e.tensor_tensor(
                                out=h[:, gch, :], in0=ps, in1=h[:, gch, :], op=ALU.add), [pb, b_h[gch]], [b_h[gch]])
                    precast_step()

        def ple(i):
            s_ = i % 2
            s0 = i * T
            h, b_h = hB[s_], b_hB[s_]
            dbg_dump('h2', h, b_h)
            def cons_pp(ch, ps, pb, psh, pbh):
                op("act", lambda e: e.activation(out=cv[:, ch, :], in_=ps, func=AF.Copy), [pb], [b_cv[ch]])
            proj(P_PP, lambda kc: pbf[:, kc, :], [b_pbf], 2, cons_pp, ncols=1024)
            for pp in range(2):
                def cons_pg(ch, ps, pb, psh, pbh, pp=pp):
                    gch = pp * 4 + ch
                    op("act", lambda e: e.activation(out=gt[:, gch, :], in_=ps, func=AF.Sigmoid),
                       [pb], [b_gt[gch]])
                    tfa, tfb = next_tf()
                    op("dve", lambda e: e.tensor_tensor(out=tfa, in0=cv[:, gch, :], in1=gt[:, gch, :], op=ALU.mult),
                       [b_cv[gch], b_gt[gch]], [tfb])
                    op("dve", lambda e: e.tensor_tensor(out=h[:, gch, :], in0=h[:, gch, :], in1=tfa, op=ALU.add),
                       [tfb, b_h[gch]], [b_h[gch]])
                proj(P_PG + pp, xn_main, b_xn, 8, cons_pg, fine=(pp == 0))

            if l < n_layers - 1:
                dma("pool", hscr[:, s0:s0 + T].rearrange("(c p) t -> p c t", p=128), h, b_h, [b_hscr[i]])
            else:
                for c in range(8):
                    op("act", lambda e, c=c: e.activation(out=sq[:, c, :], in_=h[:, c, :], func=AF.Square),
                       [b_h[c]], [b_sq[c]])
                ps, pb = next_bank()
                group("pe", [lambda e, c=c, ps=ps: e.matmul(ps, lhsT=ones_b, rhs=sq[:, c, :], start=(c == 0),
                                                             stop=(c == 7)) for c in range(8)], b_sq + [b_const], [pb])
                op("act", lambda e, ps=ps: e.activation(out=tm, in_=ps, func=AF.Ln, bias=eps_ap, scale=1.0 / D),
                   [pb, b_const], [b_tm])
                op("act", lambda e: e.activation(out=rs, in_=tm, func=AF.Exp, scale=-0.5), [b_tm], [b_rs])
                for c in range(8):
                    op("dve", lambda e, c=c: e.scalar_tensor_tensor(
                        out=cv[:, c, :], in0=h[:, c, :], scalar=vecs[:, V_GF + c:V_GF + c + 1], in1=rs,
                        op0=ALU.mult, op1=ALU.mult), [b_h[c], b_rs, b_const], [b_cv[c]])
                dma("pool", yT[:, s0:s0 + T].rearrange("(c p) t -> p c t", p=128), cv, b_cv, [b_yT])

        load_h(0)
        norm1(0)
        for i in range(NTILE):
            s_ = i % 2
            if i + 1 < NTILE:
                load_h(i + 1)
            mixer(i)
            dma("pool", pt, pT[l * 256:(l + 1) * 256, i * T:(i + 1) * T].rearrange("(c p) t -> p c t", p=128),
                [b_in], [b_st[0], b_st[1]])
            op("dve", lambda e: e.tensor_copy(out=pbf, in_=pt), [b_st[0], b_st[1]], [b_pbf])
            ffn(i)
            rmsnorm(hB[s_], b_hB[s_], sq, b_sq, rs, b_rs, tm, b_tm, xn, b_xn, vb + V_G3, T)
            ple(i)
        end_phase()


    for l in range(n_layers):
        layer(l)
    fw_.barrier()
    return nc, fw_


_CACHE = {}


def kernel(**inputs):
    inp = {k: np.asarray(v) for k, v in inputs.items()}
    wsrc, wfT, fw, vecs = _prep_weights(inp)
    fcc = _fcc_table()
    ident = np.eye(128, dtype=np.float32).astype(ml_dtypes.bfloat16)
    consts = {}
    for kind in ("prompt", "sample"):
        m1s, m1p, m2 = _dft_consts(kind)
        consts[kind] = dict(m1s=m1s, m1p=m1p, m2=m2, cst=_cst_table(kind))
    xp, xs = inp["x_prompt"], inp["x_sample"]
    pp_, ps_ = inp["p_prompt"], inp["p_sample"]
    in_maps = []
    for c in range(8):
        if c < 4:
            kind = "prompt"
            x = xp[4 * c:4 * c + 4].reshape(NT, D)
            p = pp_[:, 4 * c:4 * c + 4].reshape(L, NT, 256)
        else:
            kind = "sample"
            x = xs[c - 4]
            p = ps_[:, c - 4]
        m = dict(
            xT=np.ascontiguousarray(x.T),
            pT=np.ascontiguousarray(p.transpose(0, 2, 1)).reshape(L * 256, NT),
            wsrc=wsrc, wfT=wfT, fw=fw, vecs=vecs, fcc=fcc, ident=ident,
        )
        m.update(consts[kind])
        in_maps.append(m)
    if "nc" not in _CACHE:
        _CACHE["nc"] = build_program()[0]
    res = run_bass_kernel_spmd(_CACHE["nc"], in_maps, core_ids=list(range(8)))
    outs = [np.asarray(r["yT"]).T for r in res.results]
    y_prompt = np.stack([o.reshape(4, 2048, D) for o in outs[:4]], 0).reshape(16, 2048, D).astype(np.float32)
    y_sample = np.stack(outs[4:], 0).astype(np.float32)
    return (np.ascontiguousarray(y_prompt), np.ascontiguousarray(y_sample))
```

```python
import contextlib
import numpy as np
import ml_dtypes
import concourse.bass as bass
import concourse.mybir as mybir
from concourse.bass_utils import run_bass_kernel_spmd

F32 = mybir.dt.float32
BF16 = mybir.dt.bfloat16
AF = mybir.ActivationFunctionType
ALU = mybir.AluOpType

D = 1024
NT = 8192
T = 512
NTILE = NT // T
L = 2
EPS = 1e-6
NPANEL = 44
PAN = 4096
WINS = (2, 4, 8, 16)

DCH = (0, 1)

P_POOL = 0
P_GA = 2
P_POOLW = 4
P_CG0, P_CA0, P_CG1, P_CA1 = 5, 6, 7, 8
P_CD = 9
P_GB = 17
P_CO = 19
P_GC = 21
P_WO = 23
P_FF1 = 25
P_FF2 = 33
P_PG = 41
P_PP = 43

ORDER = [0, 1, 2, 3, 4, 5, 6, 7, 8] + [9 + c for c in range(8) if c not in DCH] + [21, 22, 17, 18, 19, 20, 23, 24] + list(range(25, 41)) + [43, 41, 42]

V_G1, V_BG, V_PS, V_CB, V_LG, V_LB, V_G2, V_G3 = 0, 8, 32, 40, 48, 56, 64, 72
V_GF = 160
V_CW = 168
NVEC = 168 + 2 * 248
C_MSEG, C_EPS, C_TAB = 0, 1, 8
NCST = 136


class Buf:
    __slots__ = ("name", "w", "r")

    def __init__(self, name):
        self.name = name
        self.w = None
        self.r = {}


class EngState:
    def __init__(self, name, e, sem):
        self.name = name
        self.e = e
        self.sem = sem
        self.count = 0
        self.known = {}


class FW:
    def __init__(self, nc, n_dma=40):
        self.nc = nc
        self.eng = {}
        for name, e in (("pe", nc.tensor), ("act", nc.scalar), ("dve", nc.vector),
                        ("pool", nc.gpsimd), ("sp", nc.sync)):
            self.eng[name] = EngState(name, e, nc.alloc_semaphore("s_" + name))
        self.dsem = [nc.alloc_semaphore("s_dma%d" % i) for i in range(n_dma)]
        self.dgen = [0] * n_dma
        self.dnext = 0
        self.snap = {}
        self.nwait = 0

    def _wait(self, es, key, val, skip_self=False):
        if es.known.get(key, 0) >= val:
            return
        if key == es.name:
            if es.name == "pe" or skip_self or es.count - val >= 3:
                es.known[key] = val
                return
        if isinstance(key, str):
            es.e.wait_ge(self.eng[key].sem, val)
        else:
            es.e.wait_ge(self.dsem[key[1]], 16 * val)
        self.nwait += 1
        es.known[key] = val
        sn = self.snap.get((key, val))
        if sn:
            kn = es.known
            for k, v in sn.items():
                if kn.get(k, 0) < v:
                    kn[k] = v

    def _deps(self, es, reads, writes, extra=None, skip_self=False):
        evs = {}
        for b in reads:
            if b.w is not None:
                k, v = b.w
                if evs.get(k, 0) < v:
                    evs[k] = v
        for b in writes:
            if b.w is not None:
                k, v = b.w
                if evs.get(k, 0) < v:
                    evs[k] = v
            for k, v in b.r.items():
                if evs.get(k, 0) < v:
                    evs[k] = v
        if extra is not None:
            k, v = extra
            if evs.get(k, 0) < v:
                evs[k] = v
        for k, v in evs.items():
            self._wait(es, k, v, skip_self)

    def _record(self, ev, es, reads, writes):
        kn = es.known
        self.snap[ev] = {k: kn[k] for k in self.eng if k in kn}
        k, v = ev
        for b in reads:
            if b.r.get(k, 0) < v:
                b.r[k] = v
        for b in writes:
            b.w = ev
            b.r = {}

    def op(self, eng, fn, reads=(), writes=(), skip_self=False):
        es = self.eng[eng]
        self._deps(es, reads, writes, None, skip_self)
        ins = fn(es.e)
        es.count += 1
        ins.then_inc(es.sem, 1)
        ev = (es.name, es.count)
        self._record(ev, es, reads, writes)
        return ev

    def group(self, eng, fns, reads=(), writes=()):
        es = self.eng[eng]
        self._deps(es, reads, writes)
        ins = None
        for fn in fns:
            ins = fn(es.e)
        es.count += 1
        ins.then_inc(es.sem, 1)
        ev = (es.name, es.count)
        self._record(ev, es, reads, writes)
        return ev

    def dma(self, q, out, in_, reads=(), writes=()):
        es = self.eng[q]
        slot = self.dnext
        self.dnext = (slot + 1) % len(self.dsem)
        gen = self.dgen[slot]
        extra = (("d", slot), gen) if gen > 0 else None
        self._deps(es, reads, writes, extra)
        ins = es.e.dma_start(out=out, in_=in_)
        ins.then_inc(self.dsem[slot], 16)
        self.dgen[slot] = gen + 1
        ev = (("d", slot), gen + 1)
        self._record(ev, es, reads, writes)
        return ev

    def barrier(self):
        for es in self.eng.values():
            for o in self.eng.values():
                if o is not es and o.count > 0:
                    self._wait(es, o.name, o.count)
            for slot, gen in enumerate(self.dgen):
                if gen > 0:
                    self._wait(es, ("d", slot), gen)


def _dft_consts(kind):
    al = np.arange(64)[:, None].astype(np.float64)
    ka = np.arange(64)[None, :].astype(np.float64)
    M1 = np.zeros((128, 128, 128), np.float64)
    for bt in range(128):
        if kind == "sample":
            ang = -2 * np.pi * (ka * al / 64 + ka * bt / 8192)
        else:
            ang = -2 * np.pi * (ka * al / 64 + ka * (bt % 32) / 2048)
        Er, Ei = np.cos(ang), np.sin(ang)
        M1[0:64, bt, 0:64] = Er
        M1[0:64, bt, 64:128] = Ei
        M1[64:128, bt, 0:64] = -Ei
        M1[64:128, bt, 64:128] = Er
    M2 = np.zeros((128, 2, 128), np.float64)
    if kind == "sample":
        b = np.arange(128)[:, None].astype(np.float64)
        j = np.arange(128)[None, :].astype(np.float64)
        M2[:, 0, :] = np.cos(2 * np.pi * j * b / 128) / np.sqrt(8192)
        M2[:, 1, :] = np.sin(2 * np.pi * j * b / 128) / np.sqrt(8192)
    else:
        bb = np.arange(32)[:, None].astype(np.float64)
        jj = np.arange(32)[None, :].astype(np.float64)
        for q in range(4):
            M2[32 * q:32 * q + 32, 0, 32 * q:32 * q + 32] = np.cos(2 * np.pi * jj * bb / 32) / np.sqrt(2048)
            M2[32 * q:32 * q + 32, 1, 32 * q:32 * q + 32] = np.sin(2 * np.pi * jj * bb / 32) / np.sqrt(2048)
    if kind == "sample":
        p = np.arange(128)
        old = ((p % 32) // 16) * 64 + 16 * (p // 32) + p % 16
        M1 = M1[old]
    M1 = np.ascontiguousarray(M1.reshape(128, 16, 8, 128).transpose(1, 0, 2, 3))
    z = np.zeros_like(M1)
    bf = ml_dtypes.bfloat16
    if kind == "sample":
        return M1.astype(bf), z.astype(bf), M2.astype(bf)
    return z.astype(bf), M1.astype(bf), M2.astype(bf)


def _cst_table(kind):
    c = np.zeros((128, NCST), np.float32)
    c[:, C_MSEG] = 1.0 if kind == "sample" else 0.0
    c[:, C_EPS] = EPS
    for k in range(2):
        for side in range(2):
            for g, w in enumerate(WINS):
                for j in range(8):
                    if k == 1 and kind == "sample":
                        cnt = w
                    elif side == 0:
                        cnt = min(w, j + w // 2)
                    else:
                        r = 7 - j
                        cnt = min(w, r + w // 2 + 1)
                    c[:, C_TAB + ((k * 2 + side) * 4 + g) * 8 + j] = 1.0 / cnt
    return c


def _fcc_table():
    cp = np.arange(256)[:, None].astype(np.float64)
    c = np.arange(256)[None, :].astype(np.float64)
    ang = 2 * np.pi * cp * c / 256
    fc = np.stack([np.cos(ang) / 16, -np.sin(ang) / 16], 1)
    return np.ascontiguousarray(fc.reshape(2, 128, 2, 256).transpose(1, 0, 2, 3)).astype(np.float32)


def _kpanel(W, col0, ncols=512, kc0=0, nkc=8):
    blk = W[kc0 * 128:(kc0 + nkc) * 128, col0:col0 + ncols].reshape(nkc, 128, ncols).transpose(1, 0, 2)
    out = np.zeros((128, PAN), np.float32)
    out[:, :nkc * ncols] = blk.reshape(128, nkc * ncols)
    return out


def _prep_weights(inp):
    wsrc = np.zeros((L * NPANEL, 128, PAN), np.float32)
    for l in range(L):
        w_in = inp["w_in"][l]
        P = wsrc[l * NPANEL:(l + 1) * NPANEL]
        for i in range(2):
            P[P_POOL + i] = _kpanel(w_in, 512 * i)
            P[P_GA + i] = _kpanel(w_in, 4096 + 512 * i)
            P[P_GB + i] = _kpanel(w_in, 5120 + 512 * i)
            P[P_GC + i] = _kpanel(w_in, 6144 + 512 * i)
            P[P_CO + i] = _kpanel(inp["conv_out_w"][l], 512 * i)
            P[P_WO + i] = _kpanel(inp["w_o"][l], 512 * i)
            P[P_PG + i] = _kpanel(inp["w_ple_gate"][l], 512 * i)
        P[P_CA0] = _kpanel(w_in, 1024)
        P[P_CA1] = _kpanel(w_in, 1536)
        P[P_CG0] = _kpanel(w_in, 2048)
        P[P_CG1] = _kpanel(w_in, 2560)
        pw = inp["pool_w"][l]
        blk = pw.reshape(4, 2, 128, 256).transpose(2, 0, 1, 3)
        P[P_POOLW][:, :2048] = blk.reshape(128, 2048)
        cw = inp["conv_w"][l]
        for ch in range(8):
            dg = np.zeros((128, 31, 128), np.float32)
            idx = np.arange(128)
            dg[idx, :, idx] = cw[:, ch * 128:(ch + 1) * 128].T
            P[P_CD + ch][:, :31 * 128] = dg.reshape(128, 31 * 128)
        for i in range(8):
            P[P_FF1 + i] = _kpanel(inp["w_ff1"][l], 512 * i)
        for og in range(2):
            for kg in range(4):
                P[P_FF2 + og * 4 + kg] = _kpanel(inp["w_ff2"][l], 512 * og, 512, kg * 8, 8)
        P[P_PP] = _kpanel(inp["w_ple_proj"][l], 0, 1024, 0, 2)
    wfT = np.zeros((L, 4, 128, 2, 1024), np.float32)
    fw = np.zeros((L, 4, 128, 2, 256), np.float32)
    for l in range(L):
        for g in range(4):
            wf = inp["w_in"][l][:, 3072 + 256 * g:3072 + 256 * (g + 1)]
            wfT[l, g] = wf.T.reshape(2, 128, 1024).transpose(1, 0, 2)
            fw[l, g] = inp["fnet_w"][l][g].reshape(2, 128, 256).transpose(1, 0, 2)
    vecs = np.zeros((128, NVEC), np.float32)

    def col(v):
        return np.asarray(v, np.float32).reshape(-1, 128).T

    for l in range(L):
        b = l * 80
        vecs[:, b + V_G1:b + V_G1 + 8] = col(inp["norm_mix_g"][l])
        vecs[:, b + V_BG:b + V_BG + 24] = col(inp["b_gate"][l])
        vecs[:, b + V_PS:b + V_PS + 8] = col(inp["pool_scale"][l])
        vecs[:, b + V_CB:b + V_CB + 8] = col(inp["conv_b"][l])
        vecs[:, b + V_LG:b + V_LG + 8] = col(inp["conv_ln_g"][l])
        vecs[:, b + V_LB:b + V_LB + 8] = col(inp["conv_ln_b"][l])
        vecs[:, b + V_G2:b + V_G2 + 8] = col(inp["norm_ff_g"][l])
        vecs[:, b + V_G3:b + V_G3 + 8] = col(inp["norm_ple_g"][l])
    vecs[:, V_GF:V_GF + 8] = col(inp["final_norm_g"])
    for l in range(L):
        cw = inp["conv_w"][l]
        for ch in range(8):
            vecs[:, V_CW + l * 248 + ch * 31:V_CW + l * 248 + (ch + 1) * 31] = cw[:, ch * 128:(ch + 1) * 128].T
    return wsrc, wfT, fw, vecs


def build_program(n_layers=L, debug=False):
    nc = bass.Bass("TRN2", target_bir_lowering=False)
    fw_ = FW(nc)
    op, group, dma = fw_.op, fw_.group, fw_.dma

    def din(name, shape, dt=F32):
        return nc.dram_tensor(name, list(shape), dt, kind="ExternalInput").ap()

    xT = din("xT", [D, NT])
    pT = din("pT", [L * 256, NT])
    wsrc = din("wsrc", [L * NPANEL, 128, PAN])
    wfT_d = din("wfT", [L, 4, 128, 2, 1024])
    fwd_d = din("fw", [L, 4, 128, 2, 256])
    vecs_d = din("vecs", [128, NVEC])
    cst_d = din("cst", [128, NCST])
    fcc_d = din("fcc", [128, 2, 2, 256])
    m1s_d = din("m1s", [16, 128, 8, 128], BF16)
    m1p_d = din("m1p", [16, 128, 8, 128], BF16)
    m2_d = din("m2", [128, 2, 128], BF16)
    ident_d = din("ident", [128, 128], BF16)
    yT = nc.dram_tensor("yT", [D, NT], F32, kind="ExternalOutput").ap()

    dk = "ExternalOutput" if debug else "Internal"
    wbf = nc.dram_tensor("wbf", [L * NPANEL, 128, PAN], BF16, kind=dk).ap()
    hscr = nc.dram_tensor("hscr", [D, NT], F32, kind=dk).ap()
    zs = nc.dram_tensor("zs", [4, 2, 2048, 1024], BF16, kind=dk).ap()
    ys = nc.dram_tensor("ys", [128, 128, 1024], BF16, kind=dk).ap()
    ycs = nc.dram_tensor("ycs", [NT, 1024], BF16, kind=dk).ap()
    wfs = nc.dram_tensor("wfs", [128, 8, 2048], BF16).ap()
    b_wfs = [Buf("wfs%d" % g) for g in range(8)]

    b_wbf = [Buf("wbf%d" % i) for i in range(L * NPANEL)]
    b_hscr = [Buf("hscr%d" % i) for i in range(NTILE)]
    b_zs = [Buf("zs%d_%d" % (r, i)) for r in range(2) for i in range(NTILE)]
    b_ys = [Buf("ys%d" % i) for i in range(16)]
    b_ycs = Buf("ycs")
    b_yT = Buf("yT")
    b_in = Buf("inputs")

    def sb(name, shape, dt):
        return nc.alloc_sbuf_tensor("sb_" + name, list(shape), dt)

    vecs = sb("vecs", [128, NVEC], F32).ap()
    cst = sb("cst", [128, NCST], F32).ap()
    ident = sb("ident", [128, 128], BF16).ap()
    ones_b = sb("ones_b", [128, 128], BF16).ap()
    ones_f = sb("ones_f", [128, 128], F32).ap()
    dummy = sb("dummy", [128, 2], F32).ap()
    b_dummy = Buf("dummy")

    def preload_ln_table():
        op("act", lambda e: e.activation(out=dummy[:, 0:1], in_=cst[:, C_EPS:C_EPS + 1], func=AF.Ln),
           [b_const], [b_dummy])
    b_const = Buf("const")
    psb = [nc.alloc_psum_tensor("ps%d" % i, [128, 512], F32).ap() for i in range(8)]
    b_ps = [Buf("ps%d" % i) for i in range(8)]
    ps_state = {"n": 0}

    def next_bank():
        i = ps_state["n"] % 8
        ps_state["n"] += 1
        return psb[i], b_ps[i]

    dma("sp", vecs, vecs_d, [b_in], [b_const])
    dma("sp", cst, cst_d, [b_in], [b_const])
    dma("sp", ident, ident_d, [b_in], [b_const])
    op("dve", lambda e: e.memset(ones_b, 1.0), [], [b_const])
    op("dve", lambda e: e.memset(ones_f, 1.0), [], [b_const])
    eps_ap = cst[:, C_EPS:C_EPS + 1]
    mseg_ap = cst[:, C_MSEG:C_MSEG + 1]

    evac_state = {"n": 0}

    def evac_copy(out, in_, reads, writes):
        evac_state["n"] += 1
        if evac_state["n"] % 2 == 0:
            op("act", lambda e: e.activation(out=out, in_=in_, func=AF.Copy), reads, writes)
        else:
            op("dve", lambda e: e.tensor_copy(out=out, in_=in_), reads, writes)

    dumped = {}

    def dbg_dump(name, ap, bufs):
        if not debug or name in dumped:
            return
        dumped[name] = 1
        shape = list(ap.shape)
        dt_ = ap.dtype
        o = nc.dram_tensor("dbg_" + name, shape, dt_, kind="ExternalOutput").ap()
        dma("pool", o, ap, list(bufs), [Buf("dbg")])

    def precast_panels(indices, stk, engines=("act", "dve", "act")):
        nbuf = len(engines)
        count = len(indices)
        first = indices[0]
        sf = [stk.enter_context(nc.sbuf_tensor("pc_f%d_%d" % (first, i), [128, PAN], F32)).ap() for i in range(nbuf)]
        sbf = [stk.enter_context(nc.sbuf_tensor("pc_b%d_%d" % (first, i), [128, PAN], BF16)).ap() for i in range(nbuf)]
        bf_ = [Buf("pcf%d" % i) for i in range(nbuf)]
        bb_ = [Buf("pcb%d" % i) for i in range(nbuf)]

        def load(j):
            s_ = j % nbuf
            dma("pool", sf[s_], wsrc[indices[j]], [b_in], [bf_[s_]])

        def cast(j):
            i = indices[j]
            s_ = j % nbuf
            eng = engines[s_]
            if eng == "act":
                op("act", lambda e: e.activation(out=sbf[s_], in_=sf[s_], func=AF.Copy), [bf_[s_]], [bb_[s_]])
            else:
                op(eng, lambda e: e.tensor_copy(out=sbf[s_], in_=sf[s_]), [bf_[s_]], [bb_[s_]])
            dma("pool", wbf[i], sbf[s_], [bb_[s_]], [b_wbf[i]])
        return [lambda j=j: load(j) for j in range(count)], [lambda j=j: cast(j) for j in range(count)]

    def rmsnorm(h, hb, sq, sqb, rs, rsb, tmp, tmpb, xn, xnb, gcol, n):
        def pick(lst, c):
            return lst[c] if len(lst) == 8 else lst[0]
        for c in range(8):
            op("act", lambda e, c=c: e.activation(out=sq[:, c, :], in_=h[:, c, :], func=AF.Square),
               [pick(hb, c)], [pick(sqb, c)])
        ps, pb = next_bank()
        for c in range(8):
            group("pe", [lambda e, c=c: e.matmul(ps[:, 0:n], lhsT=ones_b, rhs=sq[:, c, :], start=(c == 0),
                                                 stop=(c == 7))], [pick(sqb, c), b_const], [pb])
        op("act", lambda e: e.activation(out=tmp, in_=ps[:, 0:n], func=AF.Ln, bias=eps_ap, scale=1.0 / D),
           [pb, b_const], [tmpb])
        op("act", lambda e: e.activation(out=rs, in_=tmp, func=AF.Exp, scale=-0.5), [tmpb], [rsb])
        for c in range(8):
            op("dve", lambda e, c=c: e.scalar_tensor_tensor(out=xn[:, c, :], in0=h[:, c, :],
                                                            scalar=vecs[:, gcol + c:gcol + c + 1], in1=rs,
                                                            op0=ALU.mult, op1=ALU.mult),
               [pick(hb, c), rsb, b_const], [pick(xnb, c)])

    def layer(l):
        hsrc = xT if l == 0 else hscr
        vb = l * 80

        def hsrc_bufs(i):
            return [b_in] if l == 0 else [b_hscr[i]]

        stk_box = [contextlib.ExitStack()]

        def tsb(name, shape, dt):
            return stk_box[0].enter_context(nc.sbuf_tensor("L%d_%s" % (l, name), list(shape), dt)).ap()

        def end_phase():
            fw_.barrier()
            stk_box[0].close()
            stk_box[0] = contextlib.ExitStack()

        wfp = tsb("wfp", [128, 8, 2048], BF16)
        b_wfp = Buf("wfp")

        def wf_temps(alloc, tag):
            d = dict(wft=[alloc(tag + "wft%d" % k, [128, 2, 1024], F32) for k in range(2)],
                     fwt=[alloc(tag + "fwt%d" % k, [128, 2, 256], F32) for k in range(2)],
                     gsb=alloc(tag + "gsb", [128, 2, 512], F32), fcc=alloc(tag + "fcc", [128, 2, 2, 256], F32),
                     b_wft=[Buf("wft0"), Buf("wft1")], b_fwt=[Buf("fwt0"), Buf("fwt1")],
                     b_gsb=Buf("gsb"), b_fcc=Buf("fcc"))
            dma("sp", d["fcc"], fcc_d, [b_in], [d["b_fcc"]])
            return d

        def wf_load(ll, g, d):
            dma("sp", d["wft"][g % 2], wfT_d[ll, g], [b_in], [d["b_wft"][g % 2]])
            dma("sp", d["fwt"][g % 2], fwd_d[ll, g], [b_in], [d["b_fwt"][g % 2]])

        def wf_group(ll, g, d, out_fn):
            wft, fwt, gsb, fcc = d["wft"][g % 2], d["fwt"][g % 2], d["gsb"], d["fcc"]
            b_wft_, b_fwt_ = d["b_wft"][g % 2], d["b_fwt"][g % 2]
            for cc in range(2):
                ps, pb = next_bank()
                for ri in range(2):
                    group("pe", [lambda e, kc=kc, ri=ri, cc=cc: e.matmul(
                        ps[:, ri * 256:(ri + 1) * 256], lhsT=fcc[:, kc, ri, cc * 128:(cc + 1) * 128],
                        rhs=fwt[:, kc, :], start=(kc == 0), stop=(kc == 1)) for kc in range(2)],
                        [d["b_fcc"], b_fwt_], [pb])
                evac_copy(gsb[:, cc, :], ps, [pb], [d["b_gsb"]])
            for dk_ in range(8):
                ps, pb = next_bank()
                group("pe", [lambda e, kc=kc, dk_=dk_: e.matmul(
                    ps, lhsT=wft[:, kc, dk_ * 128:(dk_ + 1) * 128], rhs=gsb[:, kc, :],
                    start=(kc == 0), stop=(kc == 1)) for kc in range(2)], [b_wft_, d["b_gsb"]], [pb])
                out_fn(g, dk_, ps, pb)

        pcl, pcc = [], []
        if l == 0:
            pc_idx = [p_ for p_ in range(NPANEL) if p_ not in [P_CD + c for c in DCH]]
            pcl, pcc = precast_panels(pc_idx, stk_box[0])
        hA = [tsb("hA%d" % i, [128, 8, T], F32) for i in range(2)]
        b_hA = [Buf("hA%d" % i) for i in range(2)]
        sqA1 = tsb("sqA", [128, 8, T], BF16)
        sqA = [sqA1, sqA1]
        b_sqA1 = [Buf("sqA_%d" % c) for c in range(8)]
        b_sqA = [b_sqA1, b_sqA1]
        xnA = [tsb("xnA%d" % i, [128, 8, T], BF16) for i in range(2)]
        b_xnA = [[Buf("xnA%d_%d" % (i, c)) for c in range(8)] for i in range(2)]
        rsA = tsb("rsA", [128, T], F32)
        tmA = tsb("tmA", [128, T], F32)
        b_rsA, b_tmA = Buf("rsA"), Buf("tmA")

        def loadA(i):
            dma("sp", hA[i % 2], hsrc[:, i * T:(i + 1) * T].rearrange("(c p) t -> p c t", p=128),
                hsrc_bufs(i), [b_hA[i % 2]])

        def normA(i):
            s_ = i % 2
            rmsnorm(hA[s_], [b_hA[s_]], sqA[s_], b_sqA[s_], rsA, b_rsA, tmA, b_tmA, xnA[s_], b_xnA[s_], vb + V_G1, T)

        if l == 0 or n_layers == 1:
            sub = contextlib.ExitStack()

            def ssb(name, shape, dt):
                return sub.enter_context(nc.sbuf_tensor("L%d_%s" % (l, name), list(shape), dt)).ap()
            d0 = wf_temps(ssb, "a")

            def out_sb(g, dk_, ps, pb):
                o = wfp[:, dk_, :].rearrange("p (r g m) -> p r g m", r=2, g=4)[:, :, g, :]
                evac_copy(o, ps.rearrange("p (r m) -> p r m", r=2), [pb], [b_wfp])
            outstanding = []
            wf_load(l, 0, d0)
            for g in range(4):
                if g + 1 < 4:
                    wf_load(l, g + 1, d0)
                for _ in range(3):
                    if pcl:
                        pcl.pop(0)()
                        outstanding.append(pcc.pop(0))
                wf_group(l, g, d0, out_sb)
                if g == 0:
                    loadA(0)
                    loadA(1)
                if g == 2:
                    normA(0)
                for f_ in outstanding:
                    f_()
                outstanding = []
            sub.close()
        else:
            loadA(0)
            dma("sp", wfp, wfs, b_wfs, [b_wfp])
            loadA(1)
            normA(0)

        ztA = [tsb("ztA%d" % i, [128, 4, 2048], BF16) for i in range(2)]
        b_ztA = [Buf("ztA%d" % i) for i in range(2)]
        pc_per_tile = -(-len(pcl) // NTILE)

        for i in range(NTILE):
            s = i % 2
            if i + 2 < NTILE:
                loadA(i + 2)
            npc_t = min(pc_per_tile, len(pcl))
            for _ in range(npc_t):
                pcl.pop(0)()
            for tb in range(4):
                if tb == 2 and i + 1 < NTILE:
                    normA(i + 1)
                for pn in range(4):
                    ps, pb = next_bank()
                    group("pe", [lambda e, kc=kc, tb=tb, pn=pn: e.matmul(
                        ps, lhsT=xnA[s][:, kc, tb * 128:(tb + 1) * 128], rhs=wfp[:, kc, pn * 512:(pn + 1) * 512],
                        start=(kc == 0), stop=(kc == 7)) for kc in range(8)], b_xnA[s] + [b_wfp], [pb])
                    evac_copy(ztA[s][:, tb, pn * 512:(pn + 1) * 512], ps, [pb], [b_ztA[s]])
            for ri in range(2):
                dma("sp", zs[i // 4, ri, (i % 4) * T:(i % 4 + 1) * T, :].rearrange("(tb p) c -> p tb c", p=128),
                    ztA[s][:, :, ri * 1024:(ri + 1) * 1024], [b_ztA[s]], [b_zs[ri * NTILE + i]])
            for _ in range(npc_t):
                pcc.pop(0)()
        assert not pcl and not pcc
        end_phase()

        in1s = [tsb("in1s%d" % i, [128, 8, 1024], BF16) for i in range(2)]
        in1p = [tsb("in1p%d" % i, [128, 8, 1024], BF16) for i in range(2)]
        m1s = [tsb("m1s%d" % i, [128, 8, 128], BF16) for i in range(2)]
        m1p = [tsb("m1p%d" % i, [128, 8, 128], BF16) for i in range(2)]
        o1 = [tsb("o1_%d" % i, [128, 8, 1024], BF16) for i in range(2)]
        b_in1 = [[Buf("in1_%d_%d" % (i, j)) for j in range(6)] for i in range(2)]
        do_wf1 = (l == 0 and n_layers > 1)
        if do_wf1:
            d1 = wf_temps(tsb, "b")
            wstg = tsb("wstg", [128, 8, 512], BF16)
            b_wstg = Buf("wstg")

            def out_dram(g, dk_, ps, pb):
                evac_copy(wstg[:, dk_, :], ps, [pb], [b_wstg])
                if dk_ == 7:
                    for r_ in range(2):
                        c0 = r_ * 1024 + g * 256
                        dma("sp", wfs[:, :, c0:c0 + 256], wstg[:, :, r_ * 256:(r_ + 1) * 256],
                            [b_wstg], [b_wfs[2 * g + r_]])
        b_o1 = [Buf("o1_%d" % i) for i in range(2)]
        for gi in range(16):
            s = gi % 2
            b0 = gi * 8
            q, bp0 = b0 // 32, b0 % 32
            src = zs.rearrange("q r t c -> (q r t) c").rearrange("(p b) c -> p b c", b=128)[:, b0:b0 + 8, :]
            dma("sp", in1s[s], src, b_zs, [b_in1[s][0], b_in1[s][1]])
            src = zs[q].rearrange("r t c -> (r t) c").rearrange("(p b) c -> p b c", b=32)[:, bp0:bp0 + 8, :]
            dma("sp", in1p[s], src, b_zs, [b_in1[s][2], b_in1[s][3]])
            dma("sp", m1s[s], m1s_d[gi], [b_in], [b_in1[s][4]])
            dma("sp", m1p[s], m1p_d[gi], [b_in], [b_in1[s][5]])
            for bb in range(8):
                for hf in range(2):
                    ps, pb = next_bank()
                    group("pe", [
                        lambda e, bb=bb, hf=hf, ps=ps: e.matmul(ps, lhsT=m1s[s][:, bb, :],
                                                                rhs=in1s[s][:, bb, hf * 512:(hf + 1) * 512],
                                                                start=True, stop=False),
                        lambda e, bb=bb, hf=hf, ps=ps: e.matmul(ps, lhsT=m1p[s][:, bb, :],
                                                                rhs=in1p[s][:, bb, hf * 512:(hf + 1) * 512],
                                                                start=False, stop=True)],
                        b_in1[s], [pb])
                    evac_copy(o1[s][:, bb, hf * 512:(hf + 1) * 512], ps, [pb], [b_o1[s]])
            dma("pool", ys[b0:b0 + 8].rearrange("b p c -> p b c"), o1[s], [b_o1[s]], [b_ys[gi]])
            if do_wf1 and gi % 4 == 1:
                wf_load(1, gi // 4, d1)
                wf_group(1, gi // 4, d1, out_dram)
        end_phase()

        y2g = [tsb("y2g%d" % i, [128, 2, 8, 1024], BF16) for i in range(2)]
        b_y2g = [Buf("y2g%d" % i) for i in range(2)]
        m2 = tsb("m2", [128, 2, 128], BF16)
        b_m2 = Buf("m2")
        o2 = [tsb("o2_%d" % i, [128, 8, 1024], BF16) for i in range(2)]
        b_o2 = [Buf("o2_%d" % i) for i in range(2)]
        dma("sp", m2, m2_d, [b_in], [b_m2])
        ys4 = ys.rearrange("b (r k) c -> b r k c", r=2)
        for kg in range(8):
            s = kg % 2
            dma("sp", y2g[s], ys4[:, :, kg * 8:(kg + 1) * 8, :], b_ys, [b_y2g[s]])
            for kk in range(8):
                for hf in range(2):
                    ps, pb = next_bank()
                    group("pe", [lambda e, r=r, kk=kk, hf=hf, ps=ps: e.matmul(
                        ps, lhsT=m2[:, r, :], rhs=y2g[s][:, r, kk, hf * 512:(hf + 1) * 512],
                        start=(r == 0), stop=(r == 1)) for r in range(2)], [b_m2, b_y2g[s]], [pb])
                    evac_copy(o2[s][:, kk, hf * 512:(hf + 1) * 512], ps, [pb], [b_o2[s]])
            dst = ycs.rearrange("(j k) c -> j k c", k=64)[:, kg * 8:(kg + 1) * 8, :]
            dma("pool", dst, o2[s], [b_o2[s]], [b_ycs])
        end_phase()

        NR = 4
        wr = [tsb("wr%d" % i, [128, PAN], BF16) for i in range(NR)]
        b_wr = [Buf("wr%d" % i) for i in range(NR)]
        hB = [tsb("h%d" % i, [128, 8, T], F32) for i in range(2)]
        b_hB = [[Buf("h%d_%d" % (i, c)) for c in range(8)] for i in range(2)]
        hhB = [tsb("hh%d" % i, [128, 8, 2, 16], F32) for i in range(2)]
        b_hhL = [Buf("hhL%d" % i) for i in range(2)]
        b_hhR = [Buf("hhR%d" % i) for i in range(2)]
        xn = tsb("xn", [128, 8, T], BF16)
        b_xn = [Buf("xn%d" % c) for c in range(8)]
        xn1 = tsb("xn1", [128, 8, T], BF16)
        b_xn1 = [Buf("xn1_%d" % c) for c in range(8)]
        xh = tsb("xh", [128, 8, 2, 16], BF16)
        b_xh = Buf("xh")
        sq = tsb("sq", [128, 8, T], BF16)
        b_sq = [Buf("sq%d" % c) for c in range(8)]
        sqh = tsb("sqh", [128, 8, 2, 16], BF16)
        b_sqh = Buf("sqh")
        rs = tsb("rs", [128, T], F32)
        tm = tsb("tm", [128, T], F32)
        mu = tsb("mu", [128, T], F32)
        msq = tsb("msq", [128, T], F32)
        rsh = tsb("rsh", [128, 32], F32)
        tmh = tsb("tmh", [128, 32], F32)
        b_rs, b_tm, b_mu, b_msq, b_rsh, b_tmh = (Buf(n) for n in ("rs", "tm", "mu", "msq", "rsh", "tmh"))
        upy = tsb("upy", [128, 4352], BF16)
        up = upy.bitcast(F32).rearrange("p (c t) -> p c t", c=4)
        yct = upy[:, 0:4096].rearrange("p (b c) -> p b c", b=4)
        b_up = [Buf("up%d" % c) for c in range(4)]
        stt = tsb("st", [128, 3, 544], F32)
        st = [stt[:, k, :] for k in range(3)]
        b_st = [Buf("st%d" % k) for k in range(3)]
        pt = stt[:, 0:2, 0:512]
        pl = tsb("pl", [128, 8, T], BF16)
        b_pl = [Buf("pl%d" % c) for c in range(8)]
        gt = tsb("gt", [128, 8, T], BF16)
        b_gt = [Buf("gt%d" % c) for c in range(8)]
        sg = tsb("sg", [128, 4, 544], BF16)
        b_sg = [Buf("sg%d" % c) for c in range(4)]
        vx = tsb("vx", [128, 8, 544], BF16)
        b_vx = [Buf("vx%d" % c) for c in range(8)]
        big = tsb("big", [128, 32, T], BF16)
        hid = big
        bigf = big.rearrange("p c t -> p (c t)").bitcast(F32)
        cv = bigf[:, 0:4096].rearrange("p (c t) -> p c t", c=8)
        mg = bigf[:, 4096:8192].rearrange("p (c t) -> p c t", c=8)
        b_cv = [Buf("cv%d" % c) for c in range(8)]
        b_mg = [Buf("mg%d" % c) for c in range(8)]
        tf = [tsb("tf%d" % i, [128, T], F32) for i in range(3)]
        b_tf = [Buf("tf%d" % i) for i in range(3)]
        rl = [tsb("rl%d" % i, [128, T], BF16) for i in range(2)]
        b_rl = [Buf("rl%d" % i) for i in range(2)]
        pbf = tsb("pbf", [128, 2, T], BF16)
        b_pbf = Buf("pbf")
        mb = sq
        b_mb = b_sq
        tstate = {"n": 0, "r": 0}

        def next_tf():
            k = tstate["n"] % 3
            tstate["n"] += 1
            return tf[k], b_tf[k]

        def next_rl():
            k = tstate["r"] % 2
            tstate["r"] += 1
            return rl[k], b_rl[k]

        wstate = {"issued": 0, "used": 0}
        do_pc = (l == 0 and n_layers > 1)
        seq = []
        npc = 0
        for i in range(NTILE):
            for pi in ORDER:
                seq.append(("w", pi))
                if do_pc and P_FF2 <= pi < P_FF2 + 8 and npc < 2 * NPANEL:
                    seq.append(("c", npc))
                    npc += 1
        assert (not do_pc) or npc == 2 * NPANEL
        cst_b = [tsb("cstg%d" % k, [128, PAN // 2], BF16) for k in range(2)] if do_pc else []
        b_cstg = [Buf("cstg%d" % k) for k in range(2)]

        def ring_issue(u):
            while wstate["issued"] < min(len(seq), u + NR):
                j = wstate["issued"]
                kind, a = seq[j]
                if kind == "w":
                    gi = l * NPANEL + a
                    dma("sp", wr[j % NR], wbf[gi], [b_wbf[gi]], [b_wr[j % NR]])
                else:
                    gi = NPANEL + a // 2
                    hf = a % 2
                    dma("sp", wr[j % NR].bitcast(F32), wsrc[gi][:, hf * 2048:(hf + 1) * 2048], [b_in], [b_wr[j % NR]])
                wstate["issued"] += 1

        def panel(pi_expected):
            u = wstate["used"]
            assert seq[u] == ("w", pi_expected), (seq[u], pi_expected)
            ring_issue(u)
            wstate["used"] += 1
            return wr[u % NR], b_wr[u % NR]

        def precast_step():
            u = wstate["used"]
            if u >= len(seq) or seq[u][0] != "c":
                return
            a = seq[u][1]
            ring_issue(u)
            wstate["used"] += 1
            gi = NPANEL + a // 2
            hf = a % 2
            k = a % 2
            src = wr[u % NR].bitcast(F32)
            op("pool", lambda e: e.tensor_copy(out=cst_b[k], in_=src), [b_wr[u % NR]], [b_cstg[k]])
            dma("pool", wbf[gi][:, hf * 2048:(hf + 1) * 2048], cst_b[k], [b_cstg[k]], [b_wbf[gi]])

        def halo_view(a2d):
            return a2d.rearrange("p (a b) -> p a b", b=16)[:, 0:34:33, :]

        def load_h(i):
            s_ = i % 2
            s0 = i * T
            h, hh = hB[s_], hhB[s_]
            dma("pool", h, hsrc[:, s0:s0 + T].rearrange("(c p) t -> p c t", p=128), hsrc_bufs(i), b_hB[s_])
            if i > 0:
                dma("pool", hh[:, :, 0, :], hsrc[:, s0 - 16:s0].rearrange("(c p) t -> p c t", p=128),
                    hsrc_bufs(i - 1), [b_hhL[s_]])
            else:
                op("dve", lambda e: e.memset(hh[:, :, 0, :], 0.0), [], [b_hhL[s_]])
            if i < NTILE - 1:
                dma("pool", hh[:, :, 1, :], hsrc[:, s0 + T:s0 + T + 16].rearrange("(c p) t -> p c t", p=128),
                    hsrc_bufs(i + 1), [b_hhR[s_]])
            else:
                op("dve", lambda e: e.memset(hh[:, :, 1, :], 0.0), [], [b_hhR[s_]])
            if i % 4 == 0 and i > 0:
                op("pool", lambda e: e.tensor_scalar(out=hh[:, :, 0, :], in0=hh[:, :, 0, :], scalar1=mseg_ap,
                                                     scalar2=1.0, op0=ALU.mult, op1=ALU.mult),
                   [b_hhL[s_], b_const], [b_hhL[s_]])
            if i % 4 == 3 and i < NTILE - 1:
                op("pool", lambda e: e.tensor_scalar(out=hh[:, :, 1, :], in0=hh[:, :, 1, :], scalar1=mseg_ap,
                                                     scalar2=1.0, op0=ALU.mult, op1=ALU.mult),
                   [b_hhR[s_], b_const], [b_hhR[s_]])

        def norm1(i):
            s_ = i % 2
            h, hh = hB[s_], hhB[s_]
            rmsnorm(h, b_hB[s_], sq, b_sq, rs, b_rs, tm, b_tm, xn1, b_xn1, vb + V_G1, T)
            op("act", lambda e: e.activation(out=sqh, in_=hh, func=AF.Square), [b_hhL[s_], b_hhR[s_]], [b_sqh])
            ps, pb = next_bank()
            group("pe", [lambda e, c=c, ps=ps: e.matmul(ps[:, 0:32].rearrange("p (a b) -> p a b", b=16),
                                                         lhsT=ones_b, rhs=sqh[:, c, :, :],
                                                         start=(c == 0), stop=(c == 7)) for c in range(8)],
                  [b_sqh, b_const], [pb])
            op("act", lambda e, ps=ps: e.activation(out=tmh, in_=ps[:, 0:32], func=AF.Ln, bias=eps_ap,
                                                    scale=1.0 / D), [pb, b_const], [b_tmh])
            op("act", lambda e: e.activation(out=rsh, in_=tmh, func=AF.Exp, scale=-0.5), [b_tmh], [b_rsh])
            for c in range(8):
                op("dve", lambda e, c=c: e.scalar_tensor_tensor(
                    out=xh[:, c, :, :], in0=hh[:, c, :, :], scalar=vecs[:, vb + V_G1 + c:vb + V_G1 + c + 1],
                    in1=rsh.rearrange("p (a b) -> p a b", b=16), op0=ALU.mult, op1=ALU.mult),
                    [b_hhL[s_], b_hhR[s_], b_rsh, b_const], [b_xh])

        def proj(pi, rhs_fn, rhs_bufs, nkc, consume, halo_fn=None, ncols=512, fine=False):
            w, wb = panel(pi)
            w3 = w[:, 0:nkc * ncols].rearrange("p (k n) -> p k n", k=nkc)
            for ch in range(ncols // 128):
                ps, pb = next_bank()
                if fine and ch == 0:
                    for kc in range(nkc):
                        group("pe", [lambda e, kc=kc, ch=ch, ps=ps: e.matmul(
                            ps, lhsT=w3[:, kc, ch * 128:(ch + 1) * 128], rhs=rhs_fn(kc),
                            start=(kc == 0), stop=(kc == nkc - 1))], [wb, rhs_bufs[kc]], [pb])
                else:
                    group("pe", [lambda e, kc=kc, ch=ch, ps=ps: e.matmul(
                        ps, lhsT=w3[:, kc, ch * 128:(ch + 1) * 128], rhs=rhs_fn(kc),
                        start=(kc == 0), stop=(kc == nkc - 1)) for kc in range(nkc)], [wb] + rhs_bufs, [pb])
                psh = pbh = None
                if halo_fn is not None:
                    psh, pbh = next_bank()
                    pshv = psh[:, 0:32].rearrange("p (a b) -> p a b", b=16)
                    group("pe", [lambda e, kc=kc, ch=ch, pshv=pshv: e.matmul(
                        pshv, lhsT=w3[:, kc, ch * 128:(ch + 1) * 128], rhs=halo_fn(kc),
                        start=(kc == 0), stop=(kc == nkc - 1)) for kc in range(nkc)], [wb, b_xh], [pbh])
                consume(ch, ps, pb, psh, pbh)

        xn1_main = lambda kc: xn1[:, kc, :]
        xn_main = lambda kc: xn[:, kc, :]
        xn_halo = lambda kc: xh[:, kc, :, :]

        def gate_proj(pbase, branch):
            for pp in range(2):
                def cons_g(ch, ps, pb, psh, pbh, pp=pp):
                    gch = pp * 4 + ch
                    col = vb + V_BG + 8 * branch + gch
                    op("act", lambda e: e.activation(out=gt[:, gch, :], in_=ps, func=AF.Sigmoid,
                                                     bias=vecs[:, col:col + 1], scale=1.0),
                       [pb, b_const], [b_gt[gch]])
                proj(pbase + pp, xn1_main, b_xn1, 8, cons_g)

        def mixer(i):
            s_ = i % 2
            s0 = i * T
            h, b_h = hB[s_], b_hB[s_]
            left_edge = (i % 4 == 0)
            right_edge = (i % 4 == 3)
            kindL = 0 if i == 0 else 1
            kindR = 0 if i == NTILE - 1 else 1

            for pp in range(2):
                def cons_pool(ch, ps, pb, psh, pbh, pp=pp):
                    op("act", lambda e: e.activation(out=up[:, ch, 16:528], in_=ps, func=AF.Copy), [pb], [b_up[ch]])
                    op("act", lambda e: e.activation(out=halo_view(up[:, ch, :]),
                                                     in_=psh[:, 0:32].rearrange("p (a b) -> p a b", b=16),
                                                     func=AF.Copy), [pbh], [b_up[ch]])
                    g = pp * 2 + ch // 2
                    w = WINS[g]
                    gch = pp * 4 + ch
                    u = up[:, ch, :]
                    cur, cb = u, b_up[ch]
                    lvl = 0
                    ranges = {2: (9, 535), 4: (10, 534), 8: (12, 532), 16: (16, 528)}
                    offs = {2: (-1, 0), 4: (-1, 1), 8: (-2, 2), 16: (-4, 4)}
                    ww = 2
                    while ww <= w:
                        a, b = ranges[ww]
                        o0, o1_ = offs[ww]
                        dst, db = st[lvl % 3], b_st[lvl % 3]
                        op("dve", lambda e, dst=dst, cur=cur, a=a, b=b, o0=o0, o1_=o1_: e.tensor_tensor(
                            out=dst[:, a:b], in0=cur[:, a + o0:b + o0], in1=cur[:, a + o1_:b + o1_], op=ALU.add),
                            [cb], [db])
                        cur, cb = dst, db
                        lvl += 1
                        ww *= 2
                    op("dve", lambda e, cur=cur: e.scalar_tensor_tensor(
                        out=pl[:, gch, :], in0=cur[:, 16:528], scalar=1.0 / w, in1=u[:, 16:528],
                        op0=ALU.mult, op1=ALU.subtract), [cb, b_up[ch]], [b_pl[gch]])
                    for side, flag, kind in ((0, left_edge, kindL), (1, right_edge, kindR)):
                        if not flag:
                            continue
                        c0 = C_TAB + ((kind * 2 + side) * 4 + g) * 8
                        e0 = 16 if side == 0 else 520
                        m0 = 0 if side == 0 else 504
                        tfa, tfb = next_tf()
                        op("dve", lambda e, cur=cur, c0=c0, e0=e0, tfa=tfa: e.tensor_tensor(
                            out=tfa[:, 0:8], in0=cur[:, e0:e0 + 8], in1=cst[:, c0:c0 + 8], op=ALU.mult),
                            [cb, b_const], [tfb])
                        op("dve", lambda e, e0=e0, m0=m0, tfa=tfa: e.tensor_tensor(
                            out=pl[:, gch, m0:m0 + 8], in0=tfa[:, 0:8], in1=u[:, e0:e0 + 8], op=ALU.subtract),
                            [tfb, b_up[ch]], [b_pl[gch]])
                proj(P_POOL + pp, xn1_main, b_xn1, 8, cons_pool, halo_fn=xn_halo)
            dbg_dump('xn1', xn1, b_xn1)
            dbg_dump('xh', xh, [b_xh])
            dbg_dump('pl0', pl, b_pl)
            gate_proj(P_GA, 0)
            w, wb = panel(P_POOLW)
            w4 = w[:, 0:2048].rearrange("p (g k n) -> p g k n", g=4, k=2)
            for dch in range(8):
                g, half = dch // 2, dch % 2
                ps, pb = next_bank()
                group("pe", [lambda e, kc=kc, g=g, half=half, ps=ps: e.matmul(
                    ps, lhsT=w4[:, g, kc, half * 128:(half + 1) * 128], rhs=pl[:, 2 * g + kc, :],
                    start=(kc == 0), stop=(kc == 1)) for kc in range(2)],
                    [wb, b_pl[2 * g], b_pl[2 * g + 1]], [pb])
                op("dve", lambda e, dch=dch, ps=ps: e.scalar_tensor_tensor(
                    out=mg[:, dch, :], in0=ps, scalar=vecs[:, vb + V_PS + dch:vb + V_PS + dch + 1],
                    in1=gt[:, dch, :], op0=ALU.mult, op1=ALU.mult), [pb, b_const, b_gt[dch]], [b_mg[dch]])

            dma("pool", yct, ycs[s0:s0 + T, :].rearrange("(b p) c -> p b c", p=128), [b_ycs], b_up)

            bg = []
            for pp in range(2):
                def cons_cg(ch, ps, pb, psh, pbh):
                    op("act", lambda e: e.activation(out=sg[:, ch, 16:528], in_=ps, func=AF.Sigmoid),
                       [pb], [b_sg[ch]])
                    op("act", lambda e: e.activation(out=halo_view(sg[:, ch, :]),
                                                     in_=psh[:, 0:32].rearrange("p (a b) -> p a b", b=16),
                                                     func=AF.Sigmoid), [pbh], [b_sg[ch]])
                proj(P_CG0 if pp == 0 else P_CG1, xn1_main, b_xn1, 8, cons_cg, halo_fn=xn_halo)

                def cons_ca(ch, ps, pb, psh, pbh, pp=pp):
                    gch = pp * 4 + ch
                    op("dve", lambda e: e.tensor_tensor(out=vx[:, gch, 16:528], in0=ps, in1=sg[:, ch, 16:528],
                                                        op=ALU.mult), [pb, b_sg[ch]], [b_vx[gch]])
                    op("dve", lambda e: e.tensor_tensor(out=halo_view(vx[:, gch, :]),
                                                        in0=psh[:, 0:32].rearrange("p (a b) -> p a b", b=16),
                                                        in1=halo_view(sg[:, ch, :]), op=ALU.mult),
                       [pbh, b_sg[ch]], [b_vx[gch]])
                    if pp == 1:
                        for _ in range(4):
                            if bg:
                                bg.pop(0)()
                proj(P_CA0 if pp == 0 else P_CA1, xn1_main, b_xn1, 8, cons_ca, halo_fn=xn_halo)
                if pp == 0:
                    for k in range(31):
                        for ch in DCH:
                            wcol = vecs[:, V_CW + l * 248 + ch * 31 + k:V_CW + l * 248 + ch * 31 + k + 1]
                            if k == 0:
                                bcol = vecs[:, vb + V_CB + ch:vb + V_CB + ch + 1]
                                bg.append(lambda ch=ch, wcol=wcol, bcol=bcol: op(
                                    "dve", lambda e: e.tensor_scalar(out=cv[:, ch, :], in0=vx[:, ch, 1:513],
                                                                     scalar1=wcol, scalar2=bcol,
                                                                     op0=ALU.mult, op1=ALU.add),
                                    [b_vx[ch], b_const], [b_cv[ch]], skip_self=True))
                            else:
                                bg.append(lambda ch=ch, wcol=wcol, k=k: op(
                                    "dve", lambda e: e.scalar_tensor_tensor(
                                        out=cv[:, ch, :], in0=vx[:, ch, 1 + k:513 + k], scalar=wcol,
                                        in1=cv[:, ch, :], op0=ALU.mult, op1=ALU.add),
                                    [b_vx[ch], b_cv[ch], b_const], [b_cv[ch]], skip_self=True))
            while bg:
                bg.pop(0)()
            pe_chunks = [c for c in range(8) if c not in DCH]
            for ch in pe_chunks:
                w, wb = panel(P_CD + ch)
                w3 = w[:, 0:31 * 128].rearrange("p (k n) -> p k n", k=31)
                ps, pb = next_bank()
                group("pe", [lambda e, k=k, ch=ch, ps=ps, w3=w3: e.matmul(
                    ps, lhsT=w3[:, k, :], rhs=vx[:, ch, 1 + k:513 + k], start=(k == 0), stop=(k == 30))
                    for k in range(31)], [wb, b_vx[ch]], [pb])
                bcol = vecs[:, vb + V_CB + ch:vb + V_CB + ch + 1]
                op("act", lambda e, ch=ch, ps=ps, bcol=bcol: e.activation(out=cv[:, ch, :], in_=ps, func=AF.Identity,
                                                                          bias=bcol, scale=1.0),
                   [pb, b_const], [b_cv[ch]])
                op("act", lambda e, ch=ch, ps=ps, bcol=bcol: e.activation(out=sq[:, ch, :], in_=ps, func=AF.Square,
                                                                          bias=bcol, scale=1.0),
                   [pb, b_const], [b_sq[ch]])
                op("act", lambda e, ch=ch, ps=ps, bcol=bcol: e.activation(out=vx[:, ch, 16:528], in_=ps,
                                                                          func=AF.Identity, bias=bcol, scale=1.0),
                   [pb, b_const], [b_vx[ch]])
                if ch == pe_chunks[1]:
                    for dc in DCH:
                        op("act", lambda e, dc=dc: e.activation(out=sq[:, dc, :], in_=cv[:, dc, :], func=AF.Square),
                           [b_cv[dc]], [b_sq[dc]])
                        op("act", lambda e, dc=dc: e.activation(out=vx[:, dc, 16:528], in_=cv[:, dc, :],
                                                                func=AF.Identity), [b_cv[dc]], [b_vx[dc]])
            dbg_dump('vx', vx, b_vx)
            preload_ln_table()
            ps1, pb1 = next_bank()
            ps2, pb2 = next_bank()
            corder = list(DCH) + pe_chunks
            for n_, c in enumerate(corder):
                group("pe", [lambda e, c=c, n_=n_: e.matmul(ps2, lhsT=ones_b, rhs=sq[:, c, :], start=(n_ == 0),
                                                            stop=(n_ == 7))], [b_sq[c], b_const], [pb2])
                group("pe", [lambda e, c=c, n_=n_: e.matmul(ps1, lhsT=ones_b, rhs=vx[:, c, 16:528], start=(n_ == 0),
                                                            stop=(n_ == 7))], [b_vx[c], b_const], [pb1])
            op("act", lambda e: e.activation(out=mu, in_=ps1, func=AF.Identity, scale=1.0 / D), [pb1], [b_mu])
            op("dve", lambda e: e.tensor_tensor(out=msq, in0=mu, in1=mu, op=ALU.mult), [b_mu], [b_msq])
            op("dve", lambda e: e.scalar_tensor_tensor(out=tm, in0=ps2, scalar=1.0 / D, in1=msq,
                                                       op0=ALU.mult, op1=ALU.subtract), [pb2, b_msq], [b_tm])
            op("act", lambda e: e.activation(out=tm, in_=tm, func=AF.Ln, bias=eps_ap, scale=1.0),
               [b_tm, b_const], [b_tm])
            op("act", lambda e: e.activation(out=rs, in_=tm, func=AF.Exp, scale=-0.5), [b_tm], [b_rs])
            for ch in range(4, 8):
                op("pool", lambda e, ch=ch: e.tensor_tensor(out=cv[:, ch, :], in0=cv[:, ch, :], in1=mu,
                                                            op=ALU.subtract), [b_cv[ch], b_mu], [b_cv[ch]])
            for ch in range(8):
                if ch < 4:
                    op("dve", lambda e, ch=ch: e.tensor_tensor(out=cv[:, ch, :], in0=cv[:, ch, :], in1=mu,
                                                               op=ALU.subtract), [b_cv[ch], b_mu], [b_cv[ch]])
                op("dve", lambda e, ch=ch: e.tensor_tensor(out=cv[:, ch, :], in0=cv[:, ch, :], in1=rs, op=ALU.mult),
                   [b_cv[ch], b_rs], [b_cv[ch]], skip_self=True)
            gate_proj(P_GC, 2)
            for ch in range(8):
                op("act", lambda e, ch=ch: e.activation(
                    out=pl[:, ch, :], in_=cv[:, ch, :], func=AF.Silu,
                    bias=vecs[:, vb + V_LB + ch:vb + V_LB + ch + 1],
                    scale=vecs[:, vb + V_LG + ch:vb + V_LG + ch + 1]), [b_cv[ch], b_const], [b_pl[ch]])
            for dch in range(8):
                ps, pb = next_bank()
                psv = ps.bitcast(BF16)
                group("pe", [lambda e, b=b, dch=dch, psv=psv: e.transpose(
                    psv[:, b * 128:(b + 1) * 128], yct[:, b, dch * 128:(dch + 1) * 128], ident)
                    for b in range(4)], b_up + [b_const], [pb])
                tfa, tfb = next_tf()
                op("dve", lambda e, psv=psv, dch=dch, tfa=tfa: e.tensor_tensor(
                    out=tfa, in0=psv[:, 0:512], in1=gt[:, dch, :], op=ALU.mult), [pb, b_gt[dch]], [tfb])
                op("pool", lambda e, dch=dch, tfa=tfa: e.tensor_tensor(
                    out=mg[:, dch, :], in0=mg[:, dch, :], in1=tfa, op=ALU.add), [tfb, b_mg[dch]], [b_mg[dch]])
            gate_proj(P_GB, 1)
            for pp in range(2):
                def cons_co(ch, ps, pb, psh, pbh, pp=pp):
                    gch = pp * 4 + ch
                    tfa, tfb = next_tf()
                    op("dve", lambda e: e.tensor_tensor(out=tfa, in0=ps, in1=gt[:, gch, :], op=ALU.mult),
                       [pb, b_gt[gch]], [tfb])
                    op("dve", lambda e: e.tensor_tensor(out=mb[:, gch, :], in0=mg[:, gch, :], in1=tfa, op=ALU.add),
                       [tfb, b_mg[gch]], [b_mb[gch]])
                proj(P_CO + pp, lambda kc: pl[:, kc, :], list(b_pl), 8, cons_co)

            dbg_dump('ln', pl, b_pl)
            dbg_dump('mb', mb, b_mb)
            dbg_dump('rs', rs, [b_rs])
            dbg_dump('mu', mu, [b_mu])
            preload_ln_table()
            for pp in range(2):
                def cons_wo(ch, ps, pb, psh, pbh, pp=pp):
                    gch = pp * 4 + ch
                    op("dve", lambda e: e.tensor_tensor(out=h[:, gch, :], in0=ps, in1=h[:, gch, :], op=ALU.add),
                       [pb, b_h[gch]], [b_h[gch]])
                proj(P_WO + pp, lambda kc: mb[:, kc, :], list(b_mb), 8, cons_wo)

        def ffn(i):
            s_ = i % 2
            h, b_h = hB[s_], b_hB[s_]
            dbg_dump('h1', h, b_h)
            rmsnorm(h, b_h, sq, b_sq, rs, b_rs, tm, b_tm, xn, b_xn, vb + V_G2, T)
            for pp in range(8):
                def cons_ff1(ch, ps, pb, psh, pbh, pp=pp):
                    j = pp * 4 + ch
                    rla, rlb = next_rl()
                    op("act", lambda e: e.activation(out=rla, in_=ps, func=AF.Relu), [pb], [rlb])
                    op("dve", lambda e: e.tensor_tensor(out=hid[:, j, :], in0=ps, in1=rla, op=ALU.mult),
                       [pb, rlb], [b_cv[j // 2] if j < 16 else b_mg[(j - 16) // 2]])
                proj(P_FF1 + pp, xn_main, b_xn, 8, cons_ff1, fine=(pp == 0))
            if i + 1 < NTILE:
                norm1(i + 1)
            for og in range(2):
                banks = [next_bank() for _ in range(4)]
                for kg in range(4):
                    w, wb = panel(P_FF2 + og * 4 + kg)
                    w3 = w.rearrange("p (k n) -> p k n", k=8)
                    for ch in range(4):
                        ps, pb = banks[ch]
                        group("pe", [lambda e, kc=kc, ch=ch, ps=ps, w3=w3, kg=kg: e.matmul(
                            ps, lhsT=w3[:, kc, ch * 128:(ch + 1) * 128], rhs=hid[:, kg * 8 + kc, :],
                            start=(kg == 0 and kc == 0), stop=(kg == 3 and kc == 7)) for kc in range(8)],
                            [wb] + (b_cv[kg * 4:(kg + 1) * 4] if kg < 2 else b_mg[(kg - 2) * 4:(kg - 1) * 4]), [pb])
                        if kg == 3:
                            gch = og * 4 + ch
                            op("dve", lambda e, ps=ps, gch=gch: e.tensor_tensor(
                                out=h[:, gch, :], in0=ps, in1=h[:, gch, :], op=ALU.add), [pb, b_h[gch]], [b_h[gch]])
                    precast_step()

        def ple(i):
            s_ = i % 2
            s0 = i * T
            h, b_h = hB[s_], b_hB[s_]
            dbg_dump('h2', h, b_h)
            def cons_pp(ch, ps, pb, psh, pbh):
                op("act", lambda e: e.activation(out=cv[:, ch, :], in_=ps, func=AF.Copy), [pb], [b_cv[ch]])
            proj(P_PP, lambda kc: pbf[:, kc, :], [b_pbf], 2, cons_pp, ncols=1024)
            for pp in range(2):
                def cons_pg(ch, ps, pb, psh, pbh, pp=pp):
                    gch = pp * 4 + ch
                    op("act", lambda e: e.activation(out=gt[:, gch, :], in_=ps, func=AF.Sigmoid),
                       [pb], [b_gt[gch]])
                    tfa, tfb = next_tf()
                    op("dve", lambda e: e.tensor_tensor(out=tfa, in0=cv[:, gch, :], in1=gt[:, gch, :], op=ALU.mult),
                       [b_cv[gch], b_gt[gch]], [tfb])
                    op("dve", lambda e: e.tensor_tensor(out=h[:, gch, :], in0=h[:, gch, :], in1=tfa, op=ALU.add),
                       [tfb, b_h[gch]], [b_h[gch]])
                proj(P_PG + pp, xn_main, b_xn, 8, cons_pg, fine=(pp == 0))

            if l < n_layers - 1:
                dma("pool", hscr[:, s0:s0 + T].rearrange("(c p) t -> p c t", p=128), h, b_h, [b_hscr[i]])
            else:
                for c in range(8):
                    op("act", lambda e, c=c: e.activation(out=sq[:, c, :], in_=h[:, c, :], func=AF.Square),
                       [b_h[c]], [b_sq[c]])
                ps, pb = next_bank()
                group("pe", [lambda e, c=c, ps=ps: e.matmul(ps, lhsT=ones_b, rhs=sq[:, c, :], start=(c == 0),
                                                             stop=(c == 7)) for c in range(8)], b_sq + [b_const], [pb])
                op("act", lambda e, ps=ps: e.activation(out=tm, in_=ps, func=AF.Ln, bias=eps_ap, scale=1.0 / D),
                   [pb, b_const], [b_tm])
                op("act", lambda e: e.activation(out=rs, in_=tm, func=AF.Exp, scale=-0.5), [b_tm], [b_rs])
                for c in range(8):
                    op("dve", lambda e, c=c: e.scalar_tensor_tensor(
                        out=cv[:, c, :], in0=h[:, c, :], scalar=vecs[:, V_GF + c:V_GF + c + 1], in1=rs,
                        op0=ALU.mult, op1=ALU.mult), [b_h[c], b_rs, b_const], [b_cv[c]])
                dma("pool", yT[:, s0:s0 + T].rearrange("(c p) t -> p c t", p=128), cv, b_cv, [b_yT])

        load_h(0)
        norm1(0)
        for i in range(NTILE):
            s_ = i % 2
            if i + 1 < NTILE:
                load_h(i + 1)
            mixer(i)
            dma("pool", pt, pT[l * 256:(l + 1) * 256, i * T:(i + 1) * T].rearrange("(c p) t -> p c t", p=128),
                [b_in], [b_st[0], b_st[1]])
            op("dve", lambda e: e.tensor_copy(out=pbf, in_=pt), [b_st[0], b_st[1]], [b_pbf])
            ffn(i)
            rmsnorm(hB[s_], b_hB[s_], sq, b_sq, rs, b_rs, tm, b_tm, xn, b_xn, vb + V_G3, T)
            ple(i)
        end_phase()


    for l in range(n_layers):
        layer(l)
    fw_.barrier()
    return nc, fw_


_CACHE = {}


def kernel(**inputs):
    inp = {k: np.asarray(v) for k, v in inputs.items()}
    wsrc, wfT, fw, vecs = _prep_weights(inp)
    fcc = _fcc_table()
    ident = np.eye(128, dtype=np.float32).astype(ml_dtypes.bfloat16)
    consts = {}
    for kind in ("prompt", "sample"):
        m1s, m1p, m2 = _dft_consts(kind)
        consts[kind] = dict(m1s=m1s, m1p=m1p, m2=m2, cst=_cst_table(kind))
    xp, xs = inp["x_prompt"], inp["x_sample"]
    pp_, ps_ = inp["p_prompt"], inp["p_sample"]
    in_maps = []
    for c in range(8):
        if c < 4:
            kind = "prompt"
            x = xp[4 * c:4 * c + 4].reshape(NT, D)
            p = pp_[:, 4 * c:4 * c + 4].reshape(L, NT, 256)
        else:
            kind = "sample"
            x = xs[c - 4]
            p = ps_[:, c - 4]
        m = dict(
            xT=np.ascontiguousarray(x.T),
            pT=np.ascontiguousarray(p.transpose(0, 2, 1)).reshape(L * 256, NT),
            wsrc=wsrc, wfT=wfT, fw=fw, vecs=vecs, fcc=fcc, ident=ident,
        )
        m.update(consts[kind])
        in_maps.append(m)
    if "nc" not in _CACHE:
        _CACHE["nc"] = build_program()[0]
    res = run_bass_kernel_spmd(_CACHE["nc"], in_maps, core_ids=list(range(8)))
    outs = [np.asarray(r["yT"]).T for r in res.results]
    y_prompt = np.stack([o.reshape(4, 2048, D) for o in outs[:4]], 0).reshape(16, 2048, D).astype(np.float32)
    y_sample = np.stack(outs[4:], 0).astype(np.float32)
    return (np.ascontiguousarray(y_prompt), np.ascontiguousarray(y_sample))
```

```python
import contextlib
import numpy as np
import ml_dtypes
import concourse.bass as bass
import concourse.mybir as mybir
from concourse.bass_utils import run_bass_kernel_spmd

F32 = mybir.dt.float32
BF16 = mybir.dt.bfloat16
AF = mybir.ActivationFunctionType
ALU = mybir.AluOpType

D = 1024
NT = 8192
T = 512
NTILE = NT // T
L = 2
EPS = 1e-6
NPANEL = 44
PAN = 4096
WINS = (2, 4, 8, 16)

DCH = (0, 1)

P_POOL = 0
P_GA = 2
P_POOLW = 4
P_CG0, P_CA0, P_CG1, P_CA1 = 5, 6, 7, 8
P_CD = 9
P_GB = 17
P_CO = 19
P_GC = 21
P_WO = 23
P_FF1 = 25
P_FF2 = 33
P_PG = 41
P_PP = 43

ORDER = [0, 1, 2, 3, 4, 5, 6, 7, 8] + [9 + c for c in range(8) if c not in DCH] + [21, 22, 17, 18, 19, 20, 23, 24] + list(range(25, 41)) + [43, 41, 42]

V_G1, V_BG, V_PS, V_CB, V_LG, V_LB, V_G2, V_G3 = 0, 8, 32, 40, 48, 56, 64, 72
V_GF = 160
V_CW = 168
NVEC = 168 + 2 * 248
C_MSEG, C_EPS, C_TAB = 0, 1, 8
NCST = 136


class Buf:
    __slots__ = ("name", "w", "r")

    def __init__(self, name):
        self.name = name
        self.w = None
        self.r = {}


class EngState:
    def __init__(self, name, e, sem):
        self.name = name
        self.e = e
        self.sem = sem
        self.count = 0
        self.known = {}


class FW:
    def __init__(self, nc, n_dma=40):
        self.nc = nc
        self.eng = {}
        for name, e in (("pe", nc.tensor), ("act", nc.scalar), ("dve", nc.vector),
                        ("pool", nc.gpsimd), ("sp", nc.sync)):
            self.eng[name] = EngState(name, e, nc.alloc_semaphore("s_" + name))
        self.dsem = [nc.alloc_semaphore("s_dma%d" % i) for i in range(n_dma)]
        self.dgen = [0] * n_dma
        self.dnext = 0
        self.snap = {}
        self.nwait = 0

    def _wait(self, es, key, val, skip_self=False):
        if es.known.get(key, 0) >= val:
            return
        if key == es.name:
            if es.name == "pe" or skip_self or es.count - val >= 3:
                es.known[key] = val
                return
        if isinstance(key, str):
            es.e.wait_ge(self.eng[key].sem, val)
        else:
            es.e.wait_ge(self.dsem[key[1]], 16 * val)
        self.nwait += 1
        es.known[key] = val
        sn = self.snap.get((key, val))
        if sn:
            kn = es.known
            for k, v in sn.items():
                if kn.get(k, 0) < v:
                    kn[k] = v

    def _deps(self, es, reads, writes, extra=None, skip_self=False):
        evs = {}
        for b in reads:
            if b.w is not None:
                k, v = b.w
                if evs.get(k, 0) < v:
                    evs[k] = v
        for b in writes:
            if b.w is not None:
                k, v = b.w
                if evs.get(k, 0) < v:
                    evs[k] = v
            for k, v in b.r.items():
                if evs.get(k, 0) < v:
                    evs[k] = v
        if extra is not None:
            k, v = extra
            if evs.get(k, 0) < v:
                evs[k] = v
        for k, v in evs.items():
            self._wait(es, k, v, skip_self)

    def _record(self, ev, es, reads, writes):
        kn = es.known
        self.snap[ev] = {k: kn[k] for k in self.eng if k in kn}
        k, v = ev
        for b in reads:
            if b.r.get(k, 0) < v:
                b.r[k] = v
        for b in writes:
            b.w = ev
            b.r = {}

    def op(self, eng, fn, reads=(), writes=(), skip_self=False):
        es = self.eng[eng]
        self._deps(es, reads, writes, None, skip_self)
        ins = fn(es.e)
        es.count += 1
        ins.then_inc(es.sem, 1)
        ev = (es.name, es.count)
        self._record(ev, es, reads, writes)
        return ev

    def group(self, eng, fns, reads=(), writes=()):
        es = self.eng[eng]
        self._deps(es, reads, writes)
        ins = None
        for fn in fns:
            ins = fn(es.e)
        es.count += 1
        ins.then_inc(es.sem, 1)
        ev = (es.name, es.count)
        self._record(ev, es, reads, writes)
        return ev

    def dma(self, q, out, in_, reads=(), writes=()):
        es = self.eng[q]
        slot = self.dnext
        self.dnext = (slot + 1) % len(self.dsem)
        gen = self.dgen[slot]
        extra = (("d", slot), gen) if gen > 0 else None
        self._deps(es, reads, writes, extra)
        ins = es.e.dma_start(out=out, in_=in_)
        ins.then_inc(self.dsem[slot], 16)
        self.dgen[slot] = gen + 1
        ev = (("d", slot), gen + 1)
        self._record(ev, es, reads, writes)
        return ev

    def barrier(self):
        for es in self.eng.values():
            for o in self.eng.values():
                if o is not es and o.count > 0:
                    self._wait(es, o.name, o.count)
            for slot, gen in enumerate(self.dgen):
                if gen > 0:
                    self._wait(es, ("d", slot), gen)


def _dft_consts(kind):
    al = np.arange(64)[:, None].astype(np.float64)
    ka = np.arange(64)[None, :].astype(np.float64)
    M1 = np.zeros((128, 128, 128), np.float64)
    for bt in range(128):
        if kind == "sample":
            ang = -2 * np.pi * (ka * al / 64 + ka * bt / 8192)
        else:
            ang = -2 * np.pi * (ka * al / 64 + ka * (bt % 32) / 2048)
        Er, Ei = np.cos(ang), np.sin(ang)
        M1[0:64, bt, 0:64] = Er
        M1[0:64, bt, 64:128] = Ei
        M1[64:128, bt, 0:64] = -Ei
        M1[64:128, bt, 64:128] = Er
    M2 = np.zeros((128, 2, 128), np.float64)
    if kind == "sample":
        b = np.arange(128)[:, None].astype(np.float64)
        j = np.arange(128)[None, :].astype(np.float64)
        M2[:, 0, :] = np.cos(2 * np.pi * j * b / 128) / np.sqrt(8192)
        M2[:, 1, :] = np.sin(2 * np.pi * j * b / 128) / np.sqrt(8192)
    else:
        bb = np.arange(32)[:, None].astype(np.float64)
        jj = np.arange(32)[None, :].astype(np.float64)
        for q in range(4):
            M2[32 * q:32 * q + 32, 0, 32 * q:32 * q + 32] = np.cos(2 * np.pi * jj * bb / 32) / np.sqrt(2048)
            M2[32 * q:32 * q + 32, 1, 32 * q:32 * q + 32] = np.sin(2 * np.pi * jj * bb / 32) / np.sqrt(2048)
    if kind == "sample":
        p = np.arange(128)
        old = ((p % 32) // 16) * 64 + 16 * (p // 32) + p % 16
        M1 = M1[old]
    M1 = np.ascontiguousarray(M1.reshape(128, 16, 8, 128).transpose(1, 0, 2, 3))
    z = np.zeros_like(M1)
    bf = ml_dtypes.bfloat16
    if kind == "sample":
        return M1.astype(bf), z.astype(bf), M2.astype(bf)
    return z.astype(bf), M1.astype(bf), M2.astype(bf)


def _cst_table(kind):
    c = np.zeros((128, NCST), np.float32)
    c[:, C_MSEG] = 1.0 if kind == "sample" else 0.0
    c[:, C_EPS] = EPS
    for k in range(2):
        for side in range(2):
            for g, w in enumerate(WINS):
                for j in range(8):
                    if k == 1 and kind == "sample":
                        cnt = w
                    elif side == 0:
                        cnt = min(w, j + w // 2)
                    else:
                        r = 7 - j
                        cnt = min(w, r + w // 2 + 1)
                    c[:, C_TAB + ((k * 2 + side) * 4 + g) * 8 + j] = 1.0 / cnt
    return c


def _fcc_table():
    cp = np.arange(256)[:, None].astype(np.float64)
    c = np.arange(256)[None, :].astype(np.float64)
    ang = 2 * np.pi * cp * c / 256
    fc = np.stack([np.cos(ang) / 16, -np.sin(ang) / 16], 1)
    return np.ascontiguousarray(fc.reshape(2, 128, 2, 256).transpose(1, 0, 2, 3)).astype(np.float32)


def _kpanel(W, col0, ncols=512, kc0=0, nkc=8):
    blk = W[kc0 * 128:(kc0 + nkc) * 128, col0:col0 + ncols].reshape(nkc, 128, ncols).transpose(1, 0, 2)
    out = np.zeros((128, PAN), np.float32)
    out[:, :nkc * ncols] = blk.reshape(128, nkc * ncols)
    return out


def _prep_weights(inp):
    wsrc = np.zeros((L * NPANEL, 128, PAN), np.float32)
    for l in range(L):
        w_in = inp["w_in"][l]
        P = wsrc[l * NPANEL:(l + 1) * NPANEL]
        for i in range(2):
            P[P_POOL + i] = _kpanel(w_in, 512 * i)
            P[P_GA + i] = _kpanel(w_in, 4096 + 512 * i)
            P[P_GB + i] = _kpanel(w_in, 5120 + 512 * i)
            P[P_GC + i] = _kpanel(w_in, 6144 + 512 * i)
            P[P_CO + i] = _kpanel(inp["conv_out_w"][l], 512 * i)
            P[P_WO + i] = _kpanel(inp["w_o"][l], 512 * i)
            P[P_PG + i] = _kpanel(inp["w_ple_gate"][l], 512 * i)
        P[P_CA0] = _kpanel(w_in, 1024)
        P[P_CA1] = _kpanel(w_in, 1536)
        P[P_CG0] = _kpanel(w_in, 2048)
        P[P_CG1] = _kpanel(w_in, 2560)
        pw = inp["pool_w"][l]
        blk = pw.reshape(4, 2, 128, 256).transpose(2, 0, 1, 3)
        P[P_POOLW][:, :2048] = blk.reshape(128, 2048)
        cw = inp["conv_w"][l]
        for ch in range(8):
            dg = np.zeros((128, 31, 128), np.float32)
            idx = np.arange(128)
            dg[idx, :, idx] = cw[:, ch * 128:(ch + 1) * 128].T
            P[P_CD + ch][:, :31 * 128] = dg.reshape(128, 31 * 128)
        for i in range(8):
            P[P_FF1 + i] = _kpanel(inp["w_ff1"][l], 512 * i)
        for og in range(2):
            for kg in range(4):
                P[P_FF2 + og * 4 + kg] = _kpanel(inp["w_ff2"][l], 512 * og, 512, kg * 8, 8)
        P[P_PP] = _kpanel(inp["w_ple_proj"][l], 0, 1024, 0, 2)
    wfT = np.zeros((L, 4, 128, 2, 1024), np.float32)
    fw = np.zeros((L, 4, 128, 2, 256), np.float32)
    for l in range(L):
        for g in range(4):
            wf = inp["w_in"][l][:, 3072 + 256 * g:3072 + 256 * (g + 1)]
            wfT[l, g] = wf.T.reshape(2, 128, 1024).transpose(1, 0, 2)
            fw[l, g] = inp["fnet_w"][l][g].reshape(2, 128, 256).transpose(1, 0, 2)
    vecs = np.zeros((128, NVEC), np.float32)

    def col(v):
        return np.asarray(v, np.float32).reshape(-1, 128).T

    for l in range(L):
        b = l * 80
        vecs[:, b + V_G1:b + V_G1 + 8] = col(inp["norm_mix_g"][l])
        vecs[:, b + V_BG:b + V_BG + 24] = col(inp["b_gate"][l])
        vecs[:, b + V_PS:b + V_PS + 8] = col(inp["pool_scale"][l])
        vecs[:, b + V_CB:b + V_CB + 8] = col(inp["conv_b"][l])
        vecs[:, b + V_LG:b + V_LG + 8] = col(inp["conv_ln_g"][l])
        vecs[:, b + V_LB:b + V_LB + 8] = col(inp["conv_ln_b"][l])
        vecs[:, b + V_G2:b + V_G2 + 8] = col(inp["norm_ff_g"][l])
        vecs[:, b + V_G3:b + V_G3 + 8] = col(inp["norm_ple_g"][l])
    vecs[:, V_GF:V_GF + 8] = col(inp["final_norm_g"])
    for l in range(L):
        cw = inp["conv_w"][l]
        for ch in range(8):
            vecs[:, V_CW + l * 248 + ch * 31:V_CW + l * 248 + (ch + 1) * 31] = cw[:, ch * 128:(ch + 1) * 128].T
    return wsrc, wfT, fw, vecs


def build_program(n_layers=L, debug=False):
    nc = bass.Bass("TRN2", target_bir_lowering=False)
    fw_ = FW(nc)
    op, group, dma = fw_.op, fw_.group, fw_.dma

    def din(name, shape, dt=F32):
        return nc.dram_tensor(name, list(shape), dt, kind="ExternalInput").ap()

    xT = din("xT", [D, NT])
    pT = din("pT", [L * 256, NT])
    wsrc = din("wsrc", [L * NPANEL, 128, PAN])
    wfT_d = din("wfT", [L, 4, 128, 2, 1024])
    fwd_d = din("fw", [L, 4, 128, 2, 256])
    vecs_d = din("vecs", [128, NVEC])
    cst_d = din("cst", [128, NCST])
    fcc_d = din("fcc", [128, 2, 2, 256])
    m1s_d = din("m1s", [16, 128, 8, 128], BF16)
    m1p_d = din("m1p", [16, 128, 8, 128], BF16)
    m2_d = din("m2", [128, 2, 128], BF16)
    ident_d = din("ident", [128, 128], BF16)
    yT = nc.dram_tensor("yT", [D, NT], F32, kind="ExternalOutput").ap()

    dk = "ExternalOutput" if debug else "Internal"
    wbf = nc.dram_tensor("wbf", [L * NPANEL, 128, PAN], BF16, kind=dk).ap()
    hscr = nc.dram_tensor("hscr", [D, NT], F32, kind=dk).ap()
    zs = nc.dram_tensor("zs", [4, 2, 2048, 1024], BF16, kind=dk).ap()
    ys = nc.dram_tensor("ys", [128, 128, 1024], BF16, kind=dk).ap()
    ycs = nc.dram_tensor("ycs", [NT, 1024], BF16, kind=dk).ap()
    wfs = nc.dram_tensor("wfs", [128, 8, 2048], BF16).ap()
    b_wfs = [Buf("wfs%d" % g) for g in range(8)]

    b_wbf = [Buf("wbf%d" % i) for i in range(L * NPANEL)]
    b_hscr = [Buf("hscr%d" % i) for i in range(NTILE)]
    b_zs = [Buf("zs%d_%d" % (r, i)) for r in range(2) for i in range(NTILE)]
    b_ys = [Buf("ys%d" % i) for i in range(16)]
    b_ycs = Buf("ycs")
    b_yT = Buf("yT")
    b_in = Buf("inputs")

    def sb(name, shape, dt):
        return nc.alloc_sbuf_tensor("sb_" + name, list(shape), dt)

    vecs = sb("vecs", [128, NVEC], F32).ap()
    cst = sb("cst", [128, NCST], F32).ap()
    ident = sb("ident", [128, 128], BF16).ap()
    ones_b = sb("ones_b", [128, 128], BF16).ap()
    ones_f = sb("ones_f", [128, 128], F32).ap()
    dummy = sb("dummy", [128, 2], F32).ap()
    b_dummy = Buf("dummy")

    def preload_ln_table():
        op("act", lambda e: e.activation(out=dummy[:, 0:1], in_=cst[:, C_EPS:C_EPS + 1], func=AF.Ln),
           [b_const], [b_dummy])
    b_const = Buf("const")
    psb = [nc.alloc_psum_tensor("ps%d" % i, [128, 512], F32).ap() for i in range(8)]
    b_ps = [Buf("ps%d" % i) for i in range(8)]
    ps_state = {"n": 0}

    def next_bank():
        i = ps_state["n"] % 8
        ps_state["n"] += 1
        return psb[i], b_ps[i]

    dma("sp", vecs, vecs_d, [b_in], [b_const])
    dma("sp", cst, cst_d, [b_in], [b_const])
    dma("sp", ident, ident_d, [b_in], [b_const])
    op("dve", lambda e: e.memset(ones_b, 1.0), [], [b_const])
    op("dve", lambda e: e.memset(ones_f, 1.0), [], [b_const])
    eps_ap = cst[:, C_EPS:C_EPS + 1]
    mseg_ap = cst[:, C_MSEG:C_MSEG + 1]

    evac_state = {"n": 0}

    def evac_copy(out, in_, reads, writes):
        evac_state["n"] += 1
        if evac_state["n"] % 2 == 0:
            op("act", lambda e: e.activation(out=out, in_=in_, func=AF.Copy), reads, writes)
        else:
            op("dve", lambda e: e.tensor_copy(out=out, in_=in_), reads, writes)

    dumped = {}

    def dbg_dump(name, ap, bufs):
        if not debug or name in dumped:
            return
        dumped[name] = 1
        shape = list(ap.shape)
        dt_ = ap.dtype
        o = nc.dram_tensor("dbg_" + name, shape, dt_, kind="ExternalOutput").ap()
        dma("pool", o, ap, list(bufs), [Buf("dbg")])

    def precast_panels(indices, stk, engines=("act", "dve", "act")):
        nbuf = len(engines)
        count = len(indices)
        first = indices[0]
        sf = [stk.enter_context(nc.sbuf_tensor("pc_f%d_%d" % (first, i), [128, PAN], F32)).ap() for i in range(nbuf)]
        sbf = [stk.enter_context(nc.sbuf_tensor("pc_b%d_%d" % (first, i), [128, PAN], BF16)).ap() for i in range(nbuf)]
        bf_ = [Buf("pcf%d" % i) for i in range(nbuf)]
        bb_ = [Buf("pcb%d" % i) for i in range(nbuf)]

        def load(j):
            s_ = j % nbuf
            dma("pool", sf[s_], wsrc[indices[j]], [b_in], [bf_[s_]])

        def cast(j):
            i = indices[j]
            s_ = j % nbuf
            eng = engines[s_]
            if eng == "act":
                op("act", lambda e: e.activation(out=sbf[s_], in_=sf[s_], func=AF.Copy), [bf_[s_]], [bb_[s_]])
            else:
                op(eng, lambda e: e.tensor_copy(out=sbf[s_], in_=sf[s_]), [bf_[s_]], [bb_[s_]])
            dma("pool", wbf[i], sbf[s_], [bb_[s_]], [b_wbf[i]])
        return [lambda j=j: load(j) for j in range(count)], [lambda j=j: cast(j) for j in range(count)]

    def rmsnorm(h, hb, sq, sqb, rs, rsb, tmp, tmpb, xn, xnb, gcol, n):
        def pick(lst, c):
            return lst[c] if len(lst) == 8 else lst[0]
        for c in range(8):
            op("act", lambda e, c=c: e.activation(out=sq[:, c, :], in_=h[:, c, :], func=AF.Square),
               [pick(hb, c)], [pick(sqb, c)])
        ps, pb = next_bank()
        for c in range(8):
            group("pe", [lambda e, c=c: e.matmul(ps[:, 0:n], lhsT=ones_b, rhs=sq[:, c, :], start=(c == 0),
                                                 stop=(c == 7))], [pick(sqb, c), b_const], [pb])
        op("act", lambda e: e.activation(out=tmp, in_=ps[:, 0:n], func=AF.Ln, bias=eps_ap, scale=1.0 / D),
           [pb, b_const], [tmpb])
        op("act", lambda e: e.activation(out=rs, in_=tmp, func=AF.Exp, scale=-0.5), [tmpb], [rsb])
        for c in range(8):
            op("dve", lambda e, c=c: e.scalar_tensor_tensor(out=xn[:, c, :], in0=h[:, c, :],
                                                            scalar=vecs[:, gcol + c:gcol + c + 1], in1=rs,
                                                            op0=ALU.mult, op1=ALU.mult),
               [pick(hb, c), rsb, b_const], [pick(xnb, c)])

    def layer(l):
        hsrc = xT if l == 0 else hscr
        vb = l * 80

        def hsrc_bufs(i):
            return [b_in] if l == 0 else [b_hscr[i]]

        stk_box = [contextlib.ExitStack()]

        def tsb(name, shape, dt):
            return stk_box[0].enter_context(nc.sbuf_tensor("L%d_%s" % (l, name), list(shape), dt)).ap()

        def end_phase():
            fw_.barrier()
            stk_box[0].close()
            stk_box[0] = contextlib.ExitStack()

        wfp = tsb("wfp", [128, 8, 2048], BF16)
        b_wfp = Buf("wfp")

        def wf_temps(alloc, tag):
            d = dict(wft=[alloc(tag + "wft%d" % k, [128, 2, 1024], F32) for k in range(2)],
                     fwt=[alloc(tag + "fwt%d" % k, [128, 2, 256], F32) for k in range(2)],
                     gsb=alloc(tag + "gsb", [128, 2, 512], F32), fcc=alloc(tag + "fcc", [128, 2, 2, 256], F32),
                     b_wft=[Buf("wft0"), Buf("wft1")], b_fwt=[Buf("fwt0"), Buf("fwt1")],
                     b_gsb=Buf("gsb"), b_fcc=Buf("fcc"))
            dma("sp", d["fcc"], fcc_d, [b_in], [d["b_fcc"]])
            return d

        def wf_load(ll, g, d):
            dma("sp", d["wft"][g % 2], wfT_d[ll, g], [b_in], [d["b_wft"][g % 2]])
            dma("sp", d["fwt"][g % 2], fwd_d[ll, g], [b_in], [d["b_fwt"][g % 2]])

        def wf_group(ll, g, d, out_fn):
            wft, fwt, gsb, fcc = d["wft"][g % 2], d["fwt"][g % 2], d["gsb"], d["fcc"]
            b_wft_, b_fwt_ = d["b_wft"][g % 2], d["b_fwt"][g % 2]
            for cc in range(2):
                ps, pb = next_bank()
                for ri in range(2):
                    group("pe", [lambda e, kc=kc, ri=ri, cc=cc: e.matmul(
                        ps[:, ri * 256:(ri + 1) * 256], lhsT=fcc[:, kc, ri, cc * 128:(cc + 1) * 128],
                        rhs=fwt[:, kc, :], start=(kc == 0), stop=(kc == 1)) for kc in range(2)],
                        [d["b_fcc"], b_fwt_], [pb])
                evac_copy(gsb[:, cc, :], ps, [pb], [d["b_gsb"]])
            for dk_ in range(8):
                ps, pb = next_bank()
                group("pe", [lambda e, kc=kc, dk_=dk_: e.matmul(
                    ps, lhsT=wft[:, kc, dk_ * 128:(dk_ + 1) * 128], rhs=gsb[:, kc, :],
                    start=(kc == 0), stop=(kc == 1)) for kc in range(2)], [b_wft_, d["b_gsb"]], [pb])
                out_fn(g, dk_, ps, pb)

        pcl, pcc = [], []
        if l == 0:
            pc_idx = [p_ for p_ in range(NPANEL) if p_ not in [P_CD + c for c in DCH]]
            pcl, pcc = precast_panels(pc_idx, stk_box[0])
        hA = [tsb("hA%d" % i, [128, 8, T], F32) for i in range(2)]
        b_hA = [Buf("hA%d" % i) for i in range(2)]
        sqA1 = tsb("sqA", [128, 8, T], BF16)
        sqA = [sqA1, sqA1]
        b_sqA1 = [Buf("sqA_%d" % c) for c in range(8)]
        b_sqA = [b_sqA1, b_sqA1]
        xnA = [tsb("xnA%d" % i, [128, 8, T], BF16) for i in range(2)]
        b_xnA = [[Buf("xnA%d_%d" % (i, c)) for c in range(8)] for i in range(2)]
        rsA = tsb("rsA", [128, T], F32)
        tmA = tsb("tmA", [128, T], F32)
        b_rsA, b_tmA = Buf("rsA"), Buf("tmA")

        def loadA(i):
            dma("sp", hA[i % 2], hsrc[:, i * T:(i + 1) * T].rearrange("(c p) t -> p c t", p=128),
                hsrc_bufs(i), [b_hA[i % 2]])

        def normA(i):
            s_ = i % 2
            rmsnorm(hA[s_], [b_hA[s_]], sqA[s_], b_sqA[s_], rsA, b_rsA, tmA, b_tmA, xnA[s_], b_xnA[s_], vb + V_G1, T)

        if l == 0 or n_layers == 1:
            sub = contextlib.ExitStack()

            def ssb(name, shape, dt):
                return sub.enter_context(nc.sbuf_tensor("L%d_%s" % (l, name), list(shape), dt)).ap()
            d0 = wf_temps(ssb, "a")

            def out_sb(g, dk_, ps, pb):
                o = wfp[:, dk_, :].rearrange("p (r g m) -> p r g m", r=2, g=4)[:, :, g, :]
                evac_copy(o, ps.rearrange("p (r m) -> p r m", r=2), [pb], [b_wfp])
            outstanding = []
            wf_load(l, 0, d0)
            for g in range(4):
                if g + 1 < 4:
                    wf_load(l, g + 1, d0)
                for _ in range(3):
                    if pcl:
                        pcl.pop(0)()
                        outstanding.append(pcc.pop(0))
                wf_group(l, g, d0, out_sb)
                if g == 0:
                    loadA(0)
                    loadA(1)
                for f_ in outstanding:
                    f_()
                outstanding = []
            normA(0)
            sub.close()
        else:
            loadA(0)
            dma("sp", wfp, wfs, b_wfs, [b_wfp])
            loadA(1)
            normA(0)

        ztA = [tsb("ztA%d" % i, [128, 4, 2048], BF16) for i in range(2)]
        b_ztA = [Buf("ztA%d" % i) for i in range(2)]
        pc_per_tile = -(-len(pcl) // NTILE)

        for i in range(NTILE):
            s = i % 2
            if i + 2 < NTILE:
                loadA(i + 2)
            npc_t = min(pc_per_tile, len(pcl))
            for _ in range(npc_t):
                pcl.pop(0)()
            for tb in range(4):
                if tb == 2 and i + 1 < NTILE:
                    normA(i + 1)
                for pn in range(4):
                    ps, pb = next_bank()
                    group("pe", [lambda e, kc=kc, tb=tb, pn=pn: e.matmul(
                        ps, lhsT=xnA[s][:, kc, tb * 128:(tb + 1) * 128], rhs=wfp[:, kc, pn * 512:(pn + 1) * 512],
                        start=(kc == 0), stop=(kc == 7)) for kc in range(8)], b_xnA[s] + [b_wfp], [pb])
                    evac_copy(ztA[s][:, tb, pn * 512:(pn + 1) * 512], ps, [pb], [b_ztA[s]])
            for ri in range(2):
                dma("sp", zs[i // 4, ri, (i % 4) * T:(i % 4 + 1) * T, :].rearrange("(tb p) c -> p tb c", p=128),
                    ztA[s][:, :, ri * 1024:(ri + 1) * 1024], [b_ztA[s]], [b_zs[ri * NTILE + i]])
            for _ in range(npc_t):
                pcc.pop(0)()
        assert not pcl and not pcc
        end_phase()

        in1s = [tsb("in1s%d" % i, [128, 8, 1024], BF16) for i in range(2)]
        in1p = [tsb("in1p%d" % i, [128, 8, 1024], BF16) for i in range(2)]
        m1s = [tsb("m1s%d" % i, [128, 8, 128], BF16) for i in range(2)]
        m1p = [tsb("m1p%d" % i, [128, 8, 128], BF16) for i in range(2)]
        o1 = [tsb("o1_%d" % i, [128, 8, 1024], BF16) for i in range(2)]
        b_in1 = [[Buf("in1_%d_%d" % (i, j)) for j in range(6)] for i in range(2)]
        do_wf1 = (l == 0 and n_layers > 1)
        if do_wf1:
            d1 = wf_temps(tsb, "b")
            wstg = tsb("wstg", [128, 8, 512], BF16)
            b_wstg = Buf("wstg")

            def out_dram(g, dk_, ps, pb):
                evac_copy(wstg[:, dk_, :], ps, [pb], [b_wstg])
                if dk_ == 7:
                    for r_ in range(2):
                        c0 = r_ * 1024 + g * 256
                        dma("sp", wfs[:, :, c0:c0 + 256], wstg[:, :, r_ * 256:(r_ + 1) * 256],
                            [b_wstg], [b_wfs[2 * g + r_]])
        b_o1 = [Buf("o1_%d" % i) for i in range(2)]
        for gi in range(16):
            s = gi % 2
            b0 = gi * 8
            q, bp0 = b0 // 32, b0 % 32
            src = zs.rearrange("q r t c -> (q r t) c").rearrange("(p b) c -> p b c", b=128)[:, b0:b0 + 8, :]
            dma("sp", in1s[s], src, b_zs, [b_in1[s][0], b_in1[s][1]])
            src = zs[q].rearrange("r t c -> (r t) c").rearrange("(p b) c -> p b c", b=32)[:, bp0:bp0 + 8, :]
            dma("sp", in1p[s], src, b_zs, [b_in1[s][2], b_in1[s][3]])
            dma("sp", m1s[s], m1s_d[gi], [b_in], [b_in1[s][4]])
            dma("sp", m1p[s], m1p_d[gi], [b_in], [b_in1[s][5]])
            for bb in range(8):
                for hf in range(2):
                    ps, pb = next_bank()
                    group("pe", [
                        lambda e, bb=bb, hf=hf, ps=ps: e.matmul(ps, lhsT=m1s[s][:, bb, :],
                                                                rhs=in1s[s][:, bb, hf * 512:(hf + 1) * 512],
                                                                start=True, stop=False),
                        lambda e, bb=bb, hf=hf, ps=ps: e.matmul(ps, lhsT=m1p[s][:, bb, :],
                                                                rhs=in1p[s][:, bb, hf * 512:(hf + 1) * 512],
                                                                start=False, stop=True)],
                        b_in1[s], [pb])
                    evac_copy(o1[s][:, bb, hf * 512:(hf + 1) * 512], ps, [pb], [b_o1[s]])
            dma("pool", ys[b0:b0 + 8].rearrange("b p c -> p b c"), o1[s], [b_o1[s]], [b_ys[gi]])
            if do_wf1 and gi % 4 == 1:
                wf_load(1, gi // 4, d1)
                wf_group(1, gi // 4, d1, out_dram)
        end_phase()

        y2g = [tsb("y2g%d" % i, [128, 2, 8, 1024], BF16) for i in range(2)]
        b_y2g = [Buf("y2g%d" % i) for i in range(2)]
        m2 = tsb("m2", [128, 2, 128], BF16)
        b_m2 = Buf("m2")
        o2 = [tsb("o2_%d" % i, [128, 8, 1024], BF16) for i in range(2)]
        b_o2 = [Buf("o2_%d" % i) for i in range(2)]
        dma("sp", m2, m2_d, [b_in], [b_m2])
        ys4 = ys.rearrange("b (r k) c -> b r k c", r=2)
        for kg in range(8):
            s = kg % 2
            dma("sp", y2g[s], ys4[:, :, kg * 8:(kg + 1) * 8, :], b_ys, [b_y2g[s]])
            for kk in range(8):
                for hf in range(2):
                    ps, pb = next_bank()
                    group("pe", [lambda e, r=r, kk=kk, hf=hf, ps=ps: e.matmul(
                        ps, lhsT=m2[:, r, :], rhs=y2g[s][:, r, kk, hf * 512:(hf + 1) * 512],
                        start=(r == 0), stop=(r == 1)) for r in range(2)], [b_m2, b_y2g[s]], [pb])
                    evac_copy(o2[s][:, kk, hf * 512:(hf + 1) * 512], ps, [pb], [b_o2[s]])
            dst = ycs.rearrange("(j k) c -> j k c", k=64)[:, kg * 8:(kg + 1) * 8, :]
            dma("pool", dst, o2[s], [b_o2[s]], [b_ycs])
        end_phase()

        NR = 4
        wr = [tsb("wr%d" % i, [128, PAN], BF16) for i in range(NR)]
        b_wr = [Buf("wr%d" % i) for i in range(NR)]
        hB = [tsb("h%d" % i, [128, 8, T], F32) for i in range(2)]
        b_hB = [[Buf("h%d_%d" % (i, c)) for c in range(8)] for i in range(2)]
        hhB = [tsb("hh%d" % i, [128, 8, 2, 16], F32) for i in range(2)]
        b_hhL = [Buf("hhL%d" % i) for i in range(2)]
        b_hhR = [Buf("hhR%d" % i) for i in range(2)]
        xn = tsb("xn", [128, 8, T], BF16)
        b_xn = [Buf("xn%d" % c) for c in range(8)]
        xn1 = tsb("xn1", [128, 8, T], BF16)
        b_xn1 = [Buf("xn1_%d" % c) for c in range(8)]
        xh = tsb("xh", [128, 8, 2, 16], BF16)
        b_xh = Buf("xh")
        sq = tsb("sq", [128, 8, T], BF16)
        b_sq = [Buf("sq%d" % c) for c in range(8)]
        sqh = tsb("sqh", [128, 8, 2, 16], BF16)
        b_sqh = Buf("sqh")
        rs = tsb("rs", [128, T], F32)
        tm = tsb("tm", [128, T], F32)
        mu = tsb("mu", [128, T], F32)
        msq = tsb("msq", [128, T], F32)
        rsh = tsb("rsh", [128, 32], F32)
        tmh = tsb("tmh", [128, 32], F32)
        b_rs, b_tm, b_mu, b_msq, b_rsh, b_tmh = (Buf(n) for n in ("rs", "tm", "mu", "msq", "rsh", "tmh"))
        upy = tsb("upy", [128, 4352], BF16)
        up = upy.bitcast(F32).rearrange("p (c t) -> p c t", c=4)
        yct = upy[:, 0:4096].rearrange("p (b c) -> p b c", b=4)
        b_up = [Buf("up%d" % c) for c in range(4)]
        stt = tsb("st", [128, 3, 544], F32)
        st = [stt[:, k, :] for k in range(3)]
        b_st = [Buf("st%d" % k) for k in range(3)]
        pt = stt[:, 0:2, 0:512]
        pl = tsb("pl", [128, 8, T], BF16)
        b_pl = [Buf("pl%d" % c) for c in range(8)]
        gt = tsb("gt", [128, 8, T], BF16)
        b_gt = [Buf("gt%d" % c) for c in range(8)]
        sg = tsb("sg", [128, 4, 544], BF16)
        b_sg = [Buf("sg%d" % c) for c in range(4)]
        vx = tsb("vx", [128, 8, 544], BF16)
        b_vx = [Buf("vx%d" % c) for c in range(8)]
        big = tsb("big", [128, 32, T], BF16)
        hid = big
        bigf = big.rearrange("p c t -> p (c t)").bitcast(F32)
        cv = bigf[:, 0:4096].rearrange("p (c t) -> p c t", c=8)
        mg = bigf[:, 4096:8192].rearrange("p (c t) -> p c t", c=8)
        b_cv = [Buf("cv%d" % c) for c in range(8)]
        b_mg = [Buf("mg%d" % c) for c in range(8)]
        tf = [tsb("tf%d" % i, [128, T], F32) for i in range(3)]
        b_tf = [Buf("tf%d" % i) for i in range(3)]
        rl = [tsb("rl%d" % i, [128, T], BF16) for i in range(2)]
        b_rl = [Buf("rl%d" % i) for i in range(2)]
        pbf = tsb("pbf", [128, 2, T], BF16)
        b_pbf = Buf("pbf")
        mb = sq
        b_mb = b_sq
        tstate = {"n": 0, "r": 0}

        def next_tf():
            k = tstate["n"] % 3
            tstate["n"] += 1
            return tf[k], b_tf[k]

        def next_rl():
            k = tstate["r"] % 2
            tstate["r"] += 1
            return rl[k], b_rl[k]

        wstate = {"issued": 0, "used": 0}
        do_pc = (l == 0 and n_layers > 1)
        seq = []
        npc = 0
        for i in range(NTILE):
            for pi in ORDER:
                seq.append(("w", pi))
                if do_pc and P_FF2 <= pi < P_FF2 + 8 and npc < 2 * NPANEL:
                    seq.append(("c", npc))
                    npc += 1
        assert (not do_pc) or npc == 2 * NPANEL
        cst_b = [tsb("cstg%d" % k, [128, PAN // 2], BF16) for k in range(2)] if do_pc else []
        b_cstg = [Buf("cstg%d" % k) for k in range(2)]

        def ring_issue(u):
            while wstate["issued"] < min(len(seq), u + NR):
                j = wstate["issued"]
                kind, a = seq[j]
                if kind == "w":
                    gi = l * NPANEL + a
                    dma("sp", wr[j % NR], wbf[gi], [b_wbf[gi]], [b_wr[j % NR]])
                else:
                    gi = NPANEL + a // 2
                    hf = a % 2
                    dma("sp", wr[j % NR].bitcast(F32), wsrc[gi][:, hf * 2048:(hf + 1) * 2048], [b_in], [b_wr[j % NR]])
                wstate["issued"] += 1

        def panel(pi_expected):
            u = wstate["used"]
            assert seq[u] == ("w", pi_expected), (seq[u], pi_expected)
            ring_issue(u)
            wstate["used"] += 1
            return wr[u % NR], b_wr[u % NR]

        def precast_step():
            u = wstate["used"]
            if u >= len(seq) or seq[u][0] != "c":
                return
            a = seq[u][1]
            ring_issue(u)
            wstate["used"] += 1
            gi = NPANEL + a // 2
            hf = a % 2
            k = a % 2
            src = wr[u % NR].bitcast(F32)
            op("pool", lambda e: e.tensor_copy(out=cst_b[k], in_=src), [b_wr[u % NR]], [b_cstg[k]])
            dma("pool", wbf[gi][:, hf * 2048:(hf + 1) * 2048], cst_b[k], [b_cstg[k]], [b_wbf[gi]])

        def halo_view(a2d):
            return a2d.rearrange("p (a b) -> p a b", b=16)[:, 0:34:33, :]

        def load_h(i):
            s_ = i % 2
            s0 = i * T
            h, hh = hB[s_], hhB[s_]
            dma("pool", h, hsrc[:, s0:s0 + T].rearrange("(c p) t -> p c t", p=128), hsrc_bufs(i), b_hB[s_])
            if i > 0:
                dma("pool", hh[:, :, 0, :], hsrc[:, s0 - 16:s0].rearrange("(c p) t -> p c t", p=128),
                    hsrc_bufs(i - 1), [b_hhL[s_]])
            else:
                op("dve", lambda e: e.memset(hh[:, :, 0, :], 0.0), [], [b_hhL[s_]])
            if i < NTILE - 1:
                dma("pool", hh[:, :, 1, :], hsrc[:, s0 + T:s0 + T + 16].rearrange("(c p) t -> p c t", p=128),
                    hsrc_bufs(i + 1), [b_hhR[s_]])
            else:
                op("dve", lambda e: e.memset(hh[:, :, 1, :], 0.0), [], [b_hhR[s_]])
            if i % 4 == 0 and i > 0:
                op("pool", lambda e: e.tensor_scalar(out=hh[:, :, 0, :], in0=hh[:, :, 0, :], scalar1=mseg_ap,
                                                     scalar2=1.0, op0=ALU.mult, op1=ALU.mult),
                   [b_hhL[s_], b_const], [b_hhL[s_]])
            if i % 4 == 3 and i < NTILE - 1:
                op("pool", lambda e: e.tensor_scalar(out=hh[:, :, 1, :], in0=hh[:, :, 1, :], scalar1=mseg_ap,
                                                     scalar2=1.0, op0=ALU.mult, op1=ALU.mult),
                   [b_hhR[s_], b_const], [b_hhR[s_]])

        def norm1(i):
            s_ = i % 2
            h, hh = hB[s_], hhB[s_]
            rmsnorm(h, b_hB[s_], sq, b_sq, rs, b_rs, tm, b_tm, xn1, b_xn1, vb + V_G1, T)
            op("act", lambda e: e.activation(out=sqh, in_=hh, func=AF.Square), [b_hhL[s_], b_hhR[s_]], [b_sqh])
            ps, pb = next_bank()
            group("pe", [lambda e, c=c, ps=ps: e.matmul(ps[:, 0:32].rearrange("p (a b) -> p a b", b=16),
                                                         lhsT=ones_b, rhs=sqh[:, c, :, :],
                                                         start=(c == 0), stop=(c == 7)) for c in range(8)],
                  [b_sqh, b_const], [pb])
            op("act", lambda e, ps=ps: e.activation(out=tmh, in_=ps[:, 0:32], func=AF.Ln, bias=eps_ap,
                                                    scale=1.0 / D), [pb, b_const], [b_tmh])
            op("act", lambda e: e.activation(out=rsh, in_=tmh, func=AF.Exp, scale=-0.5), [b_tmh], [b_rsh])
            for c in range(8):
                op("dve", lambda e, c=c: e.scalar_tensor_tensor(
                    out=xh[:, c, :, :], in0=hh[:, c, :, :], scalar=vecs[:, vb + V_G1 + c:vb + V_G1 + c + 1],
                    in1=rsh.rearrange("p (a b) -> p a b", b=16), op0=ALU.mult, op1=ALU.mult),
                    [b_hhL[s_], b_hhR[s_], b_rsh, b_const], [b_xh])

        def proj(pi, rhs_fn, rhs_bufs, nkc, consume, halo_fn=None, ncols=512, fine=False):
            w, wb = panel(pi)
            w3 = w[:, 0:nkc * ncols].rearrange("p (k n) -> p k n", k=nkc)
            for ch in range(ncols // 128):
                ps, pb = next_bank()
                if fine and ch == 0:
                    for kc in range(nkc):
                        group("pe", [lambda e, kc=kc, ch=ch, ps=ps: e.matmul(
                            ps, lhsT=w3[:, kc, ch * 128:(ch + 1) * 128], rhs=rhs_fn(kc),
                            start=(kc == 0), stop=(kc == nkc - 1))], [wb, rhs_bufs[kc]], [pb])
                elif halo_fn is None:
                    group("pe", [lambda e, kc=kc, ch=ch, ps=ps: e.matmul(
                        ps, lhsT=w3[:, kc, ch * 128:(ch + 1) * 128], rhs=rhs_fn(kc),
                        start=(kc == 0), stop=(kc == nkc - 1)) for kc in range(nkc)], [wb] + rhs_bufs, [pb])
                psh = pbh = None
                if halo_fn is not None:
                    psh, pbh = next_bank()
                    pshv = psh[:, 0:32].rearrange("p (a b) -> p a b", b=16)
                    fns = [lambda e, kc=kc, ch=ch, ps=ps: e.matmul(
                        ps, lhsT=w3[:, kc, ch * 128:(ch + 1) * 128], rhs=rhs_fn(kc),
                        start=(kc == 0), stop=(kc == nkc - 1)) for kc in range(nkc)]
                    fns += [lambda e, kc=kc, ch=ch, pshv=pshv: e.matmul(
                        pshv, lhsT=w3[:, kc, ch * 128:(ch + 1) * 128], rhs=halo_fn(kc),
                        start=(kc == 0), stop=(kc == nkc - 1)) for kc in range(nkc)]
                    group("pe", fns, [wb, b_xh] + rhs_bufs, [pb, pbh])
                consume(ch, ps, pb, psh, pbh)

        xn1_main = lambda kc: xn1[:, kc, :]
        xn_main = lambda kc: xn[:, kc, :]
        xn_halo = lambda kc: xh[:, kc, :, :]

        def gate_proj(pbase, branch):
            for pp in range(2):
                def cons_g(ch, ps, pb, psh, pbh, pp=pp):
                    gch = pp * 4 + ch
                    col = vb + V_BG + 8 * branch + gch
                    op("act", lambda e: e.activation(out=gt[:, gch, :], in_=ps, func=AF.Sigmoid,
                                                     bias=vecs[:, col:col + 1], scale=1.0),
                       [pb, b_const], [b_gt[gch]])
                proj(pbase + pp, xn1_main, b_xn1, 8, cons_g)

        def mixer(i):
            s_ = i % 2
            s0 = i * T
            h, b_h = hB[s_], b_hB[s_]
            left_edge = (i % 4 == 0)
            right_edge = (i % 4 == 3)
            kindL = 0 if i == 0 else 1
            kindR = 0 if i == NTILE - 1 else 1

            for pp in range(2):
                def cons_pool(ch, ps, pb, psh, pbh, pp=pp):
                    op("act", lambda e: e.activation(out=up[:, ch, 16:528], in_=ps, func=AF.Copy), [pb], [b_up[ch]])
                    op("act", lambda e: e.activation(out=halo_view(up[:, ch, :]),
                                                     in_=psh[:, 0:32].rearrange("p (a b) -> p a b", b=16),
                                                     func=AF.Copy), [pbh], [b_up[ch]])
                    g = pp * 2 + ch // 2
                    w = WINS[g]
                    gch = pp * 4 + ch
                    u = up[:, ch, :]
                    cur, cb = u, b_up[ch]
                    lvl = 0
                    ranges = {2: (9, 535), 4: (10, 534), 8: (12, 532), 16: (16, 528)}
                    offs = {2: (-1, 0), 4: (-1, 1), 8: (-2, 2), 16: (-4, 4)}
                    ww = 2
                    while ww <= w:
                        a, b = ranges[ww]
                        o0, o1_ = offs[ww]
                        dst, db = st[lvl % 3], b_st[lvl % 3]
                        op("dve", lambda e, dst=dst, cur=cur, a=a, b=b, o0=o0, o1_=o1_: e.tensor_tensor(
                            out=dst[:, a:b], in0=cur[:, a + o0:b + o0], in1=cur[:, a + o1_:b + o1_], op=ALU.add),
                            [cb], [db])
                        cur, cb = dst, db
                        lvl += 1
                        ww *= 2
                    op("dve", lambda e, cur=cur: e.scalar_tensor_tensor(
                        out=pl[:, gch, :], in0=cur[:, 16:528], scalar=1.0 / w, in1=u[:, 16:528],
                        op0=ALU.mult, op1=ALU.subtract), [cb, b_up[ch]], [b_pl[gch]])
                    for side, flag, kind in ((0, left_edge, kindL), (1, right_edge, kindR)):
                        if not flag:
                            continue
                        c0 = C_TAB + ((kind * 2 + side) * 4 + g) * 8
                        e0 = 16 if side == 0 else 520
                        m0 = 0 if side == 0 else 504
                        tfa, tfb = next_tf()
                        op("dve", lambda e, cur=cur, c0=c0, e0=e0, tfa=tfa: e.tensor_tensor(
                            out=tfa[:, 0:8], in0=cur[:, e0:e0 + 8], in1=cst[:, c0:c0 + 8], op=ALU.mult),
                            [cb, b_const], [tfb])
                        op("dve", lambda e, e0=e0, m0=m0, tfa=tfa: e.tensor_tensor(
                            out=pl[:, gch, m0:m0 + 8], in0=tfa[:, 0:8], in1=u[:, e0:e0 + 8], op=ALU.subtract),
                            [tfb, b_up[ch]], [b_pl[gch]])
                proj(P_POOL + pp, xn1_main, b_xn1, 8, cons_pool, halo_fn=xn_halo)
            dbg_dump('xn1', xn1, b_xn1)
            dbg_dump('xh', xh, [b_xh])
            dbg_dump('pl0', pl, b_pl)
            gate_proj(P_GA, 0)
            w, wb = panel(P_POOLW)
            w4 = w[:, 0:2048].rearrange("p (g k n) -> p g k n", g=4, k=2)
            for dch in range(8):
                g, half = dch // 2, dch % 2
                ps, pb = next_bank()
                group("pe", [lambda e, kc=kc, g=g, half=half, ps=ps: e.matmul(
                    ps, lhsT=w4[:, g, kc, half * 128:(half + 1) * 128], rhs=pl[:, 2 * g + kc, :],
                    start=(kc == 0), stop=(kc == 1)) for kc in range(2)],
                    [wb, b_pl[2 * g], b_pl[2 * g + 1]], [pb])
                op("dve", lambda e, dch=dch, ps=ps: e.scalar_tensor_tensor(
                    out=mg[:, dch, :], in0=ps, scalar=vecs[:, vb + V_PS + dch:vb + V_PS + dch + 1],
                    in1=gt[:, dch, :], op0=ALU.mult, op1=ALU.mult), [pb, b_const, b_gt[dch]], [b_mg[dch]])

            dma("pool", yct, ycs[s0:s0 + T, :].rearrange("(b p) c -> p b c", p=128), [b_ycs], b_up)

            bg = []
            for pp in range(2):
                def cons_cg(ch, ps, pb, psh, pbh):
                    op("act", lambda e: e.activation(out=sg[:, ch, 16:528], in_=ps, func=AF.Sigmoid),
                       [pb], [b_sg[ch]])
                    op("act", lambda e: e.activation(out=halo_view(sg[:, ch, :]),
                                                     in_=psh[:, 0:32].rearrange("p (a b) -> p a b", b=16),
                                                     func=AF.Sigmoid), [pbh], [b_sg[ch]])
                proj(P_CG0 if pp == 0 else P_CG1, xn1_main, b_xn1, 8, cons_cg, halo_fn=xn_halo)

                def cons_ca(ch, ps, pb, psh, pbh, pp=pp):
                    gch = pp * 4 + ch
                    op("dve", lambda e: e.tensor_tensor(out=vx[:, gch, 16:528], in0=ps, in1=sg[:, ch, 16:528],
                                                        op=ALU.mult), [pb, b_sg[ch]], [b_vx[gch]])
                    op("dve", lambda e: e.tensor_tensor(out=halo_view(vx[:, gch, :]),
                                                        in0=psh[:, 0:32].rearrange("p (a b) -> p a b", b=16),
                                                        in1=halo_view(sg[:, ch, :]), op=ALU.mult),
                       [pbh, b_sg[ch]], [b_vx[gch]])
                    if pp == 1:
                        for _ in range(4):
                            if bg:
                                bg.pop(0)()
                proj(P_CA0 if pp == 0 else P_CA1, xn1_main, b_xn1, 8, cons_ca, halo_fn=xn_halo)
                if pp == 0:
                    for k in range(31):
                        for ch in DCH:
                            wcol = vecs[:, V_CW + l * 248 + ch * 31 + k:V_CW + l * 248 + ch * 31 + k + 1]
                            if k == 0:
                                bcol = vecs[:, vb + V_CB + ch:vb + V_CB + ch + 1]
                                bg.append(lambda ch=ch, wcol=wcol, bcol=bcol: op(
                                    "dve", lambda e: e.tensor_scalar(out=cv[:, ch, :], in0=vx[:, ch, 1:513],
                                                                     scalar1=wcol, scalar2=bcol,
                                                                     op0=ALU.mult, op1=ALU.add),
                                    [b_vx[ch], b_const], [b_cv[ch]], skip_self=True))
                            else:
                                bg.append(lambda ch=ch, wcol=wcol, k=k: op(
                                    "dve", lambda e: e.scalar_tensor_tensor(
                                        out=cv[:, ch, :], in0=vx[:, ch, 1 + k:513 + k], scalar=wcol,
                                        in1=cv[:, ch, :], op0=ALU.mult, op1=ALU.add),
                                    [b_vx[ch], b_cv[ch], b_const], [b_cv[ch]], skip_self=True))
            while bg:
                bg.pop(0)()
            pe_chunks = [c for c in range(8) if c not in DCH]
            for ch in pe_chunks:
                w, wb = panel(P_CD + ch)
                w3 = w[:, 0:31 * 128].rearrange("p (k n) -> p k n", k=31)
                ps, pb = next_bank()
                group("pe", [lambda e, k=k, ch=ch, ps=ps, w3=w3: e.matmul(
                    ps, lhsT=w3[:, k, :], rhs=vx[:, ch, 1 + k:513 + k], start=(k == 0), stop=(k == 30))
                    for k in range(31)], [wb, b_vx[ch]], [pb])
                bcol = vecs[:, vb + V_CB + ch:vb + V_CB + ch + 1]
                op("act", lambda e, ch=ch, ps=ps, bcol=bcol: e.activation(out=cv[:, ch, :], in_=ps, func=AF.Identity,
                                                                          bias=bcol, scale=1.0),
                   [pb, b_const], [b_cv[ch]])
                op("act", lambda e, ch=ch, ps=ps, bcol=bcol: e.activation(out=sq[:, ch, :], in_=ps, func=AF.Square,
                                                                          bias=bcol, scale=1.0),
                   [pb, b_const], [b_sq[ch]])
                op("act", lambda e, ch=ch, ps=ps, bcol=bcol: e.activation(out=vx[:, ch, 16:528], in_=ps,
                                                                          func=AF.Identity, bias=bcol, scale=1.0),
                   [pb, b_const], [b_vx[ch]])
                if ch == pe_chunks[1]:
                    for dc in DCH:
                        op("act", lambda e, dc=dc: e.activation(out=sq[:, dc, :], in_=cv[:, dc, :], func=AF.Square),
                           [b_cv[dc]], [b_sq[dc]])
                        op("act", lambda e, dc=dc: e.activation(out=vx[:, dc, 16:528], in_=cv[:, dc, :],
                                                                func=AF.Identity), [b_cv[dc]], [b_vx[dc]])
            dbg_dump('vx', vx, b_vx)
            preload_ln_table()
            ps1, pb1 = next_bank()
            ps2, pb2 = next_bank()
            corder = list(DCH) + pe_chunks
            for n_, c in enumerate(corder):
                group("pe", [lambda e, c=c, n_=n_: e.matmul(ps2, lhsT=ones_b, rhs=sq[:, c, :], start=(n_ == 0),
                                                            stop=(n_ == 7))], [b_sq[c], b_const], [pb2])
                group("pe", [lambda e, c=c, n_=n_: e.matmul(ps1, lhsT=ones_b, rhs=vx[:, c, 16:528], start=(n_ == 0),
                                                            stop=(n_ == 7))], [b_vx[c], b_const], [pb1])
            op("act", lambda e: e.activation(out=mu, in_=ps1, func=AF.Identity, scale=1.0 / D), [pb1], [b_mu])
            op("dve", lambda e: e.tensor_tensor(out=msq, in0=mu, in1=mu, op=ALU.mult), [b_mu], [b_msq])
            op("dve", lambda e: e.scalar_tensor_tensor(out=tm, in0=ps2, scalar=1.0 / D, in1=msq,
                                                       op0=ALU.mult, op1=ALU.subtract), [pb2, b_msq], [b_tm])
            op("act", lambda e: e.activation(out=tm, in_=tm, func=AF.Ln, bias=eps_ap, scale=1.0),
               [b_tm, b_const], [b_tm])
            op("act", lambda e: e.activation(out=rs, in_=tm, func=AF.Exp, scale=-0.5), [b_tm], [b_rs])
            for ch in range(4, 8):
                op("pool", lambda e, ch=ch: e.tensor_tensor(out=cv[:, ch, :], in0=cv[:, ch, :], in1=mu,
                                                            op=ALU.subtract), [b_cv[ch], b_mu], [b_cv[ch]])
            for ch in range(8):
                if ch < 4:
                    op("dve", lambda e, ch=ch: e.tensor_tensor(out=cv[:, ch, :], in0=cv[:, ch, :], in1=mu,
                                                               op=ALU.subtract), [b_cv[ch], b_mu], [b_cv[ch]])
                op("dve", lambda e, ch=ch: e.tensor_tensor(out=cv[:, ch, :], in0=cv[:, ch, :], in1=rs, op=ALU.mult),
                   [b_cv[ch], b_rs], [b_cv[ch]], skip_self=True)
            gate_proj(P_GC, 2)
            for ch in range(8):
                op("act", lambda e, ch=ch: e.activation(
                    out=pl[:, ch, :], in_=cv[:, ch, :], func=AF.Silu,
                    bias=vecs[:, vb + V_LB + ch:vb + V_LB + ch + 1],
                    scale=vecs[:, vb + V_LG + ch:vb + V_LG + ch + 1]), [b_cv[ch], b_const], [b_pl[ch]])
            for dch in range(8):
                ps, pb = next_bank()
                psv = ps.bitcast(BF16)
                group("pe", [lambda e, b=b, dch=dch, psv=psv: e.transpose(
                    psv[:, b * 128:(b + 1) * 128], yct[:, b, dch * 128:(dch + 1) * 128], ident)
                    for b in range(4)], b_up + [b_const], [pb])
                tfa, tfb = next_tf()
                op("dve", lambda e, psv=psv, dch=dch, tfa=tfa: e.tensor_tensor(
                    out=tfa, in0=psv[:, 0:512], in1=gt[:, dch, :], op=ALU.mult), [pb, b_gt[dch]], [tfb])
                op("pool", lambda e, dch=dch, tfa=tfa: e.tensor_tensor(
                    out=mg[:, dch, :], in0=mg[:, dch, :], in1=tfa, op=ALU.add), [tfb, b_mg[dch]], [b_mg[dch]])
            gate_proj(P_GB, 1)
            for pp in range(2):
                def cons_co(ch, ps, pb, psh, pbh, pp=pp):
                    gch = pp * 4 + ch
                    tfa, tfb = next_tf()
                    op("dve", lambda e: e.tensor_tensor(out=tfa, in0=ps, in1=gt[:, gch, :], op=ALU.mult),
                       [pb, b_gt[gch]], [tfb])
                    op("dve", lambda e: e.tensor_tensor(out=mb[:, gch, :], in0=mg[:, gch, :], in1=tfa, op=ALU.add),
                       [tfb, b_mg[gch]], [b_mb[gch]])
                proj(P_CO + pp, lambda kc: pl[:, kc, :], list(b_pl), 8, cons_co)

            dbg_dump('ln', pl, b_pl)
            dbg_dump('mb', mb, b_mb)
            dbg_dump('rs', rs, [b_rs])
            dbg_dump('mu', mu, [b_mu])
            preload_ln_table()
            for pp in range(2):
                def cons_wo(ch, ps, pb, psh, pbh, pp=pp):
                    gch = pp * 4 + ch
                    op("dve", lambda e: e.tensor_tensor(out=h[:, gch, :], in0=ps, in1=h[:, gch, :], op=ALU.add),
                       [pb, b_h[gch]], [b_h[gch]])
                proj(P_WO + pp, lambda kc: mb[:, kc, :], list(b_mb), 8, cons_wo)

        def ffn(i):
            s_ = i % 2
            h, b_h = hB[s_], b_hB[s_]
            dbg_dump('h1', h, b_h)
            rmsnorm(h, b_h, sq, b_sq, rs, b_rs, tm, b_tm, xn, b_xn, vb + V_G2, T)
            for pp in range(8):
                def cons_ff1(ch, ps, pb, psh, pbh, pp=pp):
                    j = pp * 4 + ch
                    rla, rlb = next_rl()
                    op("act", lambda e: e.activation(out=rla, in_=ps, func=AF.Relu), [pb], [rlb])
                    op("dve", lambda e: e.tensor_tensor(out=hid[:, j, :], in0=ps, in1=rla, op=ALU.mult),
                       [pb, rlb], [b_cv[j // 2] if j < 16 else b_mg[(j - 16) // 2]])
                proj(P_FF1 + pp, xn_main, b_xn, 8, cons_ff1, fine=(pp == 0))
            if i + 1 < NTILE:
                norm1(i + 1)
            for og in range(2):
                banks = [next_bank() for _ in range(4)]
                for kg in range(4):
                    w, wb = panel(P_FF2 + og * 4 + kg)
                    w3 = w.rearrange("p (k n) -> p k n", k=8)
                    for ch in range(4):
                        ps, pb = banks[ch]
                        group("pe", [lambda e, kc=kc, ch=ch, ps=ps, w3=w3, kg=kg: e.matmul(
                            ps, lhsT=w3[:, kc, ch * 128:(ch + 1) * 128], rhs=hid[:, kg * 8 + kc, :],
                            start=(kg == 0 and kc == 0), stop=(kg == 3 and kc == 7)) for kc in range(8)],
                            [wb] + (b_cv[kg * 4:(kg + 1) * 4] if kg < 2 else b_mg[(kg - 2) * 4:(kg - 1) * 4]), [pb])
                        if kg == 3:
                            gch = og * 4 + ch
                            op("dve", lambda e, ps=ps, gch=gch: e.tensor_tensor(
                                out=h[:, gch, :], in0=ps, in1=h[:, gch, :], op=ALU.add), [pb, b_h[gch]], [b_h[gch]])
                    precast_step()

        def ple(i):
            s_ = i % 2
            s0 = i * T
            h, b_h = hB[s_], b_hB[s_]
            dbg_dump('h2', h, b_h)
            def cons_pp(ch, ps, pb, psh, pbh):
                op("act", lambda e: e.activation(out=cv[:, ch, :], in_=ps, func=AF.Copy), [pb], [b_cv[ch]])
            proj(P_PP, lambda kc: pbf[:, kc, :], [b_pbf], 2, cons_pp, ncols=1024)
            for pp in range(2):
                def cons_pg(ch, ps, pb, psh, pbh, pp=pp):
                    gch = pp * 4 + ch
                    op("act", lambda e: e.activation(out=gt[:, gch, :], in_=ps, func=AF.Sigmoid),
                       [pb], [b_gt[gch]])
                    tfa, tfb = next_tf()
                    op("dve", lambda e: e.tensor_tensor(out=tfa, in0=cv[:, gch, :], in1=gt[:, gch, :], op=ALU.mult),
                       [b_cv[gch], b_gt[gch]], [tfb])
                    op("dve", lambda e: e.tensor_tensor(out=h[:, gch, :], in0=h[:, gch, :], in1=tfa, op=ALU.add),
                       [tfb, b_h[gch]], [b_h[gch]])
                proj(P_PG + pp, xn_main, b_xn, 8, cons_pg, fine=(pp == 0))

            if l < n_layers - 1:
                dma("pool", hscr[:, s0:s0 + T].rearrange("(c p) t -> p c t", p=128), h, b_h, [b_hscr[i]])
            else:
                for c in range(8):
                    op("act", lambda e, c=c: e.activation(out=sq[:, c, :], in_=h[:, c, :], func=AF.Square),
                       [b_h[c]], [b_sq[c]])
                ps, pb = next_bank()
                group("pe", [lambda e, c=c, ps=ps: e.matmul(ps, lhsT=ones_b, rhs=sq[:, c, :], start=(c == 0),
                                                             stop=(c == 7)) for c in range(8)], b_sq + [b_const], [pb])
                op("act", lambda e, ps=ps: e.activation(out=tm, in_=ps, func=AF.Ln, bias=eps_ap, scale=1.0 / D),
                   [pb, b_const], [b_tm])
                op("act", lambda e: e.activation(out=rs, in_=tm, func=AF.Exp, scale=-0.5), [b_tm], [b_rs])
                for c in range(8):
                    op("dve", lambda e, c=c: e.scalar_tensor_tensor(
                        out=cv[:, c, :], in0=h[:, c, :], scalar=vecs[:, V_GF + c:V_GF + c + 1], in1=rs,
                        op0=ALU.mult, op1=ALU.mult), [b_h[c], b_rs, b_const], [b_cv[c]])
                dma("pool", yT[:, s0:s0 + T].rearrange("(c p) t -> p c t", p=128), cv, b_cv, [b_yT])

        load_h(0)
        norm1(0)
        for i in range(NTILE):
            s_ = i % 2
            if i + 1 < NTILE:
                load_h(i + 1)
            mixer(i)
            dma("pool", pt, pT[l * 256:(l + 1) * 256, i * T:(i + 1) * T].rearrange("(c p) t -> p c t", p=128),
                [b_in], [b_st[0], b_st[1]])
            op("dve", lambda e: e.tensor_copy(out=pbf, in_=pt), [b_st[0], b_st[1]], [b_pbf])
            ffn(i)
            rmsnorm(hB[s_], b_hB[s_], sq, b_sq, rs, b_rs, tm, b_tm, xn, b_xn, vb + V_G3, T)
            ple(i)
        end_phase()


    for l in range(n_layers):
        layer(l)
    fw_.barrier()
    return nc, fw_


_CACHE = {}


def kernel(**inputs):
    inp = {k: np.asarray(v) for k, v in inputs.items()}
    wsrc, wfT, fw, vecs = _prep_weights(inp)
    fcc = _fcc_table()
    ident = np.eye(128, dtype=np.float32).astype(ml_dtypes.bfloat16)
    consts = {}
    for kind in ("prompt", "sample"):
        m1s, m1p, m2 = _dft_consts(kind)
        consts[kind] = dict(m1s=m1s, m1p=m1p, m2=m2, cst=_cst_table(kind))
    xp, xs = inp["x_prompt"], inp["x_sample"]
    pp_, ps_ = inp["p_prompt"], inp["p_sample"]
    in_maps = []
    for c in range(8):
        if c < 4:
            kind = "prompt"
            x = xp[4 * c:4 * c + 4].reshape(NT, D)
            p = pp_[:, 4 * c:4 * c + 4].reshape(L, NT, 256)
        else:
            kind = "sample"
            x = xs[c - 4]
            p = ps_[:, c - 4]
        m = dict(
            xT=np.ascontiguousarray(x.T),
            pT=np.ascontiguousarray(p.transpose(0, 2, 1)).reshape(L * 256, NT),
            wsrc=wsrc, wfT=wfT, fw=fw, vecs=vecs, fcc=fcc, ident=ident,
        )
        m.update(consts[kind])
        in_maps.append(m)
    if "nc" not in _CACHE:
        _CACHE["nc"] = build_program()[0]
    res = run_bass_kernel_spmd(_CACHE["nc"], in_maps, core_ids=list(range(8)))
    outs = [np.asarray(r["yT"]).T for r in res.results]
    y_prompt = np.stack([o.reshape(4, 2048, D) for o in outs[:4]], 0).reshape(16, 2048, D).astype(np.float32)
    y_sample = np.stack(outs[4:], 0).astype(np.float32)
    return (np.ascontiguousarray(y_prompt), np.ascontiguousarray(y_sample))
```

```python
import contextlib
import numpy as np
import ml_dtypes
import concourse.bass as bass
import concourse.mybir as mybir
from concourse.bass_utils import run_bass_kernel_spmd

F32 = mybir.dt.float32
BF16 = mybir.dt.bfloat16
AF = mybir.ActivationFunctionType
ALU = mybir.AluOpType

D = 1024
NT = 8192
T = 512
NTILE = NT // T
L = 2
EPS = 1e-6
NPANEL = 44
PAN = 4096
WINS = (2, 4, 8, 16)

DCH = (0, 1)

P_POOL = 0
P_GA = 2
P_POOLW = 4
P_CG0, P_CA0, P_CG1, P_CA1 = 5, 6, 7, 8
P_CD = 9
P_GB = 17
P_CO = 19
P_GC = 21
P_WO = 23
P_FF1 = 25
P_FF2 = 33
P_PG = 41
P_PP = 43

ORDER = [0, 1, 2, 3, 4, 5, 6, 7, 8] + [9 + c for c in range(8) if c not in DCH] + [21, 22, 17, 18, 19, 20, 23, 24] + list(range(25, 41)) + [43, 41, 42]

V_G1, V_BG, V_PS, V_CB, V_LG, V_LB, V_G2, V_G3 = 0, 8, 32, 40, 48, 56, 64, 72
V_GF = 160
V_CW = 168
NVEC = 168 + 2 * 248
C_MSEG, C_EPS, C_TAB = 0, 1, 8
NCST = 136


class Buf:
    __slots__ = ("name", "w", "r")

    def __init__(self, name):
        self.name = name
        self.w = None
        self.r = {}


class EngState:
    def __init__(self, name, e, sem):
        self.name = name
        self.e = e
        self.sem = sem
        self.count = 0
        self.known = {}


class FW:
    def __init__(self, nc, n_dma=40):
        self.nc = nc
        self.eng = {}
        for name, e in (("pe", nc.tensor), ("act", nc.scalar), ("dve", nc.vector),
                        ("pool", nc.gpsimd), ("sp", nc.sync)):
            self.eng[name] = EngState(name, e, nc.alloc_semaphore("s_" + name))
        self.dsem = [nc.alloc_semaphore("s_dma%d" % i) for i in range(n_dma)]
        self.dgen = [0] * n_dma
        self.dnext = 0
        self.snap = {}
        self.nwait = 0

    def _wait(self, es, key, val, skip_self=False):
        if es.known.get(key, 0) >= val:
            return
        if key == es.name:
            if es.name == "pe" or skip_self or es.count - val >= 3:
                es.known[key] = val
                return
        if isinstance(key, str):
            es.e.wait_ge(self.eng[key].sem, val)
        else:
            es.e.wait_ge(self.dsem[key[1]], 16 * val)
        self.nwait += 1
        es.known[key] = val
        sn = self.snap.get((key, val))
        if sn:
            kn = es.known
            for k, v in sn.items():
                if kn.get(k, 0) < v:
                    kn[k] = v

    def _deps(self, es, reads, writes, extra=None, skip_self=False):
        evs = {}
        for b in reads:
            if b.w is not None:
                k, v = b.w
                if evs.get(k, 0) < v:
                    evs[k] = v
        for b in writes:
            if b.w is not None:
                k, v = b.w
                if evs.get(k, 0) < v:
                    evs[k] = v
            for k, v in b.r.items():
                if evs.get(k, 0) < v:
                    evs[k] = v
        if extra is not None:
            k, v = extra
            if evs.get(k, 0) < v:
                evs[k] = v
        for k, v in evs.items():
            self._wait(es, k, v, skip_self)

    def _record(self, ev, es, reads, writes):
        kn = es.known
        self.snap[ev] = {k: kn[k] for k in self.eng if k in kn}
        k, v = ev
        for b in reads:
            if b.r.get(k, 0) < v:
                b.r[k] = v
        for b in writes:
            b.w = ev
            b.r = {}

    def op(self, eng, fn, reads=(), writes=(), skip_self=False):
        es = self.eng[eng]
        self._deps(es, reads, writes, None, skip_self)
        ins = fn(es.e)
        es.count += 1
        ins.then_inc(es.sem, 1)
        ev = (es.name, es.count)
        self._record(ev, es, reads, writes)
        return ev

    def group(self, eng, fns, reads=(), writes=()):
        es = self.eng[eng]
        self._deps(es, reads, writes)
        ins = None
        for fn in fns:
            ins = fn(es.e)
        es.count += 1
        ins.then_inc(es.sem, 1)
        ev = (es.name, es.count)
        self._record(ev, es, reads, writes)
        return ev

    def dma(self, q, out, in_, reads=(), writes=()):
        es = self.eng[q]
        slot = self.dnext
        self.dnext = (slot + 1) % len(self.dsem)
        gen = self.dgen[slot]
        extra = (("d", slot), gen) if gen > 0 else None
        self._deps(es, reads, writes, extra)
        ins = es.e.dma_start(out=out, in_=in_)
        ins.then_inc(self.dsem[slot], 16)
        self.dgen[slot] = gen + 1
        ev = (("d", slot), gen + 1)
        self._record(ev, es, reads, writes)
        return ev

    def barrier(self):
        for es in self.eng.values():
            for o in self.eng.values():
                if o is not es and o.count > 0:
                    self._wait(es, o.name, o.count)
            for slot, gen in enumerate(self.dgen):
                if gen > 0:
                    self._wait(es, ("d", slot), gen)


def _dft_consts(kind):
    al = np.arange(64)[:, None].astype(np.float64)
    ka = np.arange(64)[None, :].astype(np.float64)
    M1 = np.zeros((128, 128, 128), np.float64)
    for bt in range(128):
        if kind == "sample":
            ang = -2 * np.pi * (ka * al / 64 + ka * bt / 8192)
        else:
            ang = -2 * np.pi * (ka * al / 64 + ka * (bt % 32) / 2048)
        Er, Ei = np.cos(ang), np.sin(ang)
        M1[0:64, bt, 0:64] = Er
        M1[0:64, bt, 64:128] = Ei
        M1[64:128, bt, 0:64] = -Ei
        M1[64:128, bt, 64:128] = Er
    M2 = np.zeros((128, 2, 128), np.float64)
    if kind == "sample":
        b = np.arange(128)[:, None].astype(np.float64)
        j = np.arange(128)[None, :].astype(np.float64)
        M2[:, 0, :] = np.cos(2 * np.pi * j * b / 128) / np.sqrt(8192)
        M2[:, 1, :] = np.sin(2 * np.pi * j * b / 128) / np.sqrt(8192)
    else:
        bb = np.arange(32)[:, None].astype(np.float64)
        jj = np.arange(32)[None, :].astype(np.float64)
        for q in range(4):
            M2[32 * q:32 * q + 32, 0, 32 * q:32 * q + 32] = np.cos(2 * np.pi * jj * bb / 32) / np.sqrt(2048)
            M2[32 * q:32 * q + 32, 1, 32 * q:32 * q + 32] = np.sin(2 * np.pi * jj * bb / 32) / np.sqrt(2048)
    if kind == "sample":
        p = np.arange(128)
        old = ((p % 32) // 16) * 64 + 16 * (p // 32) + p % 16
        M1 = M1[old]
    M1 = np.ascontiguousarray(M1.reshape(128, 16, 8, 128).transpose(1, 0, 2, 3))
    z = np.zeros_like(M1)
    bf = ml_dtypes.bfloat16
    if kind == "sample":
        return M1.astype(bf), z.astype(bf), M2.astype(bf)
    return z.astype(bf), M1.astype(bf), M2.astype(bf)


def _cst_table(kind):
    c = np.zeros((128, NCST), np.float32)
    c[:, C_MSEG] = 1.0 if kind == "sample" else 0.0
    c[:, C_EPS] = EPS
    for k in range(2):
        for side in range(2):
            for g, w in enumerate(WINS):
                for j in range(8):
                    if k == 1 and kind == "sample":
                        cnt = w
                    elif side == 0:
                        cnt = min(w, j + w // 2)
                    else:
                        r = 7 - j
                        cnt = min(w, r + w // 2 + 1)
                    c[:, C_TAB + ((k * 2 + side) * 4 + g) * 8 + j] = 1.0 / cnt
    return c


def _fcc_table():
    cp = np.arange(256)[:, None].astype(np.float64)
    c = np.arange(256)[None, :].astype(np.float64)
    ang = 2 * np.pi * cp * c / 256
    fc = np.stack([np.cos(ang) / 16, -np.sin(ang) / 16], 1)
    return np.ascontiguousarray(fc.reshape(2, 128, 2, 256).transpose(1, 0, 2, 3)).astype(np.float32)


def _kpanel(W, col0, ncols=512, kc0=0, nkc=8):
    blk = W[kc0 * 128:(kc0 + nkc) * 128, col0:col0 + ncols].reshape(nkc, 128, ncols).transpose(1, 0, 2)
    out = np.zeros((128, PAN), np.float32)
    out[:, :nkc * ncols] = blk.reshape(128, nkc * ncols)
    return out


def _prep_weights(inp):
    wsrc = np.zeros((L * NPANEL, 128, PAN), np.float32)
    for l in range(L):
        w_in = inp["w_in"][l]
        P = wsrc[l * NPANEL:(l + 1) * NPANEL]
        for i in range(2):
            P[P_POOL + i] = _kpanel(w_in, 512 * i)
            P[P_GA + i] = _kpanel(w_in, 4096 + 512 * i)
            P[P_GB + i] = _kpanel(w_in, 5120 + 512 * i)
            P[P_GC + i] = _kpanel(w_in, 6144 + 512 * i)
            P[P_CO + i] = _kpanel(inp["conv_out_w"][l], 512 * i)
            P[P_WO + i] = _kpanel(inp["w_o"][l], 512 * i)
            P[P_PG + i] = _kpanel(inp["w_ple_gate"][l], 512 * i)
        P[P_CA0] = _kpanel(w_in, 1024)
        P[P_CA1] = _kpanel(w_in, 1536)
        P[P_CG0] = _kpanel(w_in, 2048)
        P[P_CG1] = _kpanel(w_in, 2560)
        pw = inp["pool_w"][l]
        blk = pw.reshape(4, 2, 128, 256).transpose(2, 0, 1, 3)
        P[P_POOLW][:, :2048] = blk.reshape(128, 2048)
        cw = inp["conv_w"][l]
        for ch in range(8):
            dg = np.zeros((128, 31, 128), np.float32)
            idx = np.arange(128)
            dg[idx, :, idx] = cw[:, ch * 128:(ch + 1) * 128].T
            P[P_CD + ch][:, :31 * 128] = dg.reshape(128, 31 * 128)
        for i in range(8):
            P[P_FF1 + i] = _kpanel(inp["w_ff1"][l], 512 * i)
        for og in range(2):
            for kg in range(4):
                P[P_FF2 + og * 4 + kg] = _kpanel(inp["w_ff2"][l], 512 * og, 512, kg * 8, 8)
        P[P_PP] = _kpanel(inp["w_ple_proj"][l], 0, 1024, 0, 2)
    wfT = np.zeros((L, 4, 128, 2, 1024), np.float32)
    fw = np.zeros((L, 4, 128, 2, 256), np.float32)
    for l in range(L):
        for g in range(4):
            wf = inp["w_in"][l][:, 3072 + 256 * g:3072 + 256 * (g + 1)]
            wfT[l, g] = wf.T.reshape(2, 128, 1024).transpose(1, 0, 2)
            fw[l, g] = inp["fnet_w"][l][g].reshape(2, 128, 256).transpose(1, 0, 2)
    vecs = np.zeros((128, NVEC), np.float32)

    def col(v):
        return np.asarray(v, np.float32).reshape(-1, 128).T

    for l in range(L):
        b = l * 80
        vecs[:, b + V_G1:b + V_G1 + 8] = col(inp["norm_mix_g"][l])
        vecs[:, b + V_BG:b + V_BG + 24] = col(inp["b_gate"][l])
        vecs[:, b + V_PS:b + V_PS + 8] = col(inp["pool_scale"][l])
        vecs[:, b + V_CB:b + V_CB + 8] = col(inp["conv_b"][l])
        vecs[:, b + V_LG:b + V_LG + 8] = col(inp["conv_ln_g"][l])
        vecs[:, b + V_LB:b + V_LB + 8] = col(inp["conv_ln_b"][l])
        vecs[:, b + V_G2:b + V_G2 + 8] = col(inp["norm_ff_g"][l])
        vecs[:, b + V_G3:b + V_G3 + 8] = col(inp["norm_ple_g"][l])
    vecs[:, V_GF:V_GF + 8] = col(inp["final_norm_g"])
    for l in range(L):
        cw = inp["conv_w"][l]
        for ch in range(8):
            vecs[:, V_CW + l * 248 + ch * 31:V_CW + l * 248 + (ch + 1) * 31] = cw[:, ch * 128:(ch + 1) * 128].T
    return wsrc, wfT, fw, vecs


def build_program(n_layers=L, debug=False):
    nc = bass.Bass("TRN2", target_bir_lowering=False)
    fw_ = FW(nc)
    op, group, dma = fw_.op, fw_.group, fw_.dma

    def din(name, shape, dt=F32):
        return nc.dram_tensor(name, list(shape), dt, kind="ExternalInput").ap()

    xT = din("xT", [D, NT])
    pT = din("pT", [L * 256, NT])
    wsrc = din("wsrc", [L * NPANEL, 128, PAN])
    wfT_d = din("wfT", [L, 4, 128, 2, 1024])
    fwd_d = din("fw", [L, 4, 128, 2, 256])
    vecs_d = din("vecs", [128, NVEC])
    cst_d = din("cst", [128, NCST])
    fcc_d = din("fcc", [128, 2, 2, 256])
    m1s_d = din("m1s", [16, 128, 8, 128], BF16)
    m1p_d = din("m1p", [16, 128, 8, 128], BF16)
    m2_d = din("m2", [128, 2, 128], BF16)
    ident_d = din("ident", [128, 128], BF16)
    yT = nc.dram_tensor("yT", [D, NT], F32, kind="ExternalOutput").ap()

    dk = "ExternalOutput" if debug else "Internal"
    wbf = nc.dram_tensor("wbf", [L * NPANEL, 128, PAN], BF16, kind=dk).ap()
    hscr = nc.dram_tensor("hscr", [D, NT], F32, kind=dk).ap()
    zs = nc.dram_tensor("zs", [4, 2, 2048, 1024], BF16, kind=dk).ap()
    ys = nc.dram_tensor("ys", [128, 128, 1024], BF16, kind=dk).ap()
    ycs = nc.dram_tensor("ycs", [NT, 1024], BF16, kind=dk).ap()
    wfs = nc.dram_tensor("wfs", [128, 8, 2048], BF16).ap()
    b_wfs = [Buf("wfs%d" % g) for g in range(8)]

    b_wbf = [Buf("wbf%d" % i) for i in range(L * NPANEL)]
    b_hscr = [Buf("hscr%d" % i) for i in range(NTILE)]
    b_zs = [Buf("zs%d_%d" % (r, i)) for r in range(2) for i in range(NTILE)]
    b_ys = [Buf("ys%d" % i) for i in range(16)]
    b_ycs = Buf("ycs")
    b_yT = Buf("yT")
    b_in = Buf("inputs")

    def sb(name, shape, dt):
        return nc.alloc_sbuf_tensor("sb_" + name, list(shape), dt)

    vecs = sb("vecs", [128, NVEC], F32).ap()
    cst = sb("cst", [128, NCST], F32).ap()
    ident = sb("ident", [128, 128], BF16).ap()
    ones_b = sb("ones_b", [128, 128], BF16).ap()
    ones_f = sb("ones_f", [128, 128], F32).ap()
    dummy = sb("dummy", [128, 2], F32).ap()
    b_dummy = Buf("dummy")

    def preload_ln_table():
        op("act", lambda e: e.activation(out=dummy[:, 0:1], in_=cst[:, C_EPS:C_EPS + 1], func=AF.Ln),
           [b_const], [b_dummy])
    b_const = Buf("const")
    psb = [nc.alloc_psum_tensor("ps%d" % i, [128, 512], F32).ap() for i in range(8)]
    b_ps = [Buf("ps%d" % i) for i in range(8)]
    ps_state = {"n": 0}

    def next_bank():
        i = ps_state["n"] % 8
        ps_state["n"] += 1
        return psb[i], b_ps[i]

    dma("sp", vecs, vecs_d, [b_in], [b_const])
    dma("sp", cst, cst_d, [b_in], [b_const])
    dma("sp", ident, ident_d, [b_in], [b_const])
    op("dve", lambda e: e.memset(ones_b, 1.0), [], [b_const])
    op("dve", lambda e: e.memset(ones_f, 1.0), [], [b_const])
    eps_ap = cst[:, C_EPS:C_EPS + 1]
    mseg_ap = cst[:, C_MSEG:C_MSEG + 1]

    evac_state = {"n": 0}

    def evac_copy(out, in_, reads, writes):
        evac_state["n"] += 1
        if evac_state["n"] % 2 == 0:
            op("act", lambda e: e.activation(out=out, in_=in_, func=AF.Copy), reads, writes)
        else:
            op("dve", lambda e: e.tensor_copy(out=out, in_=in_), reads, writes)

    dumped = {}

    def dbg_dump(name, ap, bufs):
        if not debug or name in dumped:
            return
        dumped[name] = 1
        shape = list(ap.shape)
        dt_ = ap.dtype
        o = nc.dram_tensor("dbg_" + name, shape, dt_, kind="ExternalOutput").ap()
        dma("pool", o, ap, list(bufs), [Buf("dbg")])

    def precast_panels(indices, stk, engines=("act", "dve", "act")):
        nbuf = len(engines)
        count = len(indices)
        first = indices[0]
        sf = [stk.enter_context(nc.sbuf_tensor("pc_f%d_%d" % (first, i), [128, PAN], F32)).ap() for i in range(nbuf)]
        sbf = [stk.enter_context(nc.sbuf_tensor("pc_b%d_%d" % (first, i), [128, PAN], BF16)).ap() for i in range(nbuf)]
        bf_ = [Buf("pcf%d" % i) for i in range(nbuf)]
        bb_ = [Buf("pcb%d" % i) for i in range(nbuf)]

        def load(j):
            s_ = j % nbuf
            dma("pool", sf[s_], wsrc[indices[j]], [b_in], [bf_[s_]])

        def cast(j):
            i = indices[j]
            s_ = j % nbuf
            eng = engines[s_]
            if eng == "act":
                op("act", lambda e: e.activation(out=sbf[s_], in_=sf[s_], func=AF.Copy), [bf_[s_]], [bb_[s_]])
            else:
                op(eng, lambda e: e.tensor_copy(out=sbf[s_], in_=sf[s_]), [bf_[s_]], [bb_[s_]])
            dma("pool", wbf[i], sbf[s_], [bb_[s_]], [b_wbf[i]])
        return [lambda j=j: load(j) for j in range(count)], [lambda j=j: cast(j) for j in range(count)]

    def rmsnorm(h, hb, sq, sqb, rs, rsb, tmp, tmpb, xn, xnb, gcol, n):
        def pick(lst, c):
            return lst[c] if len(lst) == 8 else lst[0]
        for c in range(8):
            op("act", lambda e, c=c: e.activation(out=sq[:, c, :], in_=h[:, c, :], func=AF.Square),
               [pick(hb, c)], [pick(sqb, c)])
        ps, pb = next_bank()
        for c in range(8):
            group("pe", [lambda e, c=c: e.matmul(ps[:, 0:n], lhsT=ones_b, rhs=sq[:, c, :], start=(c == 0),
                                                 stop=(c == 7))], [pick(sqb, c), b_const], [pb])
        op("act", lambda e: e.activation(out=tmp, in_=ps[:, 0:n], func=AF.Ln, bias=eps_ap, scale=1.0 / D),
           [pb, b_const], [tmpb])
        op("act", lambda e: e.activation(out=rs, in_=tmp, func=AF.Exp, scale=-0.5), [tmpb], [rsb])
        for c in range(8):
            op("dve", lambda e, c=c: e.scalar_tensor_tensor(out=xn[:, c, :], in0=h[:, c, :],
                                                            scalar=vecs[:, gcol + c:gcol + c + 1], in1=rs,
                                                            op0=ALU.mult, op1=ALU.mult),
               [pick(hb, c), rsb, b_const], [pick(xnb, c)])

    def layer(l):
        hsrc = xT if l == 0 else hscr
        vb = l * 80

        def hsrc_bufs(i):
            return [b_in] if l == 0 else [b_hscr[i]]

        stk_box = [contextlib.ExitStack()]

        def tsb(name, shape, dt):
            return stk_box[0].enter_context(nc.sbuf_tensor("L%d_%s" % (l, name), list(shape), dt)).ap()

        def end_phase():
            fw_.barrier()
            stk_box[0].close()
            stk_box[0] = contextlib.ExitStack()

        wfp = tsb("wfp", [128, 8, 2048], BF16)
        b_wfp = Buf("wfp")

        def wf_temps(alloc, tag):
            d = dict(wft=[alloc(tag + "wft%d" % k, [128, 2, 1024], F32) for k in range(2)],
                     fwt=[alloc(tag + "fwt%d" % k, [128, 2, 256], F32) for k in range(2)],
                     gsb=alloc(tag + "gsb", [128, 2, 512], F32), fcc=alloc(tag + "fcc", [128, 2, 2, 256], F32),
                     b_wft=[Buf("wft0"), Buf("wft1")], b_fwt=[Buf("fwt0"), Buf("fwt1")],
                     b_gsb=Buf("gsb"), b_fcc=Buf("fcc"))
            dma("sp", d["fcc"], fcc_d, [b_in], [d["b_fcc"]])
            return d

        def wf_load(ll, g, d):
            dma("sp", d["wft"][g % 2], wfT_d[ll, g], [b_in], [d["b_wft"][g % 2]])
            dma("sp", d["fwt"][g % 2], fwd_d[ll, g], [b_in], [d["b_fwt"][g % 2]])

        def wf_group(ll, g, d, out_fn):
            wft, fwt, gsb, fcc = d["wft"][g % 2], d["fwt"][g % 2], d["gsb"], d["fcc"]
            b_wft_, b_fwt_ = d["b_wft"][g % 2], d["b_fwt"][g % 2]
            for cc in range(2):
                ps, pb = next_bank()
                for ri in range(2):
                    group("pe", [lambda e, kc=kc, ri=ri, cc=cc: e.matmul(
                        ps[:, ri * 256:(ri + 1) * 256], lhsT=fcc[:, kc, ri, cc * 128:(cc + 1) * 128],
                        rhs=fwt[:, kc, :], start=(kc == 0), stop=(kc == 1)) for kc in range(2)],
                        [d["b_fcc"], b_fwt_], [pb])
                evac_copy(gsb[:, cc, :], ps, [pb], [d["b_gsb"]])
            for dk_ in range(8):
                ps, pb = next_bank()
                group("pe", [lambda e, kc=kc, dk_=dk_: e.matmul(
                    ps, lhsT=wft[:, kc, dk_ * 128:(dk_ + 1) * 128], rhs=gsb[:, kc, :],
                    start=(kc == 0), stop=(kc == 1)) for kc in range(2)], [b_wft_, d["b_gsb"]], [pb])
                out_fn(g, dk_, ps, pb)

        pcl, pcc = [], []
        if l == 0:
            pc_idx = [p_ for p_ in range(NPANEL) if p_ not in [P_CD + c for c in DCH]]
            pcl, pcc = precast_panels(pc_idx, stk_box[0])
        hA = [tsb("hA%d" % i, [128, 8, T], F32) for i in range(2)]
        b_hA = [Buf("hA%d" % i) for i in range(2)]
        sqA1 = tsb("sqA", [128, 8, T], BF16)
        sqA = [sqA1, sqA1]
        b_sqA1 = [Buf("sqA_%d" % c) for c in range(8)]
        b_sqA = [b_sqA1, b_sqA1]
        xnA = [tsb("xnA%d" % i, [128, 8, T], BF16) for i in range(2)]
        b_xnA = [[Buf("xnA%d_%d" % (i, c)) for c in range(8)] for i in range(2)]
        rsA = tsb("rsA", [128, T], F32)
        tmA = tsb("tmA", [128, T], F32)
        b_rsA, b_tmA = Buf("rsA"), Buf("tmA")

        def loadA(i):
            dma("sp", hA[i % 2], hsrc[:, i * T:(i + 1) * T].rearrange("(c p) t -> p c t", p=128),
                hsrc_bufs(i), [b_hA[i % 2]])

        def normA(i):
            s_ = i % 2
            rmsnorm(hA[s_], [b_hA[s_]], sqA[s_], b_sqA[s_], rsA, b_rsA, tmA, b_tmA, xnA[s_], b_xnA[s_], vb + V_G1, T)

        if l == 0 or n_layers == 1:
            sub = contextlib.ExitStack()

            def ssb(name, shape, dt):
                return sub.enter_context(nc.sbuf_tensor("L%d_%s" % (l, name), list(shape), dt)).ap()
            d0 = wf_temps(ssb, "a")

            def out_sb(g, dk_, ps, pb):
                o = wfp[:, dk_, :].rearrange("p (r g m) -> p r g m", r=2, g=4)[:, :, g, :]
                evac_copy(o, ps.rearrange("p (r m) -> p r m", r=2), [pb], [b_wfp])
            outstanding = []
            wf_load(l, 0, d0)
            for g in range(4):
                if g + 1 < 4:
                    wf_load(l, g + 1, d0)
                for _ in range(3):
                    if pcl:
                        pcl.pop(0)()
                        outstanding.append(pcc.pop(0))
                wf_group(l, g, d0, out_sb)
                if g == 0:
                    loadA(0)
                    loadA(1)
                if g == 2:
                    normA(0)
                for f_ in outstanding:
                    f_()
                outstanding = []
            sub.close()
        else:
            loadA(0)
            dma("sp", wfp, wfs, b_wfs, [b_wfp])
            loadA(1)
            normA(0)

        ztA = [tsb("ztA%d" % i, [128, 4, 2048], BF16) for i in range(2)]
        b_ztA = [Buf("ztA%d" % i) for i in range(2)]
        pc_per_tile = -(-len(pcl) // NTILE)

        for i in range(NTILE):
            s = i % 2
            if i + 2 < NTILE:
                loadA(i + 2)
            npc_t = min(pc_per_tile, len(pcl))
            for _ in range(npc_t):
                pcl.pop(0)()
            for tb in range(4):
                if tb == 2 and i + 1 < NTILE:
                    normA(i + 1)
                for pn in range(4):
                    ps, pb = next_bank()
                    group("pe", [lambda e, kc=kc, tb=tb, pn=pn: e.matmul(
                        ps, lhsT=xnA[s][:, kc, tb * 128:(tb + 1) * 128], rhs=wfp[:, kc, pn * 512:(pn + 1) * 512],
                        start=(kc == 0), stop=(kc == 7)) for kc in range(8)], b_xnA[s] + [b_wfp], [pb])
                    evac_copy(ztA[s][:, tb, pn * 512:(pn + 1) * 512], ps, [pb], [b_ztA[s]])
            for ri in range(2):
                dma("sp", zs[i // 4, ri, (i % 4) * T:(i % 4 + 1) * T, :].rearrange("(tb p) c -> p tb c", p=128),
                    ztA[s][:, :, ri * 1024:(ri + 1) * 1024], [b_ztA[s]], [b_zs[ri * NTILE + i]])
            for _ in range(npc_t):
                pcc.pop(0)()
        assert not pcl and not pcc
        end_phase()

        in1s = [tsb("in1s%d" % i, [128, 8, 1024], BF16) for i in range(2)]
        in1p = [tsb("in1p%d" % i, [128, 8, 1024], BF16) for i in range(2)]
        m1s = [tsb("m1s%d" % i, [128, 8, 128], BF16) for i in range(2)]
        m1p = [tsb("m1p%d" % i, [128, 8, 128], BF16) for i in range(2)]
        o1 = [tsb("o1_%d" % i, [128, 8, 1024], BF16) for i in range(2)]
        b_in1 = [[Buf("in1_%d_%d" % (i, j)) for j in range(6)] for i in range(2)]
        do_wf1 = (l == 0 and n_layers > 1)
        if do_wf1:
            d1 = wf_temps(tsb, "b")
            wstg = tsb("wstg", [128, 8, 512], BF16)
            b_wstg = Buf("wstg")

            def out_dram(g, dk_, ps, pb):
                evac_copy(wstg[:, dk_, :], ps, [pb], [b_wstg])
                if dk_ == 7:
                    for r_ in range(2):
                        c0 = r_ * 1024 + g * 256
                        dma("sp", wfs[:, :, c0:c0 + 256], wstg[:, :, r_ * 256:(r_ + 1) * 256],
                            [b_wstg], [b_wfs[2 * g + r_]])
        b_o1 = [Buf("o1_%d" % i) for i in range(2)]
        for gi in range(16):
            s = gi % 2
            b0 = gi * 8
            q, bp0 = b0 // 32, b0 % 32
            src = zs.rearrange("q r t c -> (q r t) c").rearrange("(p b) c -> p b c", b=128)[:, b0:b0 + 8, :]
            dma("sp", in1s[s], src, b_zs, [b_in1[s][0], b_in1[s][1]])
            src = zs[q].rearrange("r t c -> (r t) c").rearrange("(p b) c -> p b c", b=32)[:, bp0:bp0 + 8, :]
            dma("sp", in1p[s], src, b_zs, [b_in1[s][2], b_in1[s][3]])
            dma("sp", m1s[s], m1s_d[gi], [b_in], [b_in1[s][4]])
            dma("sp", m1p[s], m1p_d[gi], [b_in], [b_in1[s][5]])
            for bb in range(8):
                for hf in range(2):
                    ps, pb = next_bank()
                    group("pe", [
                        lambda e, bb=bb, hf=hf, ps=ps: e.matmul(ps, lhsT=m1s[s][:, bb, :],
                                                                rhs=in1s[s][:, bb, hf * 512:(hf + 1) * 512],
                                                                start=True, stop=False),
                        lambda e, bb=bb, hf=hf, ps=ps: e.matmul(ps, lhsT=m1p[s][:, bb, :],
                                                                rhs=in1p[s][:, bb, hf * 512:(hf + 1) * 512],
                                                                start=False, stop=True)],
                        b_in1[s], [pb])
                    evac_copy(o1[s][:, bb, hf * 512:(hf + 1) * 512], ps, [pb], [b_o1[s]])
            dma("pool", ys[b0:b0 + 8].rearrange("b p c -> p b c"), o1[s], [b_o1[s]], [b_ys[gi]])
            if do_wf1 and gi % 4 == 1:
                wf_load(1, gi // 4, d1)
                wf_group(1, gi // 4, d1, out_dram)
        end_phase()

        y2g = [tsb("y2g%d" % i, [128, 2, 8, 1024], BF16) for i in range(2)]
        b_y2g = [Buf("y2g%d" % i) for i in range(2)]
        m2 = tsb("m2", [128, 2, 128], BF16)
        b_m2 = Buf("m2")
        o2 = [tsb("o2_%d" % i, [128, 8, 1024], BF16) for i in range(2)]
        b_o2 = [Buf("o2_%d" % i) for i in range(2)]
        dma("sp", m2, m2_d, [b_in], [b_m2])
        ys4 = ys.rearrange("b (r k) c -> b r k c", r=2)
        for kg in range(8):
            s = kg % 2
            dma("sp", y2g[s], ys4[:, :, kg * 8:(kg + 1) * 8, :], b_ys, [b_y2g[s]])
            for kk in range(8):
                for hf in range(2):
                    ps, pb = next_bank()
                    group("pe", [lambda e, r=r, kk=kk, hf=hf, ps=ps: e.matmul(
                        ps, lhsT=m2[:, r, :], rhs=y2g[s][:, r, kk, hf * 512:(hf + 1) * 512],
                        start=(r == 0), stop=(r == 1)) for r in range(2)], [b_m2, b_y2g[s]], [pb])
                    evac_copy(o2[s][:, kk, hf * 512:(hf + 1) * 512], ps, [pb], [b_o2[s]])
            dst = ycs.rearrange("(j k) c -> j k c", k=64)[:, kg * 8:(kg + 1) * 8, :]
            dma("pool", dst, o2[s], [b_o2[s]], [b_ycs])
        end_phase()

        NR = 4
        wr = [tsb("wr%d" % i, [128, PAN], BF16) for i in range(NR)]
        b_wr = [Buf("wr%d" % i) for i in range(NR)]
        hB = [tsb("h%d" % i, [128, 8, T], F32) for i in range(2)]
        b_hB = [[Buf("h%d_%d" % (i, c)) for c in range(8)] for i in range(2)]
        hhB = [tsb("hh%d" % i, [128, 8, 2, 16], F32) for i in range(2)]
        b_hhL = [Buf("hhL%d" % i) for i in range(2)]
        b_hhR = [Buf("hhR%d" % i) for i in range(2)]
        xn = tsb("xn", [128, 8, T], BF16)
        b_xn = [Buf("xn%d" % c) for c in range(8)]
        xn1 = tsb("xn1", [128, 8, T], BF16)
        b_xn1 = [Buf("xn1_%d" % c) for c in range(8)]
        xh = tsb("xh", [128, 8, 2, 16], BF16)
        b_xh = Buf("xh")
        sq = tsb("sq", [128, 8, T], BF16)
        b_sq = [Buf("sq%d" % c) for c in range(8)]
        sqh = tsb("sqh", [128, 8, 2, 16], BF16)
        b_sqh = Buf("sqh")
        rs = tsb("rs", [128, T], F32)
        tm = tsb("tm", [128, T], F32)
        mu = tsb("mu", [128, T], F32)
        msq = tsb("msq", [128, T], F32)
        rsh = tsb("rsh", [128, 32], F32)
        tmh = tsb("tmh", [128, 32], F32)
        b_rs, b_tm, b_mu, b_msq, b_rsh, b_tmh = (Buf(n) for n in ("rs", "tm", "mu", "msq", "rsh", "tmh"))
        upy = tsb("upy", [128, 4352], BF16)
        up = upy.bitcast(F32).rearrange("p (c t) -> p c t", c=4)
        yct = upy[:, 0:4096].rearrange("p (b c) -> p b c", b=4)
        b_up = [Buf("up%d" % c) for c in range(4)]
        stt = tsb("st", [128, 3, 544], F32)
        st = [stt[:, k, :] for k in range(3)]
        b_st = [Buf("st%d" % k) for k in range(3)]
        pt = stt[:, 0:2, 0:512]
        pl = tsb("pl", [128, 8, T], BF16)
        b_pl = [Buf("pl%d" % c) for c in range(8)]
        gt = tsb("gt", [128, 8, T], BF16)
        b_gt = [Buf("gt%d" % c) for c in range(8)]
        sg = tsb("sg", [128, 4, 544], BF16)
        b_sg = [Buf("sg%d" % c) for c in range(4)]
        vx = tsb("vx", [128, 8, 544], BF16)
        b_vx = [Buf("vx%d" % c) for c in range(8)]
        big = tsb("big", [128, 32, T], BF16)
        hid = big
        bigf = big.rearrange("p c t -> p (c t)").bitcast(F32)
        cv = bigf[:, 0:4096].rearrange("p (c t) -> p c t", c=8)
        mg = bigf[:, 4096:8192].rearrange("p (c t) -> p c t", c=8)
        b_cv = [Buf("cv%d" % c) for c in range(8)]
        b_mg = [Buf("mg%d" % c) for c in range(8)]
        tf = [tsb("tf%d" % i, [128, T], F32) for i in range(3)]
        b_tf = [Buf("tf%d" % i) for i in range(3)]
        rl = [tsb("rl%d" % i, [128, T], BF16) for i in range(2)]
        b_rl = [Buf("rl%d" % i) for i in range(2)]
        pbf = tsb("pbf", [128, 2, T], BF16)
        b_pbf = Buf("pbf")
        mb = sq
        b_mb = b_sq
        tstate = {"n": 0, "r": 0}

        def next_tf():
            k = tstate["n"] % 3
            tstate["n"] += 1
            return tf[k], b_tf[k]

        def next_rl():
            k = tstate["r"] % 2
            tstate["r"] += 1
            return rl[k], b_rl[k]

        wstate = {"issued": 0, "used": 0}
        do_pc = (l == 0 and n_layers > 1)
        seq = []
        npc = 0
        for i in range(NTILE):
            for pi in ORDER:
                seq.append(("w", pi))
                if do_pc and P_FF2 <= pi < P_FF2 + 8 and npc < 2 * NPANEL:
                    seq.append(("c", npc))
                    npc += 1
        assert (not do_pc) or npc == 2 * NPANEL
        cst_b = [tsb("cstg%d" % k, [128, PAN // 2], BF16) for k in range(2)] if do_pc else []
        b_cstg = [Buf("cstg%d" % k) for k in range(2)]

        def ring_issue(u):
            while wstate["issued"] < min(len(seq), u + NR):
                j = wstate["issued"]
                kind, a = seq[j]
                if kind == "w":
                    gi = l * NPANEL + a
                    dma("sp", wr[j % NR], wbf[gi], [b_wbf[gi]], [b_wr[j % NR]])
                else:
                    gi = NPANEL + a // 2
                    hf = a % 2
                    dma("sp", wr[j % NR].bitcast(F32), wsrc[gi][:, hf * 2048:(hf + 1) * 2048], [b_in], [b_wr[j % NR]])
                wstate["issued"] += 1

        def panel(pi_expected):
            u = wstate["used"]
            assert seq[u] == ("w", pi_expected), (seq[u], pi_expected)
            ring_issue(u)
            wstate["used"] += 1
            return wr[u % NR], b_wr[u % NR]

        def precast_step():
            u = wstate["used"]
            if u >= len(seq) or seq[u][0] != "c":
                return
            a = seq[u][1]
            ring_issue(u)
            wstate["used"] += 1
            gi = NPANEL + a // 2
            hf = a % 2
            k = a % 2
            src = wr[u % NR].bitcast(F32)
            op("pool", lambda e: e.tensor_copy(out=cst_b[k], in_=src), [b_wr[u % NR]], [b_cstg[k]])
            dma("pool", wbf[gi][:, hf * 2048:(hf + 1) * 2048], cst_b[k], [b_cstg[k]], [b_wbf[gi]])

        def halo_view(a2d):
            return a2d.rearrange("p (a b) -> p a b", b=16)[:, 0:34:33, :]

        def load_h(i):
            s_ = i % 2
            s0 = i * T
            h, hh = hB[s_], hhB[s_]
            dma("pool", h, hsrc[:, s0:s0 + T].rearrange("(c p) t -> p c t", p=128), hsrc_bufs(i), b_hB[s_])
            if i > 0:
                dma("pool", hh[:, :, 0, :], hsrc[:, s0 - 16:s0].rearrange("(c p) t -> p c t", p=128),
                    hsrc_bufs(i - 1), [b_hhL[s_]])
            else:
                op("dve", lambda e: e.memset(hh[:, :, 0, :], 0.0), [], [b_hhL[s_]])
            if i < NTILE - 1:
                dma("pool", hh[:, :, 1, :], hsrc[:, s0 + T:s0 + T + 16].rearrange("(c p) t -> p c t", p=128),
                    hsrc_bufs(i + 1), [b_hhR[s_]])
            else:
                op("dve", lambda e: e.memset(hh[:, :, 1, :], 0.0), [], [b_hhR[s_]])
            if i % 4 == 0 and i > 0:
                op("pool", lambda e: e.tensor_scalar(out=hh[:, :, 0, :], in0=hh[:, :, 0, :], scalar1=mseg_ap,
                                                     scalar2=1.0, op0=ALU.mult, op1=ALU.mult),
                   [b_hhL[s_], b_const], [b_hhL[s_]])
            if i % 4 == 3 and i < NTILE - 1:
                op("pool", lambda e: e.tensor_scalar(out=hh[:, :, 1, :], in0=hh[:, :, 1, :], scalar1=mseg_ap,
                                                     scalar2=1.0, op0=ALU.mult, op1=ALU.mult),
                   [b_hhR[s_], b_const], [b_hhR[s_]])

        def norm1(i):
            s_ = i % 2
            h, hh = hB[s_], hhB[s_]
            rmsnorm(h, b_hB[s_], sq, b_sq, rs, b_rs, tm, b_tm, xn1, b_xn1, vb + V_G1, T)
            op("act", lambda e: e.activation(out=sqh, in_=hh, func=AF.Square), [b_hhL[s_], b_hhR[s_]], [b_sqh])
            ps, pb = next_bank()
            group("pe", [lambda e, c=c, ps=ps: e.matmul(ps[:, 0:32].rearrange("p (a b) -> p a b", b=16),
                                                         lhsT=ones_b, rhs=sqh[:, c, :, :],
                                                         start=(c == 0), stop=(c == 7)) for c in range(8)],
                  [b_sqh, b_const], [pb])
            op("act", lambda e, ps=ps: e.activation(out=tmh, in_=ps[:, 0:32], func=AF.Ln, bias=eps_ap,
                                                    scale=1.0 / D), [pb, b_const], [b_tmh])
            op("act", lambda e: e.activation(out=rsh, in_=tmh, func=AF.Exp, scale=-0.5), [b_tmh], [b_rsh])
            for c in range(8):
                op("dve", lambda e, c=c: e.scalar_tensor_tensor(
                    out=xh[:, c, :, :], in0=hh[:, c, :, :], scalar=vecs[:, vb + V_G1 + c:vb + V_G1 + c + 1],
                    in1=rsh.rearrange("p (a b) -> p a b", b=16), op0=ALU.mult, op1=ALU.mult),
                    [b_hhL[s_], b_hhR[s_], b_rsh, b_const], [b_xh])

        def proj(pi, rhs_fn, rhs_bufs, nkc, consume, halo_fn=None, ncols=512, fine=False):
            w, wb = panel(pi)
            w3 = w[:, 0:nkc * ncols].rearrange("p (k n) -> p k n", k=nkc)
            for ch in range(ncols // 128):
                ps, pb = next_bank()
                if fine and ch == 0:
                    for kc in range(nkc):
                        group("pe", [lambda e, kc=kc, ch=ch, ps=ps: e.matmul(
                            ps, lhsT=w3[:, kc, ch * 128:(ch + 1) * 128], rhs=rhs_fn(kc),
                            start=(kc == 0), stop=(kc == nkc - 1))], [wb, rhs_bufs[kc]], [pb])
                elif halo_fn is None:
                    group("pe", [lambda e, kc=kc, ch=ch, ps=ps: e.matmul(
                        ps, lhsT=w3[:, kc, ch * 128:(ch + 1) * 128], rhs=rhs_fn(kc),
                        start=(kc == 0), stop=(kc == nkc - 1)) for kc in range(nkc)], [wb] + rhs_bufs, [pb])
                psh = pbh = None
                if halo_fn is not None:
                    psh, pbh = next_bank()
                    pshv = psh[:, 0:32].rearrange("p (a b) -> p a b", b=16)
                    fns = [lambda e, kc=kc, ch=ch, ps=ps: e.matmul(
                        ps, lhsT=w3[:, kc, ch * 128:(ch + 1) * 128], rhs=rhs_fn(kc),
                        start=(kc == 0), stop=(kc == nkc - 1)) for kc in range(nkc)]
                    fns += [lambda e, kc=kc, ch=ch, pshv=pshv: e.matmul(
                        pshv, lhsT=w3[:, kc, ch * 128:(ch + 1) * 128], rhs=halo_fn(kc),
                        start=(kc == 0), stop=(kc == nkc - 1)) for kc in range(nkc)]
                    group("pe", fns, [wb, b_xh] + rhs_bufs, [pb, pbh])
                consume(ch, ps, pb, psh, pbh)

        xn1_main = lambda kc: xn1[:, kc, :]
        xn_main = lambda kc: xn[:, kc, :]
        xn_halo = lambda kc: xh[:, kc, :, :]

        def gate_proj(pbase, branch):
            for pp in range(2):
                def cons_g(ch, ps, pb, psh, pbh, pp=pp):
                    gch = pp * 4 + ch
                    col = vb + V_BG + 8 * branch + gch
                    op("act", lambda e: e.activation(out=gt[:, gch, :], in_=ps, func=AF.Sigmoid,
                                                     bias=vecs[:, col:col + 1], scale=1.0),
                       [pb, b_const], [b_gt[gch]])
                proj(pbase + pp, xn1_main, b_xn1, 8, cons_g)

        def mixer(i):
            s_ = i % 2
            s0 = i * T
            h, b_h = hB[s_], b_hB[s_]
            left_edge = (i % 4 == 0)
            right_edge = (i % 4 == 3)
            kindL = 0 if i == 0 else 1
            kindR = 0 if i == NTILE - 1 else 1

            for pp in range(2):
                def cons_pool(ch, ps, pb, psh, pbh, pp=pp):
                    op("act", lambda e: e.activation(out=up[:, ch, 16:528], in_=ps, func=AF.Copy), [pb], [b_up[ch]])
                    op("act", lambda e: e.activation(out=halo_view(up[:, ch, :]),
                                                     in_=psh[:, 0:32].rearrange("p (a b) -> p a b", b=16),
                                                     func=AF.Copy), [pbh], [b_up[ch]])
                    g = pp * 2 + ch // 2
                    w = WINS[g]
                    gch = pp * 4 + ch
                    u = up[:, ch, :]
                    cur, cb = u, b_up[ch]
                    lvl = 0
                    ranges = {2: (9, 535), 4: (10, 534), 8: (12, 532), 16: (16, 528)}
                    offs = {2: (-1, 0), 4: (-1, 1), 8: (-2, 2), 16: (-4, 4)}
                    ww = 2
                    while ww <= w:
                        a, b = ranges[ww]
                        o0, o1_ = offs[ww]
                        dst, db = st[lvl % 3], b_st[lvl % 3]
                        op("dve", lambda e, dst=dst, cur=cur, a=a, b=b, o0=o0, o1_=o1_: e.tensor_tensor(
                            out=dst[:, a:b], in0=cur[:, a + o0:b + o0], in1=cur[:, a + o1_:b + o1_], op=ALU.add),
                            [cb], [db])
                        cur, cb = dst, db
                        lvl += 1
                        ww *= 2
                    op("dve", lambda e, cur=cur: e.scalar_tensor_tensor(
                        out=pl[:, gch, :], in0=cur[:, 16:528], scalar=1.0 / w, in1=u[:, 16:528],
                        op0=ALU.mult, op1=ALU.subtract), [cb, b_up[ch]], [b_pl[gch]])
                    for side, flag, kind in ((0, left_edge, kindL), (1, right_edge, kindR)):
                        if not flag:
                            continue
                        c0 = C_TAB + ((kind * 2 + side) * 4 + g) * 8
                        e0 = 16 if side == 0 else 520
                        m0 = 0 if side == 0 else 504
                        tfa, tfb = next_tf()
                        op("dve", lambda e, cur=cur, c0=c0, e0=e0, tfa=tfa: e.tensor_tensor(
                            out=tfa[:, 0:8], in0=cur[:, e0:e0 + 8], in1=cst[:, c0:c0 + 8], op=ALU.mult),
                            [cb, b_const], [tfb])
                        op("dve", lambda e, e0=e0, m0=m0, tfa=tfa: e.tensor_tensor(
                            out=pl[:, gch, m0:m0 + 8], in0=tfa[:, 0:8], in1=u[:, e0:e0 + 8], op=ALU.subtract),
                            [tfb, b_up[ch]], [b_pl[gch]])
                proj(P_POOL + pp, xn1_main, b_xn1, 8, cons_pool, halo_fn=xn_halo)
            dbg_dump('xn1', xn1, b_xn1)
            dbg_dump('xh', xh, [b_xh])
            dbg_dump('pl0', pl, b_pl)
            gate_proj(P_GA, 0)
            w, wb = panel(P_POOLW)
            w4 = w[:, 0:2048].rearrange("p (g k n) -> p g k n", g=4, k=2)
            for dch in range(8):
                g, half = dch // 2, dch % 2
                ps, pb = next_bank()
                group("pe", [lambda e, kc=kc, g=g, half=half, ps=ps: e.matmul(
                    ps, lhsT=w4[:, g, kc, half * 128:(half + 1) * 128], rhs=pl[:, 2 * g + kc, :],
                    start=(kc == 0), stop=(kc == 1)) for kc in range(2)],
                    [wb, b_pl[2 * g], b_pl[2 * g + 1]], [pb])
                op("dve", lambda e, dch=dch, ps=ps: e.scalar_tensor_tensor(
                    out=mg[:, dch, :], in0=ps, scalar=vecs[:, vb + V_PS + dch:vb + V_PS + dch + 1],
                    in1=gt[:, dch, :], op0=ALU.mult, op1=ALU.mult), [pb, b_const, b_gt[dch]], [b_mg[dch]])

            dma("pool", yct, ycs[s0:s0 + T, :].rearrange("(b p) c -> p b c", p=128), [b_ycs], b_up)

            bg = []
            for pp in range(2):
                def cons_cg(ch, ps, pb, psh, pbh):
                    op("act", lambda e: e.activation(out=sg[:, ch, 16:528], in_=ps, func=AF.Sigmoid),
                       [pb], [b_sg[ch]])
                    op("act", lambda e: e.activation(out=halo_view(sg[:, ch, :]),
                                                     in_=psh[:, 0:32].rearrange("p (a b) -> p a b", b=16),
                                                     func=AF.Sigmoid), [pbh], [b_sg[ch]])
                proj(P_CG0 if pp == 0 else P_CG1, xn1_main, b_xn1, 8, cons_cg, halo_fn=xn_halo)

                def cons_ca(ch, ps, pb, psh, pbh, pp=pp):
                    gch = pp * 4 + ch
                    op("dve", lambda e: e.tensor_tensor(out=vx[:, gch, 16:528], in0=ps, in1=sg[:, ch, 16:528],
                                                        op=ALU.mult), [pb, b_sg[ch]], [b_vx[gch]])
                    op("dve", lambda e: e.tensor_tensor(out=halo_view(vx[:, gch, :]),
                                                        in0=psh[:, 0:32].rearrange("p (a b) -> p a b", b=16),
                                                        in1=halo_view(sg[:, ch, :]), op=ALU.mult),
                       [pbh, b_sg[ch]], [b_vx[gch]])
                    if pp == 1:
                        for _ in range(4):
                            if bg:
                                bg.pop(0)()
                proj(P_CA0 if pp == 0 else P_CA1, xn1_main, b_xn1, 8, cons_ca, halo_fn=xn_halo)
                if pp == 0:
                    for k in range(31):
                        for ch in DCH:
                            wcol = vecs[:, V_CW + l * 248 + ch * 31 + k:V_CW + l * 248 + ch * 31 + k + 1]
                            if k == 0:
                                bcol = vecs[:, vb + V_CB + ch:vb + V_CB + ch + 1]
                                bg.append(lambda ch=ch, wcol=wcol, bcol=bcol: op(
                                    "dve", lambda e: e.tensor_scalar(out=cv[:, ch, :], in0=vx[:, ch, 1:513],
                                                                     scalar1=wcol, scalar2=bcol,
                                                                     op0=ALU.mult, op1=ALU.add),
                                    [b_vx[ch], b_const], [b_cv[ch]], skip_self=True))
                            else:
                                bg.append(lambda ch=ch, wcol=wcol, k=k: op(
                                    "dve", lambda e: e.scalar_tensor_tensor(
                                        out=cv[:, ch, :], in0=vx[:, ch, 1 + k:513 + k], scalar=wcol,
                                        in1=cv[:, ch, :], op0=ALU.mult, op1=ALU.add),
                                    [b_vx[ch], b_cv[ch], b_const], [b_cv[ch]], skip_self=True))
            while bg:
                bg.pop(0)()
            pe_chunks = [c for c in range(8) if c not in DCH]
            for ch in pe_chunks:
                w, wb = panel(P_CD + ch)
                w3 = w[:, 0:31 * 128].rearrange("p (k n) -> p k n", k=31)
                ps, pb = next_bank()
                group("pe", [lambda e, k=k, ch=ch, ps=ps, w3=w3: e.matmul(
                    ps, lhsT=w3[:, k, :], rhs=vx[:, ch, 1 + k:513 + k], start=(k == 0), stop=(k == 30))
                    for k in range(31)], [wb, b_vx[ch]], [pb])
                bcol = vecs[:, vb + V_CB + ch:vb + V_CB + ch + 1]
                op("act", lambda e, ch=ch, ps=ps, bcol=bcol: e.activation(out=cv[:, ch, :], in_=ps, func=AF.Identity,
                                                                          bias=bcol, scale=1.0),
                   [pb, b_const], [b_cv[ch]])
                op("act", lambda e, ch=ch, ps=ps, bcol=bcol: e.activation(out=sq[:, ch, :], in_=ps, func=AF.Square,
                                                                          bias=bcol, scale=1.0),
                   [pb, b_const], [b_sq[ch]])
                op("act", lambda e, ch=ch, ps=ps, bcol=bcol: e.activation(out=vx[:, ch, 16:528], in_=ps,
                                                                          func=AF.Identity, bias=bcol, scale=1.0),
                   [pb, b_const], [b_vx[ch]])
                if ch == pe_chunks[1]:
                    for dc in DCH:
                        op("act", lambda e, dc=dc: e.activation(out=sq[:, dc, :], in_=cv[:, dc, :], func=AF.Square),
                           [b_cv[dc]], [b_sq[dc]])
                        op("act", lambda e, dc=dc: e.activation(out=vx[:, dc, 16:528], in_=cv[:, dc, :],
                                                                func=AF.Identity), [b_cv[dc]], [b_vx[dc]])
            dbg_dump('vx', vx, b_vx)
            preload_ln_table()
            ps1, pb1 = next_bank()
            ps2, pb2 = next_bank()
            corder = list(DCH) + pe_chunks
            for n_, c in enumerate(corder):
                group("pe", [lambda e, c=c, n_=n_: e.matmul(ps2, lhsT=ones_b, rhs=sq[:, c, :], start=(n_ == 0),
                                                            stop=(n_ == 7))], [b_sq[c], b_const], [pb2])
                group("pe", [lambda e, c=c, n_=n_: e.matmul(ps1, lhsT=ones_b, rhs=vx[:, c, 16:528], start=(n_ == 0),
                                                            stop=(n_ == 7))], [b_vx[c], b_const], [pb1])
            op("act", lambda e: e.activation(out=mu, in_=ps1, func=AF.Identity, scale=1.0 / D), [pb1], [b_mu])
            op("dve", lambda e: e.tensor_tensor(out=msq, in0=mu, in1=mu, op=ALU.mult), [b_mu], [b_msq])
            op("dve", lambda e: e.scalar_tensor_tensor(out=tm, in0=ps2, scalar=1.0 / D, in1=msq,
                                                       op0=ALU.mult, op1=ALU.subtract), [pb2, b_msq], [b_tm])
            op("act", lambda e: e.activation(out=tm, in_=tm, func=AF.Ln, bias=eps_ap, scale=1.0),
               [b_tm, b_const], [b_tm])
            op("act", lambda e: e.activation(out=rs, in_=tm, func=AF.Exp, scale=-0.5), [b_tm], [b_rs])
            for ch in range(4, 8):
                op("pool", lambda e, ch=ch: e.tensor_tensor(out=cv[:, ch, :], in0=cv[:, ch, :], in1=mu,
                                                            op=ALU.subtract), [b_cv[ch], b_mu], [b_cv[ch]])
            for ch in range(8):
                if ch < 4:
                    op("dve", lambda e, ch=ch: e.tensor_tensor(out=cv[:, ch, :], in0=cv[:, ch, :], in1=mu,
                                                               op=ALU.subtract), [b_cv[ch], b_mu], [b_cv[ch]])
                op("dve", lambda e, ch=ch: e.tensor_tensor(out=cv[:, ch, :], in0=cv[:, ch, :], in1=rs, op=ALU.mult),
                   [b_cv[ch], b_rs], [b_cv[ch]], skip_self=True)
            gate_proj(P_GC, 2)
            for ch in range(8):
                op("act", lambda e, ch=ch: e.activation(
                    out=pl[:, ch, :], in_=cv[:, ch, :], func=AF.Silu,
                    bias=vecs[:, vb + V_LB + ch:vb + V_LB + ch + 1],
                    scale=vecs[:, vb + V_LG + ch:vb + V_LG + ch + 1]), [b_cv[ch], b_const], [b_pl[ch]])
            for dch in range(8):
                ps, pb = next_bank()
                psv = ps.bitcast(BF16)
                group("pe", [lambda e, b=b, dch=dch, psv=psv: e.transpose(
                    psv[:, b * 128:(b + 1) * 128], yct[:, b, dch * 128:(dch + 1) * 128], ident)
                    for b in range(4)], b_up + [b_const], [pb])
                tfa, tfb = next_tf()
                op("dve", lambda e, psv=psv, dch=dch, tfa=tfa: e.tensor_tensor(
                    out=tfa, in0=psv[:, 0:512], in1=gt[:, dch, :], op=ALU.mult), [pb, b_gt[dch]], [tfb])
                op("pool", lambda e, dch=dch, tfa=tfa: e.tensor_tensor(
                    out=mg[:, dch, :], in0=mg[:, dch, :], in1=tfa, op=ALU.add), [tfb, b_mg[dch]], [b_mg[dch]])
            gate_proj(P_GB, 1)
            for pp in range(2):
                def cons_co(ch, ps, pb, psh, pbh, pp=pp):
                    gch = pp * 4 + ch
                    tfa, tfb = next_tf()
                    op("dve", lambda e: e.tensor_tensor(out=tfa, in0=ps, in1=gt[:, gch, :], op=ALU.mult),
                       [pb, b_gt[gch]], [tfb])
                    op("dve", lambda e: e.tensor_tensor(out=mb[:, gch, :], in0=mg[:, gch, :], in1=tfa, op=ALU.add),
                       [tfb, b_mg[gch]], [b_mb[gch]])
                proj(P_CO + pp, lambda kc: pl[:, kc, :], list(b_pl), 8, cons_co)

            dbg_dump('ln', pl, b_pl)
            dbg_dump('mb', mb, b_mb)
            dbg_dump('rs', rs, [b_rs])
            dbg_dump('mu', mu, [b_mu])
            preload_ln_table()
            for pp in range(2):
                def cons_wo(ch, ps, pb, psh, pbh, pp=pp):
                    gch = pp * 4 + ch
                    op("dve", lambda e: e.tensor_tensor(out=h[:, gch, :], in0=ps, in1=h[:, gch, :], op=ALU.add),
                       [pb, b_h[gch]], [b_h[gch]])
                proj(P_WO + pp, lambda kc: mb[:, kc, :], list(b_mb), 8, cons_wo)

        def ffn(i):
            s_ = i % 2
            h, b_h = hB[s_], b_hB[s_]
            dbg_dump('h1', h, b_h)
            rmsnorm(h, b_h, sq, b_sq, rs, b_rs, tm, b_tm, xn, b_xn, vb + V_G2, T)
            for pp in range(8):
                def cons_ff1(ch, ps, pb, psh, pbh, pp=pp):
                    j = pp * 4 + ch
                    rla, rlb = next_rl()
                    op("act", lambda e: e.activation(out=rla, in_=ps, func=AF.Relu), [pb], [rlb])
                    op("dve", lambda e: e.tensor_tensor(out=hid[:, j, :], in0=ps, in1=rla, op=ALU.mult),
                       [pb, rlb], [b_cv[j // 2] if j < 16 else b_mg[(j - 16) // 2]])
                proj(P_FF1 + pp, xn_main, b_xn, 8, cons_ff1, fine=(pp == 0))
            if i + 1 < NTILE:
                norm1(i + 1)
            for og in range(2):
                banks = [next_bank() for _ in range(4)]
                for kg in range(4):
                    w, wb = panel(P_FF2 + og * 4 + kg)
                    w3 = w.rearrange("p (k n) -> p k n", k=8)
                    for ch in range(4):
                        ps, pb = banks[ch]
                        group("pe", [lambda e, kc=kc, ch=ch, ps=ps, w3=w3, kg=kg: e.matmul(
                            ps, lhsT=w3[:, kc, ch * 128:(ch + 1) * 128], rhs=hid[:, kg * 8 + kc, :],
                            start=(kg == 0 and kc == 0), stop=(kg == 3 and kc == 7)) for kc in range(8)],
                            [wb] + (b_cv[kg * 4:(kg + 1) * 4] if kg < 2 else b_mg[(kg - 2) * 4:(kg - 1) * 4]), [pb])
                        if kg == 3:
                            gch = og * 4 + ch
                            op("dve", lambda e, ps=ps, gch=gch: e.tensor_tensor(
                                out=h[:, gch, :], in0=ps, in1=h[:, gch, :], op=ALU.add), [pb, b_h[gch]], [b_h[gch]])
                    precast_step()

        def ple(i):
            s_ = i % 2
            s0 = i * T
            h, b_h = hB[s_], b_hB[s_]
            dbg_dump('h2', h, b_h)
            def cons_pp(ch, ps, pb, psh, pbh):
                op("act", lambda e: e.activation(out=cv[:, ch, :], in_=ps, func=AF.Copy), [pb], [b_cv[ch]])
            proj(P_PP, lambda kc: pbf[:, kc, :], [b_pbf], 2, cons_pp, ncols=1024)
            for pp in range(2):
                def cons_pg(ch, ps, pb, psh, pbh, pp=pp):
                    gch = pp * 4 + ch
                    op("act", lambda e: e.activation(out=gt[:, gch, :], in_=ps, func=AF.Sigmoid),
                       [pb], [b_gt[gch]])
                    tfa, tfb = next_tf()
                    op("dve", lambda e: e.tensor_tensor(out=tfa, in0=cv[:, gch, :], in1=gt[:, gch, :], op=ALU.mult),
                       [b_cv[gch], b_gt[gch]], [tfb])
                    op("dve", lambda e: e.tensor_tensor(out=h[:, gch, :], in0=h[:, gch, :], in1=tfa, op=ALU.add),
                       [tfb, b_h[gch]], [b_h[gch]])
                proj(P_PG + pp, xn_main, b_xn, 8, cons_pg, fine=(pp == 0))

            if l < n_layers - 1:
                dma("pool", hscr[:, s0:s0 + T].rearrange("(c p) t -> p c t", p=128), h, b_h, [b_hscr[i]])
            else:
                for c in range(8):
                    op("act", lambda e, c=c: e.activation(out=sq[:, c, :], in_=h[:, c, :], func=AF.Square),
                       [b_h[c]], [b_sq[c]])
                ps, pb = next_bank()
                group("pe", [lambda e, c=c, ps=ps: e.matmul(ps, lhsT=ones_b, rhs=sq[:, c, :], start=(c == 0),
                                                             stop=(c == 7)) for c in range(8)], b_sq + [b_const], [pb])
                op("act", lambda e, ps=ps: e.activation(out=tm, in_=ps, func=AF.Ln, bias=eps_ap, scale=1.0 / D),
                   [pb, b_const], [b_tm])
                op("act", lambda e: e.activation(out=rs, in_=tm, func=AF.Exp, scale=-0.5), [b_tm], [b_rs])
                for c in range(8):
                    op("dve", lambda e, c=c: e.scalar_tensor_tensor(
                        out=cv[:, c, :], in0=h[:, c, :], scalar=vecs[:, V_GF + c:V_GF + c + 1], in1=rs,
                        op0=ALU.mult, op1=ALU.mult), [b_h[c], b_rs, b_const], [b_cv[c]])
                dma("pool", yT[:, s0:s0 + T].rearrange("(c p) t -> p c t", p=128), cv, b_cv, [b_yT])

        load_h(0)
        norm1(0)
        for i in range(NTILE):
            s_ = i % 2
            if i + 1 < NTILE:
                load_h(i + 1)
            mixer(i)
            dma("pool", pt, pT[l * 256:(l + 1) * 256, i * T:(i + 1) * T].rearrange("(c p) t -> p c t", p=128),
                [b_in], [b_st[0], b_st[1]])
            op("dve", lambda e: e.tensor_copy(out=pbf, in_=pt), [b_st[0], b_st[1]], [b_pbf])
            ffn(i)
            rmsnorm(hB[s_], b_hB[s_], sq, b_sq, rs, b_rs, tm, b_tm, xn, b_xn, vb + V_G3, T)
            ple(i)
        end_phase()


    for l in range(n_layers):
        layer(l)
    fw_.barrier()
    return nc, fw_


_CACHE = {}


def kernel(**inputs):
    inp = {k: np.asarray(v) for k, v in inputs.items()}
    wsrc, wfT, fw, vecs = _prep_weights(inp)
    fcc = _fcc_table()
    ident = np.eye(128, dtype=np.float32).astype(ml_dtypes.bfloat16)
    consts = {}
    for kind in ("prompt", "sample"):
        m1s, m1p, m2 = _dft_consts(kind)
        consts[kind] = dict(m1s=m1s, m1p=m1p, m2=m2, cst=_cst_table(kind))
    xp, xs = inp["x_prompt"], inp["x_sample"]
    pp_, ps_ = inp["p_prompt"], inp["p_sample"]
    in_maps = []
    for c in range(8):
        if c < 4:
            kind = "prompt"
            x = xp[4 * c:4 * c + 4].reshape(NT, D)
            p = pp_[:, 4 * c:4 * c + 4].reshape(L, NT, 256)
        else:
            kind = "sample"
            x = xs[c - 4]
            p = ps_[:, c - 4]
        m = dict(
            xT=np.ascontiguousarray(x.T),
            pT=np.ascontiguousarray(p.transpose(0, 2, 1)).reshape(L * 256, NT),
            wsrc=wsrc, wfT=wfT, fw=fw, vecs=vecs, fcc=fcc, ident=ident,
        )
        m.update(consts[kind])
        in_maps.append(m)
    if "nc" not in _CACHE:
        _CACHE["nc"] = build_program()[0]
    res = run_bass_kernel_spmd(_CACHE["nc"], in_maps, core_ids=list(range(8)))
    outs = [np.asarray(r["yT"]).T for r in res.results]
    y_prompt = np.stack([o.reshape(4, 2048, D) for o in outs[:4]], 0).reshape(16, 2048, D).astype(np.float32)
    y_sample = np.stack(outs[4:], 0).astype(np.float32)
    return (np.ascontiguousarray(y_prompt), np.ascontiguousarray(y_sample))
```
